# Optimizing a Trainium2 kernel written in Bass

```python
import math
import jax, jax.numpy as jnp
from jax import lax
import numpy as np

D_MODEL = 1024
BATCH = 2
SEQ = 8192
DEPTH = 1

HEAD_DIM = 64
N_HEADS_A = 8
N_HEADS_B = 8
MIX_WIDTH = (N_HEADS_A + N_HEADS_B) * HEAD_DIM
DILATED_PAIRS = ((128, 1), (512, 4), (2048, 16))
IDX_HEADS = 8
IDX_DIM = 64
TOPK_MAX = 256
Q_BLOCK = 128
D_FF = 2816
CONV_WIDTH = 3
PLE_DIM = 256
ROPE_THETA = 10000.0
RMS_EPS = 1e-6

A_W = N_HEADS_A * HEAD_DIM
B_W = N_HEADS_B * HEAD_DIM
IQ_W = IDX_HEADS * IDX_DIM
IN_COLS = 3 * A_W + 3 * B_W + IQ_W + IDX_DIM + IDX_HEADS

kernel_name = "hybrid_dilated_dsa_convffn_ple"


def rmsnorm(x, g):
    x32 = x.astype(jnp.float32)
    y = x32 * lax.rsqrt(jnp.mean(x32 * x32, axis=-1, keepdims=True) + RMS_EPS)
    return (y * g.astype(jnp.float32)).astype(x.dtype)


def rope_tables(positions):
    inv_freq = 1.0 / (ROPE_THETA ** (jnp.arange(0, HEAD_DIM, 2, dtype=jnp.float32) / HEAD_DIM))
    ang = positions.astype(jnp.float32)[..., None] * inv_freq
    return jnp.cos(ang)[:, :, None, :], jnp.sin(ang)[:, :, None, :]


def apply_rope(x, cos, sin):
    x32 = x.astype(jnp.float32)
    x1, x2 = jnp.split(x32, 2, axis=-1)
    out = jnp.concatenate([x1 * cos - x2 * sin, x2 * cos + x1 * sin], axis=-1)
    return out.astype(x.dtype)


def dilated_branch(q, k, v, window, dil):
    B, T, H, Dh = q.shape
    n = window // dil
    seg = dil * n
    Tp = -(-T // seg) * seg
    nb = Tp // seg

    def split(a):
        a = jnp.pad(a, ((0, 0), (0, Tp - T), (0, 0), (0, 0)))
        a = a.reshape(B, Tp // dil, dil, H, Dh).transpose(0, 2, 1, 3, 4)
        return a.reshape(B, dil, nb, n, H, Dh)

    def with_prev(a):
        prev = jnp.pad(a, ((0, 0), (0, 0), (1, 0), (0, 0), (0, 0), (0, 0)))[:, :, :-1]
        return jnp.concatenate([prev, a], axis=3)

    qs = split(q)
    ks = with_prev(split(k))
    vs = with_prev(split(v))
    s = jnp.einsum('brnqhd,brnkhd->brnhqk', qs, ks).astype(jnp.float32) * (HEAD_DIM ** -0.5)
    a_idx = jnp.arange(n)[:, None]
    c_idx = jnp.arange(2 * n)[None, :]
    band = (c_idx >= a_idx) & (c_idx <= a_idx + n)
    not_first = (jnp.arange(nb) > 0)[:, None, None]
    mask = band[None] & (not_first | (c_idx >= n)[None])
    s = jnp.where(mask[None, None, :, None], s, -jnp.inf)
    lse = jax.nn.logsumexp(s, axis=-1)
    pr = jnp.exp(s - lse[..., None])
    o = jnp.einsum('brnhqk,brnkhd->brnqhd', pr.astype(v.dtype), vs)
    o = o.reshape(B, dil, Tp // dil, H, Dh).transpose(0, 2, 1, 3, 4).reshape(B, Tp, H, Dh)[:, :T]
    lse = lse.transpose(0, 1, 2, 4, 3).reshape(B, dil, Tp // dil, H).transpose(0, 2, 1, 3)
    lse = lse.reshape(B, Tp, H)[:, :T]
    return o, lse


def dilated_mixture(q, k, v):
    outs, lses = [], []
    for window, dil in DILATED_PAIRS:
        o, l = dilated_branch(q, k, v, window, dil)
        outs.append(o)
        lses.append(l)
    wts = jax.nn.softmax(jnp.stack(lses, axis=0), axis=0)
    o = jnp.sum(wts[..., None] * jnp.stack(outs, axis=0).astype(jnp.float32), axis=0)
    return o.astype(q.dtype)


def sparse_attention(q, k, v, qi, ki, wi, topk):
    B, T, H, Dh = q.shape
    nb = T // Q_BLOCK

    def blk(a):
        return a.reshape((B, nb, Q_BLOCK) + a.shape[2:]).swapaxes(0, 1)

    key_pos = jnp.arange(T)

    def one_block(args):
        qb, qib, wib, start = args
        t = start + jnp.arange(Q_BLOCK)
        sc = jnp.einsum('bqhd,bkd->bqhk', qib, ki).astype(jnp.float32) * (IDX_DIM ** -0.5)
        score = jnp.einsum('bqhk,bqh->bqk', jax.nn.relu(sc),
                           wib.astype(jnp.float32) * (IDX_HEADS ** -0.5))
        causal = key_pos[None, :] <= t[:, None]
        score = jnp.where(causal[None], score, -jnp.inf)
        _, sel = lax.top_k(score, topk)
        valid = sel <= t[None, :, None]
        kg = jax.vmap(lambda a, i: a[i])(k, sel)
        vg = jax.vmap(lambda a, i: a[i])(v, sel)
        s = jnp.einsum('bqhd,bqkhd->bhqk', qb, kg).astype(jnp.float32) * (HEAD_DIM ** -0.5)
        s = jnp.where(valid[:, None], s, -jnp.inf)
        pr = jax.nn.softmax(s, axis=-1)
        return jnp.einsum('bhqk,bqkhd->bqhd', pr.astype(v.dtype), vg)

    out = lax.map(one_block, (blk(q), blk(qi), blk(wi), jnp.arange(nb) * Q_BLOCK))
    return out.swapaxes(0, 1).reshape(B, T, H, Dh)


def causal_depthwise_conv(u, w, b):
    C = u.shape[-1]
    up = jnp.pad(u, ((0, 0), (CONV_WIDTH - 1, 0), (0, 0)))
    y = lax.conv_general_dilated(up, w[:, None, :].astype(u.dtype), window_strides=(1,),
                                 padding='VALID', dimension_numbers=('NWC', 'WIO', 'NWC'),
                                 feature_group_count=C)
    return y + b.astype(u.dtype)


def setup_inputs(seed: int = 0) -> dict:
    key = jax.random.key(seed)
    ks = jax.random.split(key, 16)
    f32 = jnp.float32
    nrm = lambda k, shape, scale: jax.random.normal(k, shape, f32) * scale
    x = nrm(ks[0], (BATCH, SEQ, D_MODEL), 1.0)
    p = nrm(ks[1], (DEPTH, BATCH, SEQ, PLE_DIM), 1.0)
    offsets = jax.random.randint(ks[2], (BATCH, 1), 0, 4096, dtype=jnp.int32)
    positions = offsets + jnp.arange(SEQ, dtype=jnp.int32)[None, :]
    attn_norm = 1.0 + nrm(ks[3], (DEPTH, D_MODEL), 0.02)
    w_in = nrm(ks[4], (DEPTH, D_MODEL, IN_COLS), D_MODEL ** -0.5)
    w_o = nrm(ks[5], (DEPTH, MIX_WIDTH, D_MODEL), MIX_WIDTH ** -0.5)
    ffn_norm = 1.0 + nrm(ks[6], (DEPTH, D_MODEL), 0.02)
    w_up = nrm(ks[7], (DEPTH, D_MODEL, 2 * D_FF), D_MODEL ** -0.5)
    conv_w = nrm(ks[8], (DEPTH, CONV_WIDTH, 2 * D_FF), CONV_WIDTH ** -0.5)
    conv_b = nrm(ks[9], (DEPTH, 2 * D_FF), 0.01)
    w_down = nrm(ks[10], (DEPTH, D_FF, D_MODEL), D_FF ** -0.5)
    ple_norm = 1.0 + nrm(ks[11], (DEPTH, D_MODEL), 0.02)
    w_ple_gate = nrm(ks[12], (DEPTH, D_MODEL, D_MODEL), D_MODEL ** -0.5)
    w_ple_proj = nrm(ks[13], (DEPTH, PLE_DIM, D_MODEL), PLE_DIM ** -0.5)
    final_norm = 1.0 + nrm(ks[14], (D_MODEL,), 0.02)
    return {"x": x, "p": p, "positions": positions, "attn_norm": attn_norm, "w_in": w_in,
            "w_o": w_o, "ffn_norm": ffn_norm, "w_up": w_up, "conv_w": conv_w, "conv_b": conv_b,
            "w_down": w_down, "ple_norm": ple_norm, "w_ple_gate": w_ple_gate,
            "w_ple_proj": w_ple_proj, "final_norm": final_norm}


def reference(x, p, positions, attn_norm, w_in, w_o, ffn_norm, w_up, conv_w, conv_b,
              w_down, ple_norm, w_ple_gate, w_ple_proj, final_norm):
    B, T, _ = x.shape
    topk = min(TOPK_MAX, T // 4)
    cos, sin = rope_tables(positions)
    splits = np.cumsum([A_W, A_W, A_W, B_W, B_W, B_W, IQ_W, IDX_DIM])
    for i in range(DEPTH):
        h = rmsnorm(x, attn_norm[i])
        z = h @ w_in[i]
        qa, ka, va, qb, kb, vb, qi, ki, wi = jnp.split(z, splits, axis=-1)
        heads = lambda a, n, d: a.reshape(B, T, n, d)
        qa = apply_rope(heads(qa, N_HEADS_A, HEAD_DIM), cos, sin)
        ka = apply_rope(heads(ka, N_HEADS_A, HEAD_DIM), cos, sin)
        va = heads(va, N_HEADS_A, HEAD_DIM)
        qb = apply_rope(heads(qb, N_HEADS_B, HEAD_DIM), cos, sin)
        kb = apply_rope(heads(kb, N_HEADS_B, HEAD_DIM), cos, sin)
        vb = heads(vb, N_HEADS_B, HEAD_DIM)
        qi = apply_rope(heads(qi, IDX_HEADS, IDX_DIM), cos, sin)
        ki = apply_rope(heads(ki, 1, IDX_DIM), cos, sin)[:, :, 0]
        out_a = dilated_mixture(qa, ka, va)
        out_b = sparse_attention(qb, kb, vb, qi, ki, wi, topk)
        mix = jnp.concatenate([out_a.reshape(B, T, A_W), out_b.reshape(B, T, B_W)], axis=-1)
        x = x + mix @ w_o[i]
        u = rmsnorm(x, ffn_norm[i]) @ w_up[i]
        u = causal_depthwise_conv(u, conv_w[i], conv_b[i])
        g, up = jnp.split(u, 2, axis=-1)
        x = x + (jax.nn.silu(g) * up) @ w_down[i]
        gate = jax.nn.sigmoid(rmsnorm(x, ple_norm[i]) @ w_ple_gate[i])
        x = x + gate * (p[i] @ w_ple_proj[i])
    return rmsnorm(x, final_norm)
```

```python
import os
from contextlib import ExitStack

import numpy as np
import concourse.bass as bass
import concourse.mybir as mybir
from concourse.bass_utils import run_bass_kernel_spmd

F32 = mybir.dt.float32
BF16 = mybir.dt.bfloat16
I32 = mybir.dt.int32
AF = mybir.ActivationFunctionType
ALU = mybir.AluOpType

ENGS = ("pe", "act", "dve", "pool", "sp")


class Buf:
    __slots__ = ("name", "last_w", "readers", "sem", "semcnt")

    def __init__(self, name=""):
        self.name = name
        self.last_w = None
        self.readers = []
        self.sem = None
        self.semcnt = 0


class Op:
    __slots__ = ("eng", "fn", "deps", "is_dma", "owner", "needs_inc", "semval", "phase", "group", "owner_sem")

    def __init__(self, eng, fn):
        self.eng = eng
        self.fn = fn
        self.deps = []
        self.is_dma = False
        self.owner = None
        self.needs_inc = False
        self.semval = None
        self.group = None
        self.phase = 0


class Group:
    def __init__(self, name):
        self.name = name
        self.sem = None
        self.base = 0
        self.n = 0


class Sched:
    def __init__(self, nc, sems):
        self.nc = nc
        self.free_sems = list(sems)
        self.ops = []
        self.eng_sem = {}
        self.eng_cnt = {}
        for e in ("pe", "act", "dve", "pool"):
            self.eng_sem[e] = self.free_sems.pop()
            self.eng_cnt[e] = 0
        self.dma_bufs = []
        self.groups = []
        self.phase = 0
        self.free_dma = []

    def _track(self, op, reads, writes):
        deps = []
        op.phase = self.phase
        for b in reads:
            if b.last_w is not None:
                deps.append(b.last_w)
        for b in writes:
            if b.last_w is not None:
                deps.append(b.last_w)
            for r in b.readers:
                deps.append(r)
        for b in writes:
            b.last_w = op
            b.readers = []
        for b in reads:
            if not op.is_dma:
                b.readers = [r for r in b.readers if r.is_dma or r.eng != op.eng]
            b.readers.append(op)
        seen = set()
        out = []
        for d in deps:
            if d is op or id(d) in seen or d.phase != self.phase:
                continue
            if op.is_dma and d.is_dma and op.group is not None and d.group is op.group:
                continue
            seen.add(id(d))
            out.append(d)
        op.deps = out

    def op(self, eng, fn, reads=(), writes=()):
        o = Op(eng, fn)
        self._track(o, reads, writes)
        self.ops.append(o)
        return o

    def dma(self, eng, fn, reads=(), writes=(), owner=None, group=None):
        o = Op(eng, fn)
        o.is_dma = True
        o.owner = owner
        o.group = group
        assert (owner is None) != (group is None)
        self._track(o, reads, writes)
        self.ops.append(o)
        return o

    def new_group(self, name):
        g = Group(name)
        g.sem = self.free_sems.pop()
        self.groups.append(g)
        return g

    def release(self, bufs):
        for b in bufs:
            if b.sem is not None:
                self.dma_bufs.remove(b)
                self.free_dma.append((b.sem, b.semcnt))
                b.sem = None

    def emit_phase(self):
        nc = self.nc
        ops = self.ops
        self.ops = []
        for o in ops:
            for d in o.deps:
                if d.is_dma:
                    continue
                if d.eng == o.eng and o.eng == "pe":
                    continue
                d.needs_inc = True
        for o in ops:
            if o.is_dma:
                if o.group is not None:
                    o.group.n += 1
                else:
                    b = o.owner
                    if b.sem is None:
                        if self.free_dma:
                            b.sem, b.semcnt = self.free_dma.pop()
                        else:
                            b.sem = self.free_sems.pop()
                            b.semcnt = 0
                        self.dma_bufs.append(b)
                    b.semcnt += 16
                    o.semval = b.semcnt
            elif o.needs_inc:
                self.eng_cnt[o.eng] += 1
                o.semval = self.eng_cnt[o.eng]
        per = {e: [] for e in ENGS}
        for o in ops:
            per[o.eng].append(o)
        waited = {e: {} for e in ENGS}
        final_waits = [(b.sem, b.semcnt) for b in self.dma_bufs]
        for g in self.groups:
            if g.n:
                final_waits.append((g.sem, g.base + 16 * g.n))

        def run(eng_name, eng):
            w = waited[eng_name]
            for o in per[eng_name]:
                for d in o.deps:
                    if d.is_dma:
                        if d.group is not None:
                            sem, val = d.group.sem, d.group.base + 16 * d.group.n
                        else:
                            sem, val = d.owner_sem, d.semval
                    else:
                        if d.eng == eng_name and eng_name == "pe":
                            continue
                        sem, val = self.eng_sem[d.eng], d.semval
                    key = id(sem)
                    if w.get(key, 0) >= val:
                        continue
                    w[key] = val
                    eng.wait_ge(sem, val)
                ins = o.fn(eng)
                if o.is_dma:
                    ins.then_inc(o.group.sem if o.group is not None else o.owner_sem, 16)
                elif o.needs_inc:
                    ins.then_inc(self.eng_sem[o.eng], 1)
            if eng_name == "sp":
                for sem, val in final_waits:
                    key = id(sem)
                    if w.get(key, 0) >= val:
                        continue
                    w[key] = val
                    eng.wait_ge(sem, val)

        for o in ops:
            if o.is_dma and o.group is None:
                o.owner_sem = o.owner.sem

        with nc.Block() as block:
            @block.tensor
            def _(e):
                run("pe", e)

            @block.scalar
            def _(e):
                run("act", e)

            @block.vector
            def _(e):
                run("dve", e)

            @block.gpsimd
            def _(e):
                run("pool", e)

            @block.sync
            def _(e):
                run("sp", e)
        for g in self.groups:
            g.base += 16 * g.n
            g.n = 0
        self.phase += 1


T = 8192
D = 1024
NCORE = 8
OWN = 2048
NB_ALL = 64
NB_A = 33
NQ = 17
IN_COLS = 3656
D_FF = 2816
NCH = 44
C_QA, C_KA, C_VA, C_QB, C_KB, C_VB, C_QI, C_KI, C_WI = 0, 512, 1024, 1536, 2048, 2560, 3072, 3584, 3648
RMS_EPS = 1e-6
BIG = 1.0e30
NBIS = 18
BIS_B = 16.0
DBG = os.environ.get("MK_DBG", "")
STOP_AFTER = os.environ.get("MK_STOP", "")


class Ctx:
    pass


def build_program():
    nc = bass.Bass("TRN2", target_bir_lowering=False)
    C = Ctx()
    C.nc = nc
    din = lambda name, shape, dt=F32: nc.dram_tensor(name, shape, dt, kind="ExternalInput").ap()
    C.xall = din("xall", [T, D])
    C.xa = din("xa", [NB_A * 128, D])
    C.pos = din("pos", [128, NB_ALL + NB_A], I32)
    C.avalid = din("avalid", [128, NB_A])
    C.tq = din("tq", [128, NQ])
    C.invf = din("invf", [1, 32])
    C.amask = din("amask", [128, NQ * 128])
    C.p_own = din("p_own", [OWN, 256])
    C.w_in = din("w_in", [D, IN_COLS])
    C.w_o = din("w_o", [D, D])
    C.w_up = din("w_up", [D, 2 * D_FF])
    C.w_down = din("w_down", [D_FF, D])
    C.w_g = din("w_g", [D, D])
    C.w_p = din("w_p", [256, D])
    C.attn_norm = din("attn_norm", [1, D])
    C.ffn_norm = din("ffn_norm", [1, D])
    C.ple_norm = din("ple_norm", [1, D])
    C.final_norm = din("final_norm", [1, D])
    C.conv_w = din("conv_w", [3, 2 * D_FF])
    C.conv_b = din("conv_b", [1, 2 * D_FF])
    C.out = nc.dram_tensor("out", [OWN, D], F32, kind="ExternalOutput").ap()
    C.vb_d = nc.dram_tensor("vb_d", [NB_ALL, 128, 520], BF16).ap()
    C.va_d = nc.dram_tensor("va_d", [NB_A, 128, 520], BF16).ap()
    C.kaT_d = nc.dram_tensor("kaT_d", [NB_A, 128, 512], BF16).ap()
    C.qaT_d = nc.dram_tensor("qaT_d", [NQ, 128, 512], BF16).ap()
    C.qbT_d = nc.dram_tensor("qbT_d", [NQ, 128, 512], BF16).ap()
    C.qiT_d = nc.dram_tensor("qiT_d", [NQ, 128, 512], BF16).ap()
    C.lohi_d = nc.dram_tensor("lohi_d", [NQ, 128, 16], F32).ap()
    C.mix_d = nc.dram_tensor("mix_d", [NQ, 128, 1024], BF16).ap()
    C.x2_d = nc.dram_tensor("x2_d", [OWN, D], F32).ap()
    C.wob_d = nc.dram_tensor("wob_d", [D, D], BF16).ap()
    C.wupb_d = nc.dram_tensor("wupb_d", [D, 2 * D_FF], BF16).ap()
    C.wdnb_d = nc.dram_tensor("wdnb_d", [D_FF, D], BF16).ap()
    C.wgb_d = nc.dram_tensor("wgb_d", [D, D], BF16).ap()
    C.wpb_d = nc.dram_tensor("wpb_d", [256, D], BF16).ap()
    C.dbg = {}
    if DBG:
        dout = lambda name, shape, dt=F32: nc.dram_tensor(name, shape, dt, kind="ExternalOutput").ap()
        C.dbg["kbT"] = dout("dbg_kbT", [128, NB_ALL * 512], BF16)
        C.dbg["kiT"] = dout("dbg_kiT", [128, T], BF16)
        C.dbg["mix"] = dout("dbg_mix", [NQ, 128, 1024], BF16)
        C.dbg["x2"] = dout("dbg_x2", [OWN, D])

    with ExitStack() as top:
        sems = [top.enter_context(nc.semaphore(f"s{i}")) for i in range(96)]
        S = Sched(nc, sems)
        C.S = S
        C.ident = top.enter_context(nc.sbuf_tensor("ident", [128, 128], BF16))
        C.identf = top.enter_context(nc.sbuf_tensor("identf", [128, 128], F32))
        C.Bident = Buf("ident")
        with ExitStack() as kv:
            C.kbT = kv.enter_context(nc.sbuf_tensor("kbT", [128, NB_ALL, 512], BF16))
            C.kiT = kv.enter_context(nc.sbuf_tensor("kiT", [128, T], BF16))
            C.BkbT = [Buf(f"kbT{i}") for i in range(NB_ALL)]
            C.BkiT = [Buf(f"kiT{i}") for i in range(NB_ALL)]
            phase1(C)
            if STOP_AFTER != "1":
                phaseA(C)
            if STOP_AFTER not in ("1", "A"):
                phaseB(C)
        if STOP_AFTER not in ("1", "A", "B"):
            phaseC1(C)
            if STOP_AFTER != "C1":
                phaseC2(C)
        if STOP_AFTER:
            final_dummy(C)
    return nc


def final_dummy(C):
    nc, S = C.nc, C.S
    with nc.sbuf_tensor("zz", [128, 1024], F32) as zz:
        Bz = Buf()
        S.op("dve", lambda e: e.memset(zz[:], 0.0), writes=[Bz])
        for i in range(16):
            S.dma("sp", lambda e, i=i: e.dma_start(out=C.out[i * 128:(i + 1) * 128, :], in_=zz[:]), reads=[Bz], owner=Bz)
        S.emit_phase()


def phase1(C):
    nc, S = C.nc, C.S
    NTB = NB_ALL + NB_A
    with ExitStack() as es:
        sb = lambda n, s, d: es.enter_context(nc.sbuf_tensor(n, s, d))
        pst = lambda n, s, d: es.enter_context(nc.psum_tensor(n, s, d))
        win = sb("win", [128, 8, IN_COLS], BF16)
        gat = sb("gat", [128, D], F32)
        posi = sb("posi", [128, NTB], I32)
        posf = sb("posf", [128, NTB], F32)
        avl = sb("avl", [128, NB_A], F32)
        invt = sb("invt", [128, 32], F32)
        cosT = sb("cosT", [128, NTB, 32], F32)
        sinT = sb("sinT", [128, NTB, 32], F32)
        identf = C.identf
        ones8 = sb("ones8", [128, 8], F32)
        xt = [sb(f"xt{i}", [128, D], F32) for i in range(2)]
        sqj = sb("sqj", [128, D], BF16)
        ss = [sb(f"ss{i}", [128, 1], F32) for i in range(2)]
        rstd = [sb(f"rstd{i}", [128, 1], F32) for i in range(2)]
        hb = [sb(f"hb{i}", [128, D], BF16) for i in range(2)]
        hT = [sb(f"hT{i}", [128, 8, 128], BF16) for i in range(2)]
        ta = [sb(f"ta{i}", [128, 512], F32) for i in range(2)]
        tb_ = [sb(f"tb{i}", [128, 512], F32) for i in range(2)]
        zb = [sb(f"zb{i}", [128, 512], BF16) for i in range(6)]
        zTs = [sb(f"zTs{i}", [128, 512], BF16) for i in range(3)]
        vsb = [sb(f"vsb{i}", [128, 520], BF16) for i in range(2)]
        vsa = [sb(f"vsa{i}", [128, 520], BF16) for i in range(2)]
        wsc = [sb(f"wsc{i}", [128, 8], F32) for i in range(2)]
        lohi = [sb(f"lohi{i}", [128, 16], F32) for i in range(2)]
        hT_ps = [pst(f"hTps{i}", [128, 1024], BF16) for i in range(2)]
        z_ps = [pst(f"zps{i}", [128, 512], F32) for i in range(4)]
        zT_ps = [pst(f"zTps{i}", [128, 1024], BF16) for i in range(2)]

        Bwin, Bgat, Bpos, Bavl, Binv, Bcs, Bidf, Bones = (Buf() for _ in range(8))
        Bxt = [Buf() for _ in range(2)]
        Bsq = Buf()
        Bss = [Buf() for _ in range(2)]
        Brs = [Buf() for _ in range(2)]
        Bhb = [Buf() for _ in range(2)]
        BhT = [Buf() for _ in range(2)]
        Bta = [Buf() for _ in range(2)]
        Btb = [Buf() for _ in range(2)]
        Bzb = [Buf() for _ in range(6)]
        BzTs = [Buf() for _ in range(3)]
        Bvsb = [Buf() for _ in range(2)]
        Bvsa = [Buf() for _ in range(2)]
        Bwsc = [Buf() for _ in range(2)]
        Blohi = [Buf() for _ in range(2)]
        BhTps = [Buf() for _ in range(2)]
        Bzps = [Buf() for _ in range(4)]
        BzTps = [Buf() for _ in range(2)]

        g0 = S.new_group("p1init")
        gw = S.new_group("p1w")
        S.dma("sp", lambda e: e.dma_start(out=posi[:], in_=C.pos), writes=[Bpos], group=g0)
        S.dma("sp", lambda e: e.dma_start(out=avl[:], in_=C.avalid), writes=[Bavl], group=g0)
        S.dma("sp", lambda e: e.dma_start(out=invt[:], in_=C.invf.partition_broadcast(128)), writes=[Binv], group=g0)
        S.dma("sp", lambda e: e.dma_start(out=gat[:], in_=C.attn_norm.partition_broadcast(128)), writes=[Bgat], group=g0)
        gw2 = S.new_group("p1w2")
        Bwin2 = Buf()
        for (c0, c1, grp, bw) in ((2048, IN_COLS, gw, Bwin), (0, 2048, gw2, Bwin2)):
            for m in range(8):
                S.dma("pool", lambda e, m=m, c0=c0, c1=c1: e.dma_start(out=win[:, m, c0:c1], in_=C.w_in[m * 128:(m + 1) * 128, c0:c1]),
                      writes=[bw], group=grp)
        S.op("pool", lambda e: e.memset(identf[:], 0.0), writes=[Bidf])
        S.op("pool", lambda e: e.affine_select(out=identf[:], in_=identf[:], compare_op=ALU.not_equal, fill=1.0,
                                               base=0, pattern=[[-1, 128]], channel_multiplier=1), reads=[Bidf], writes=[Bidf])
        S.op("dve", lambda e: e.tensor_copy(out=C.ident[:], in_=identf[:]), reads=[Bidf], writes=[C.Bident])
        S.op("dve", lambda e: e.memset(ones8[:], 1.0), writes=[Bones])
        for i in range(2):
            S.op("dve", lambda e, i=i: e.memset(vsb[i][:], 1.0), writes=[Bvsb[i]])
            S.op("dve", lambda e, i=i: e.memset(vsa[i][:], 1.0), writes=[Bvsa[i]])
        cnt = {"z": 0, "zT": 0, "zTs": 0, "vsb": 0, "vsa": 0, "zb": 0, "t": 0}

        def mm_group(s, c0, n):
            r = cnt["z"] % 4
            cnt["z"] += 1
            for m in range(8):
                S.op("pe", lambda e, m=m, r=r, s=s: e.matmul(z_ps[r][:, 0:n], lhsT=hT[s][:, m, :], rhs=win[:, m, c0:c0 + n],
                                                             start=(m == 0), stop=(m == 7)),
                     reads=[BhT[s], Bwin if c0 >= 2048 else Bwin2], writes=[Bzps[r]])
            return r

        def rope(r, n, tbk):
            zi = cnt["zb"] % 6
            cnt["zb"] += 1
            ti = cnt["t"] % 2
            cnt["t"] += 1
            H = n // 64
            zv = z_ps[r][:, 0:n].rearrange("p (h t d) -> p h t d", h=H, t=2)
            A4 = ta[ti][:, 0:n].rearrange("p (h t d) -> p h t d", h=H, t=2)
            B4 = tb_[ti][:, 0:n].rearrange("p (h t d) -> p h t d", h=H, t=2)
            Z4 = zb[zi][:, 0:n].rearrange("p (h t d) -> p h t d", h=H, t=2)
            cosb = cosT[:, tbk, :].unsqueeze(1).unsqueeze(1).to_broadcast([128, H, 2, 32])
            sinb = sinT[:, tbk, :].unsqueeze(1).to_broadcast([128, H, 32])
            S.op("dve", lambda e: e.tensor_tensor(out=A4, in0=zv, in1=cosb, op=ALU.mult), reads=[Bzps[r], Bcs], writes=[Bta[ti]])
            S.op("dve", lambda e: e.tensor_tensor(out=B4[:, :, 0, :], in0=zv[:, :, 1, :], in1=sinb, op=ALU.mult),
                 reads=[Bzps[r], Bcs], writes=[Btb[ti]])
            S.op("dve", lambda e: e.tensor_tensor(out=B4[:, :, 1, :], in0=zv[:, :, 0, :], in1=sinb, op=ALU.mult),
                 reads=[Bzps[r], Bcs], writes=[Btb[ti]])
            S.op("pool", lambda e: e.tensor_tensor(out=Z4[:, :, 0, :], in0=A4[:, :, 0, :], in1=B4[:, :, 0, :], op=ALU.subtract),
                 reads=[Bta[ti], Btb[ti]], writes=[Bzb[zi]])
            S.op("pool", lambda e: e.tensor_tensor(out=Z4[:, :, 1, :], in0=A4[:, :, 1, :], in1=B4[:, :, 1, :], op=ALU.add),
                 reads=[Bta[ti], Btb[ti]], writes=[Bzb[zi]])
            return zi

        pending = []

        def transposeT(zi, ncols, dst_fn, dst_bufs_w, then=None):
            def run():
                q = cnt["zT"] % 2
                cnt["zT"] += 1
                nt = ncols // 128
                for j in range(nt):
                    S.op("pe", lambda e, j=j, q=q: e.transpose(out=zT_ps[q][:, j * 128:(j + 1) * 128], in_=zb[zi][:, j * 128:(j + 1) * 128],
                                                               identity=C.ident[:]),
                         reads=[Bzb[zi], C.Bident], writes=[BzTps[q]])
                S.op("act", lambda e, q=q: e.activation(out=dst_fn(), in_=zT_ps[q][:, 0:ncols], func=AF.Copy),
                     reads=[BzTps[q]], writes=dst_bufs_w)
                if then is not None:
                    then()
            pending.append(run)

        def to_dram_T(zi, dram_ap):
            k = cnt["zTs"] % 3
            cnt["zTs"] += 1
            transposeT(zi, 512, lambda k=k: zTs[k][:], [BzTs[k]],
                       then=lambda k=k: S.dma("sp", lambda e, k=k: e.dma_start(out=dram_ap, in_=zTs[k][:]), reads=[BzTs[k]], writes=[Buf()],
                                              owner=BzTs[k]))

        def vcopy(r, dst, Bdst):
            S.op("act", lambda e: e.activation(out=dst[:].rearrange("p (h c) -> p h c", c=65)[:, :, 0:64],
                                               in_=z_ps[r][:, :].rearrange("p (h d) -> p h d", d=64), func=AF.Copy),
                 reads=[Bzps[r]], writes=[Bdst])

        def load_x(tbk):
            s = tbk % 2
            is_all = tbk < NB_ALL
            blk = tbk if is_all else tbk - NB_ALL
            src = C.xall if is_all else C.xa
            S.dma("sp", lambda e: e.dma_start(out=xt[s][:], in_=src[blk * 128:(blk + 1) * 128, :]), writes=[Bxt[s]], owner=Bxt[s])

        def stageA(tbk):
            s = tbk % 2
            if tbk + 1 < NTB:
                load_x(tbk + 1)
            S.op("act", lambda e, s=s: e.activation(out=sqj[:], in_=xt[s][:], func=AF.Square, accum_out=ss[s][:, 0:1]),
                 reads=[Bxt[s]], writes=[Bsq, Bss[s]])
            S.op("act", lambda e, s=s: e.activation(out=rstd[s][:], in_=ss[s][:], func=AF.Sqrt, scale=1.0 / D, bias=RMS_EPS),
                 reads=[Bss[s]], writes=[Brs[s]])
            S.op("dve", lambda e, s=s: e.reciprocal(out=rstd[s][:], in_=rstd[s][:]), reads=[Brs[s]], writes=[Brs[s]])
            S.op("dve", lambda e, s=s: e.scalar_tensor_tensor(out=hb[s][:], in0=xt[s][:], scalar=rstd[s][:, 0:1], in1=gat[:],
                                                              op0=ALU.mult, op1=ALU.mult),
                 reads=[Bxt[s], Brs[s], Bgat], writes=[Bhb[s]])
            for m in range(8):
                S.op("pe", lambda e, s=s, m=m: e.transpose(out=hT_ps[s][:, m * 128:(m + 1) * 128], in_=hb[s][:, m * 128:(m + 1) * 128],
                                                           identity=C.ident[:]),
                     reads=[Bhb[s], C.Bident], writes=[BhTps[s]])
            S.op("act", lambda e, s=s: e.activation(out=hT[s][:].rearrange("p m t -> p (m t)"), in_=hT_ps[s][:], func=AF.Copy),
                 reads=[BhTps[s]], writes=[BhT[s]])

        load_x(0)
        stageA(0)
        S.op("dve", lambda e: e.tensor_copy(out=posf[:], in_=posi[:]), reads=[Bpos], writes=[Bpos])
        MAGIC = 12582912.0
        TWO_PI = float(2 * np.pi)
        CH = 16
        for b0 in range(0, NTB, CH):
            nb_ = min(CH, NTB - b0)
            ang = ta[0][:, 0:nb_ * 32].rearrange("p (b d) -> p b d", d=32)
            kk = tb_[0][:, 0:nb_ * 32].rearrange("p (b d) -> p b d", d=32)
            Bang, Bkk = Bta[0], Btb[0]
            S.op("dve", lambda e, ang=ang, b0=b0, nb_=nb_: e.tensor_tensor(
                out=ang, in0=posf[:, b0:b0 + nb_].unsqueeze(2).to_broadcast([128, nb_, 32]),
                in1=invt[:, :].unsqueeze(1).to_broadcast([128, nb_, 32]), op=ALU.mult),
                reads=[Bpos, Binv], writes=[Bang])
            for dst, shift in ((sinT, 0.0), (cosT, float(np.pi / 2))):
                S.op("dve", lambda e, ang=ang, kk=kk, shift=shift: e.tensor_scalar(out=kk, in0=ang, scalar1=shift,
                                                                                   scalar2=1.0 / TWO_PI, op0=ALU.add, op1=ALU.mult),
                     reads=[Bang], writes=[Bkk])
                S.op("dve", lambda e, kk=kk: e.tensor_scalar(out=kk, in0=kk, scalar1=MAGIC, scalar2=None, op0=ALU.add),
                     reads=[Bkk], writes=[Bkk])
                S.op("dve", lambda e, kk=kk: e.tensor_scalar(out=kk, in0=kk, scalar1=MAGIC, scalar2=-TWO_PI,
                                                             op0=ALU.subtract, op1=ALU.mult), reads=[Bkk], writes=[Bkk])
                S.op("dve", lambda e, ang=ang, kk=kk, shift=shift: e.scalar_tensor_tensor(out=kk, in0=ang, scalar=shift, in1=kk,
                                                                                          op0=ALU.add, op1=ALU.add),
                     reads=[Bang, Bkk], writes=[Bkk])
                S.op("dve", lambda e, kk=kk: e.tensor_scalar(out=kk, in0=kk, scalar1=float(np.pi), scalar2=float(-np.pi),
                                                             op0=ALU.min, op1=ALU.max), reads=[Bkk], writes=[Bkk])
                S.op("act", lambda e, kk=kk, dst=dst, b0=b0, nb_=nb_: e.activation(out=dst[:, b0:b0 + nb_, :], in_=kk, func=AF.Sin),
                     reads=[Bkk], writes=[Bcs])

        for tbk in range(NTB):
            s = tbk % 2
            is_all = tbk < NB_ALL
            blk = tbk if is_all else tbk - NB_ALL
            isq = (not is_all) and blk >= NB_A - NQ
            qi_ = blk - (NB_A - NQ)
            if is_all:
                groups = [("kb", C_KB, 512), ("vb", C_VB, 512), ("ki", C_KI, 64)]
            elif not isq:
                groups = [("ka", C_KA, 512), ("va", C_VA, 512)]
            else:
                groups = [("ka", C_KA, 512), ("va", C_VA, 512), ("wi", C_WI, 8), ("qa", C_QA, 512), ("qb", C_QB, 512), ("qi", C_QI, 512)]
            for g0i in range(0, len(groups), 3):
                rnd = groups[g0i:g0i + 3]
                rs = [mm_group(s, c0, n) for (_, c0, n) in rnd]
                if g0i == 0 and tbk + 1 < NTB:
                    stageA(tbk + 1)
                for fn in pending:
                    fn()
                pending.clear()
                for (kind, c0, n), r in zip(rnd, rs):
                    if kind == "kb":
                        zi = rope(r, 512, tbk)
                        transposeT(zi, 512, lambda blk=blk: C.kbT[:, blk, :], [C.BkbT[blk]])
                    elif kind == "vb":
                        k = cnt["vsb"] % 2
                        cnt["vsb"] += 1
                        vcopy(r, vsb[k], Bvsb[k])
                        S.dma("sp", lambda e, k=k, blk=blk: e.dma_start(out=C.vb_d[blk], in_=vsb[k][:]), reads=[Bvsb[k]], writes=[Buf()],
                              owner=Bvsb[k])
                    elif kind == "ki":
                        zi = rope(r, 64, tbk)
                        S.op("pool", lambda e, zi=zi: e.tensor_copy(out=zb[zi][:, 64:128], in_=zb[zi][:, 0:64]), reads=[Bzb[zi]], writes=[Bzb[zi]])
                        transposeT(zi, 128, lambda blk=blk: C.kiT[:, blk * 128:(blk + 1) * 128], [C.BkiT[blk]])
                    elif kind == "ka":
                        zi = rope(r, 512, tbk)
                        to_dram_T(zi, C.kaT_d[blk])
                    elif kind == "va":
                        k = cnt["vsa"] % 2
                        cnt["vsa"] += 1
                        vcopy(r, vsa[k], Bvsa[k])
                        S.op("dve", lambda e, k=k, blk=blk: e.tensor_scalar(out=vsa[k][:].rearrange("p (h c) -> p h c", c=65)[:, :, 64],
                                                                            in0=ones8[:], scalar1=avl[:, blk:blk + 1], scalar2=None, op0=ALU.mult),
                             reads=[Bones, Bavl], writes=[Bvsa[k]])
                        S.dma("sp", lambda e, k=k, blk=blk: e.dma_start(out=C.va_d[blk], in_=vsa[k][:]), reads=[Bvsa[k]], writes=[Buf()],
                              owner=Bvsa[k])
                    elif kind == "qa":
                        zi = rope(r, 512, tbk)
                        to_dram_T(zi, C.qaT_d[qi_])
                    elif kind == "qb":
                        zi = rope(r, 512, tbk)
                        to_dram_T(zi, C.qbT_d[qi_])
                    elif kind == "wi":
                        S.op("dve", lambda e, r=r, s=s: e.tensor_scalar(out=wsc[s][:], in0=z_ps[r][:, 0:8], scalar1=float(1.0 / (8.0 * np.sqrt(8.0))),
                                                                        scalar2=None, op0=ALU.mult), reads=[Bzps[r]], writes=[Bwsc[s]])
                        S.op("dve", lambda e, s=s: e.tensor_scalar(out=lohi[s][:, 8:16], in0=wsc[s][:], scalar1=0.0, scalar2=2.0,
                                                                   op0=ALU.is_ge, op1=ALU.mult), reads=[Bwsc[s]], writes=[Blohi[s]])
                        S.op("dve", lambda e, s=s: e.tensor_scalar(out=lohi[s][:, 0:8], in0=lohi[s][:, 8:16], scalar1=-1.0, scalar2=None,
                                                                   op0=ALU.add), reads=[Blohi[s]], writes=[Blohi[s]])
                        S.dma("sp", lambda e, s=s, qi_=qi_: e.dma_start(out=C.lohi_d[qi_], in_=lohi[s][:]), reads=[Blohi[s]], writes=[Buf()],
                              owner=Blohi[s])
                    elif kind == "qi":
                        zi = rope(r, 512, tbk)
                        S.op("pool", lambda e, s=s, zi=zi: e.tensor_tensor(out=zb[zi][:].rearrange("p (h d) -> p h d", d=64),
                                                                           in0=zb[zi][:].rearrange("p (h d) -> p h d", d=64),
                                                                           in1=wsc[s][:, :].unsqueeze(2).to_broadcast([128, 8, 64]), op=ALU.mult),
                             reads=[Bzb[zi], Bwsc[s]], writes=[Bzb[zi]])
                        to_dram_T(zi, C.qiT_d[qi_])
        for fn in pending:
            fn()
        pending.clear()
        if DBG:
            gd = S.new_group("p1dbg")
            S.dma("sp", lambda e: e.dma_start(out=C.dbg["kbT"], in_=C.kbT[:].rearrange("p b c -> p (b c)")), reads=C.BkbT, writes=[Buf()],
                  group=gd)
            S.dma("sp", lambda e: e.dma_start(out=C.dbg["kiT"], in_=C.kiT[:]), reads=C.BkiT, writes=[Buf()], group=gd)
        S.emit_phase()
        S.release(Bxt + BzTs + Bvsb + Bvsa + Blohi)


def phaseA(C):
    nc, S = C.nc, C.S
    with ExitStack() as es:
        sb = lambda n, s, d: es.enter_context(nc.sbuf_tensor(n, s, d))
        pst = lambda n, s, d: es.enter_context(nc.psum_tensor(n, s, d))
        kaT = sb("kaT", [128, NB_A, 512], BF16)
        va = sb("va", [128, NB_A, 520], BF16)
        qaT = sb("qaT", [128, NQ, 512], BF16)
        amk = sb("amk", [128, NQ, 128], BF16)
        pe_ = [sb(f"pe{i}", [128, 1024], BF16) for i in range(2)]
        pm = [sb(f"pm{i}", [128, 1024], BF16) for i in range(2)]
        rd = [sb(f"rd{i}", [128, 8], F32) for i in range(2)]
        mxa = [sb(f"mxa{i}", [128, 512], BF16) for i in range(2)]
        STA = [pst(f"st{i}", [128, 1024], F32) for i in range(2)]
        st = [[STA[i][:, j * 512:(j + 1) * 512] for j in range(2)] for i in range(2)]
        acc = [[pst(f"acc{i}{j}", [128, 512], F32) for j in range(2)] for i in range(2)]
        BkaT, Bva, BqaT, Bamk = Buf(), Buf(), Buf(), Buf()
        Bpe = [Buf() for _ in range(2)]
        Bpm = [Buf() for _ in range(2)]
        Brd = [Buf() for _ in range(2)]
        Bmxa = [Buf() for _ in range(2)]
        Bst = [[Buf() for _ in range(2)] for _ in range(2)]
        Bacc = [[Buf() for _ in range(2)] for _ in range(2)]
        g = S.new_group("pAinit")
        for c0 in range(0, NB_A, 11):
            S.dma("sp", lambda e, c0=c0: e.dma_start(out=kaT[:, c0:c0 + 11, :], in_=C.kaT_d[c0:c0 + 11].rearrange("b p c -> p b c")),
                  writes=[BkaT], group=g)
            S.dma("sp", lambda e, c0=c0: e.dma_start(out=va[:, c0:c0 + 11, :], in_=C.va_d[c0:c0 + 11].rearrange("b p c -> p b c")),
                  writes=[Bva], group=g)
        S.dma("sp", lambda e: e.dma_start(out=qaT[:], in_=C.qaT_d.rearrange("b p c -> p b c")), writes=[BqaT], group=g)
        gp = S.new_group("pAinit_sw")
        S.dma("pool", lambda e: e.dma_start(out=amk[:].rearrange("p r q -> p (r q)"), in_=C.amask), writes=[Bamk], group=gp)
        def qk_A(i, r, sidx):
            kb = i + r
            for h in range(8):
                rows = slice((h % 2) * 64, (h % 2) * 64 + 64)
                pr = h // 2
                S.op("pe", lambda e, h=h, rows=rows, pr=pr: e.matmul(
                    st[sidx][h % 2][:, pr * 128:(pr + 1) * 128], lhsT=kaT[rows, kb, pr * 128:(pr + 1) * 128],
                    rhs=qaT[rows, i, pr * 128:(pr + 1) * 128], start=True, stop=True),
                    reads=[BkaT, BqaT], writes=[Bst[sidx][h % 2]])

        steps = [(i, r) for i in range(NQ) for r in range(NQ)]
        qk_A(0, 0, 0)
        qk_A(steps[1][0], steps[1][1], 1)
        for it, (i, r) in enumerate(steps):
            a = i % 2
            kb = i + r
            sidx = it % 2
            S.op("act", lambda e, sidx=sidx: e.activation(out=pe_[sidx][:, :], in_=STA[sidx][:, :], func=AF.Exp, scale=0.125),
                 reads=[Bst[sidx][0], Bst[sidx][1]], writes=[Bpe[sidx]])
            if it + 2 < len(steps):
                qk_A(steps[it + 2][0], steps[it + 2][1], sidx)
            S.op("dve", lambda e, sidx=sidx, r=r: e.tensor_tensor(out=pm[sidx][:].rearrange("p (h q) -> p h q", q=128),
                                                                  in0=pe_[sidx][:].rearrange("p (h q) -> p h q", q=128),
                                                                  in1=amk[:, r, :].unsqueeze(1).to_broadcast([128, 8, 128]), op=ALU.mult),
                 reads=[Bpe[sidx], Bamk], writes=[Bpm[sidx]])
            for h in range(8):
                gi = h // 4
                cc = (h % 4) * 65
                pcol = (h % 2) * 512 + (h // 2) * 128
                S.op("pe", lambda e, h=h, gi=gi, cc=cc, pcol=pcol, sidx=sidx, kb=kb, a=a, r=r: e.matmul(
                    acc[a][gi][:, cc:cc + 65], lhsT=pm[sidx][:, pcol:pcol + 128], rhs=va[:, kb, h * 65:(h + 1) * 65],
                    start=(r == 0 and h % 4 == 0), stop=(r == NQ - 1), skip_group_check=True),
                    reads=[Bpm[sidx], Bva], writes=[Bacc[a][gi]])
            if r != NQ - 1:
                continue
            for gi in range(2):
                accv = acc[a][gi][:, 0:260].rearrange("p (h c) -> p h c", c=65)
                S.op("dve", lambda e, a=a, gi=gi, accv=accv: e.tensor_scalar(out=rd[a][:, gi * 4:(gi + 1) * 4].unsqueeze(2), in0=accv[:, :, 64:65],
                                                                             scalar1=1e-30, scalar2=None, op0=ALU.max),
                     reads=[Bacc[a][gi]], writes=[Brd[a]])
                S.op("dve", lambda e, a=a, gi=gi: e.reciprocal(out=rd[a][:, gi * 4:(gi + 1) * 4], in_=rd[a][:, gi * 4:(gi + 1) * 4]),
                     reads=[Brd[a]], writes=[Brd[a]])
                S.op("dve", lambda e, a=a, gi=gi, accv=accv: e.tensor_tensor(
                    out=mxa[a][:, gi * 256:(gi + 1) * 256].rearrange("p (h d) -> p h d", d=64), in0=accv[:, :, 0:64],
                    in1=rd[a][:, gi * 4:(gi + 1) * 4].unsqueeze(2).to_broadcast([128, 4, 64]), op=ALU.mult),
                    reads=[Bacc[a][gi], Brd[a]], writes=[Bmxa[a]])
            S.dma("sp", lambda e, a=a, i=i: e.dma_start(out=C.mix_d[i][:, 0:512], in_=mxa[a][:]), reads=[Bmxa[a]], writes=[Buf()], owner=Bmxa[a])
        S.emit_phase()
        S.release(Bmxa)
        if DBG and STOP_AFTER == "A":
            dump_mix(C)


def dump_mix(C):
    nc, S = C.nc, C.S
    with nc.sbuf_tensor("dm0", [128, 1024], BF16) as dm0, nc.sbuf_tensor("dm1", [128, 1024], BF16) as dm1:
        dm = [dm0, dm1]
        Bd = [Buf(), Buf()]
        for i in range(NQ):
            k = i % 2
            c1 = 512 if STOP_AFTER == "A" else 1024
            S.dma("sp", lambda e, i=i, k=k, c1=c1: e.dma_start(out=dm[k][:, 0:c1], in_=C.mix_d[i][:, 0:c1]), writes=[Bd[k]], owner=Bd[k])
            S.dma("sp", lambda e, i=i, k=k, c1=c1: e.dma_start(out=C.dbg["mix"][i][:, 0:c1], in_=dm[k][:, 0:c1]), reads=[Bd[k]], writes=[Buf()],
                  owner=Bd[k])
        S.emit_phase()
        S.release(Bd)


def phaseB(C):
    nc, S = C.nc, C.S
    NCHK = T // 512
    with ExitStack() as es:
        sb = lambda n, s, d: es.enter_context(nc.sbuf_tensor(n, s, d))
        pst = lambda n, s, d: es.enter_context(nc.psum_tensor(n, s, d))
        iotaN = sb("iotaN", [128, T], F32)
        Iacc = sb("Iacc", [128, T], F32)
        Mb = sb("Mb", [128, T], BF16)
        qbT = [sb(f"qbT{i}", [128, 512], BF16) for i in range(2)]
        qiT = [sb(f"qiT{i}", [128, 512], BF16) for i in range(2)]
        lohi = [sb(f"lohiB{i}", [128, 16], F32) for i in range(2)]
        tqf = sb("tqf", [128, NQ], F32)
        cq = sb("cq", [128, NQ], F32)
        fh = [sb(f"fh{i}", [128, 1024], F32) for i in range(3)]
        lo_ = sb("lo_", [128, 1], F32)
        cand = sb("cand", [128, 1], F32)
        cntt = sb("cntt", [128, 1], F32)
        ind = sb("ind", [128, 1], F32)
        ncand = sb("ncand", [128, 1], F32)
        ssum = sb("ssum", [128, 1], F32)
        vbs = [sb(f"vbs{i}", [128, 8, 520], BF16) for i in range(2)]
        pe_ = [sb(f"peB{i}", [128, 1024], BF16) for i in range(2)]
        pm = [sb(f"pmB{i}", [128, 1024], BF16) for i in range(2)]
        mts = [sb(f"mts{i}", [128, 1024], BF16) for i in range(2)]
        rd = [sb(f"rdB{i}", [128, 8], F32) for i in range(2)]
        mxb = [sb(f"mxb{i}", [128, 512], BF16) for i in range(2)]
        PB = [pst(f"PB{i}", [128, 1024], F32) for i in range(4)]
        BP = [[Buf() for _ in range(2)] for _ in range(4)]
        scst = [[PB[i][:, j * 512:(j + 1) * 512] for j in range(2)] for i in range(2)]
        acc = [PB[2][:, j * 512:(j + 1) * 512] for j in range(2)]
        mt_ps = [PB[3][:, j * 512:(j + 1) * 512].bitcast(BF16) for j in range(2)]

        Biota, Btq, Bcq, Blo, Bcand, Bcnt, Bind, Bncand, Bssum, BMall2 = (Buf() for _ in range(10))
        BI = [Buf() for _ in range(NCHK)]
        BM = [Buf() for _ in range(NB_ALL // 8)]
        BMall = Buf()
        BqbT = [Buf() for _ in range(2)]
        BqiT = [Buf() for _ in range(2)]
        Blohi = [Buf() for _ in range(2)]
        Bfh = [Buf() for _ in range(3)]
        Bvbs = [Buf() for _ in range(2)]
        Bpe = [Buf() for _ in range(2)]
        Bpm = [Buf() for _ in range(2)]
        Bmts = [Buf() for _ in range(2)]
        Brd = [Buf() for _ in range(2)]
        Bmxb = [Buf() for _ in range(2)]
        Bscst = [[BP[i][j] for j in range(2)] for i in range(2)]
        Bacc = [BP[2][j] for j in range(2)]
        Bmtps = [BP[3][j] for j in range(2)]

        g = S.new_group("pBinit")
        S.dma("sp", lambda e: e.dma_start(out=tqf[:], in_=C.tq), writes=[Btq], group=g)
        S.op("dve", lambda e: e.tensor_scalar(out=cq[:], in0=tqf[:], scalar1=0.5, scalar2=-BIG, op0=ALU.add, op1=ALU.mult),
             reads=[Btq], writes=[Bcq])
        S.op("pool", lambda e: e.iota(Iacc[:].bitcast(I32), pattern=[[1, T]], base=0, channel_multiplier=0), writes=BI)
        S.op("dve", lambda e: e.tensor_scalar(out=iotaN[:], in0=Iacc[:].bitcast(I32), scalar1=-BIG, scalar2=None, op0=ALU.mult),
             reads=BI, writes=[Biota])

        gcast = S.new_group("pBcast")
        for (src, dst, rows, cols) in ((C.w_o, C.wob_d, D, D), (C.w_up, C.wupb_d, D, 2 * D_FF), (C.w_down, C.wdnb_d, D_FF, D),
                                       (C.w_g, C.wgb_d, D, D), (C.w_p, C.wpb_d, 256, D)):
            cw_ = 1408 if cols == 2 * D_FF else 1024
            for r0 in range(0, rows, 128):
                for c0 in range(0, cols, cw_):
                    S.dma("pool", lambda e, src=src, dst=dst, r0=r0, c0=c0, cw_=cw_: e.dma_start(out=dst[r0:r0 + 128, c0:c0 + cw_],
                                                                                         in_=src[r0:r0 + 128, c0:c0 + cw_]),
                          writes=[Buf()], group=gcast)
        fcnt = 0

        def load_q(i):
            qs = i % 2
            S.dma("sp", lambda e: e.dma_start(out=qbT[qs][:], in_=C.qbT_d[i]), writes=[BqbT[qs]], owner=BqbT[qs])
            S.dma("sp", lambda e: e.dma_start(out=qiT[qs][:], in_=C.qiT_d[i]), writes=[BqiT[qs]], owner=BqiT[qs])
            S.dma("sp", lambda e: e.dma_start(out=lohi[qs][:], in_=C.lohi_d[i]), writes=[Blohi[qs]], owner=Blohi[qs])

        load_q(0)
        for i in range(NQ):
            qs = i % 2
            if i + 1 < NQ:
                load_q(i + 1)
            nkc = min(NCHK, -(-(3 * (OWN // 128) + i) // 4))
            nkb = nkc * 4
            nkeys = nkb * 128
            HB = (nkc + 1) // 2
            h1 = HB * 512
            n_act = nkeys - h1
            cthr = float(256 - n_act // 2)
            NG = -(-nkb // 8)
            for cp in range((nkc + 1) // 2):
                nck = min(2, nkc - 2 * cp)
                w = nck * 512
                c0 = cp * 1024
                BIc = BI[2 * cp:2 * cp + nck]
                for h in range(8):
                    rows = slice((h % 2) * 64, (h % 2) * 64 + 64)
                    pr = h // 2
                    ti = (h % 2) + 2 * (pr % 2)
                    for u in range(nck):
                        S.op("pe", lambda e, rows=rows, pr=pr, ti=ti, qs=qs, u=u, c0=c0: e.matmul(
                            PB[ti][:, u * 512:(u + 1) * 512], lhsT=qiT[qs][rows, pr * 128:(pr + 1) * 128],
                            rhs=C.kiT[rows, c0 + u * 512:c0 + (u + 1) * 512], start=True, stop=True),
                            reads=[BqiT[qs]] + C.BkiT[(c0 // 128) + u * 4:(c0 // 128) + (u + 1) * 4], writes=[BP[ti][u]])
                    k = fcnt % 3
                    fcnt += 1
                    S.op("act", lambda e, ti=ti, k=k, qs=qs, h=h, w=w: e.activation(out=fh[k][:, 0:w], in_=PB[ti][:, 0:w], func=AF.Relu,
                                                                                 scale=lohi[qs][:, h:h + 1]),
                         reads=BP[ti][0:nck] + [Blohi[qs]], writes=[Bfh[k]])
                    if h == 0:
                        S.op("dve", lambda e, k=k, c0=c0, w=w, qs=qs, h=h: e.tensor_scalar(out=Iacc[:, c0:c0 + w], in0=fh[k][:, 0:w],
                                                                                       scalar1=lohi[qs][:, h:h + 1], scalar2=None, op0=ALU.mult),
                             reads=[Bfh[k], Blohi[qs]], writes=BIc)
                    else:
                        S.op("dve", lambda e, k=k, c0=c0, w=w, qs=qs, h=h: e.scalar_tensor_tensor(
                            out=Iacc[:, c0:c0 + w], in0=fh[k][:, 0:w], scalar=lohi[qs][:, h:h + 1], in1=Iacc[:, c0:c0 + w],
                            op0=ALU.mult, op1=ALU.add), reads=[Bfh[k], Blohi[qs]] + BIc, writes=BIc)
                S.op("dve", lambda e, c0=c0, w=w, i=i: e.scalar_tensor_tensor(out=Iacc[:, c0:c0 + w], in0=iotaN[:, c0:c0 + w],
                                                                              scalar=cq[:, i:i + 1], in1=Iacc[:, c0:c0 + w],
                                                                              op0=ALU.subtract, op1=ALU.min),
                     reads=[Biota, Bcq] + BIc, writes=BIc)
            S.op("dve", lambda e: e.memset(cand[:], 0.0), writes=[Bcand])
            for b in range(NBIS):
                step = float(BIS_B * 2.0 / (2 ** (b + 1)))
                nstep = step / 2.0
                S.op("dve", lambda e, h1=h1: e.tensor_scalar(out=Mb[:, 0:h1], in0=Iacc[:, 0:h1], scalar1=cand[:, 0:1], scalar2=None,
                                                      op0=ALU.is_ge, op1=ALU.add, accum_out=cntt[:, 0:1]),
                     reads=BI[:HB] + [Bcand], writes=[BMall, Bcnt])
                S.op("act", lambda e, h1=h1, nkeys=nkeys: e.activation(out=Mb[:, h1:nkeys], in_=Iacc[:, h1:nkeys], func=AF.Sign, scale=-1.0, bias=cand[:, 0:1],
                                                   accum_out=ssum[:, 0:1]),
                     reads=BI[HB:nkc] + [Bcand], writes=[BMall2, Bssum])
                S.op("dve", lambda e: e.scalar_tensor_tensor(out=ind[:], in0=ssum[:], scalar=-0.5, in1=cntt[:], op0=ALU.mult, op1=ALU.add),
                     reads=[Bssum, Bcnt], writes=[Bind])
                if b < NBIS - 1:
                    S.op("dve", lambda e, nstep=nstep, cthr=cthr: e.tensor_scalar(out=ind[:], in0=ind[:], scalar1=cthr, scalar2=2.0 * nstep,
                                                                       op0=ALU.is_ge, op1=ALU.mult), reads=[Bind], writes=[Bind])
                    S.op("dve", lambda e, nstep=nstep: e.scalar_tensor_tensor(out=cand[:], in0=ind[:], scalar=-nstep, in1=cand[:],
                                                                              op0=ALU.add, op1=ALU.add), reads=[Bind, Bcand], writes=[Bcand])
                else:
                    S.op("dve", lambda e, step=step, cthr=cthr: e.tensor_scalar(out=ind[:], in0=ind[:], scalar1=cthr, scalar2=step,
                                                                     op0=ALU.is_ge, op1=ALU.mult), reads=[Bind], writes=[Bind])
                    S.op("dve", lambda e, step=step: e.scalar_tensor_tensor(out=lo_[:], in0=ind[:], scalar=-step, in1=cand[:],
                                                                            op0=ALU.add, op1=ALU.add), reads=[Bind, Bcand], writes=[Blo])
            for gq in range(NG):
                ce = min((gq + 1) * 1024, nkeys)
                S.op("dve", lambda e, gq=gq, ce=ce: e.tensor_scalar(out=Mb[:, gq * 1024:ce], in0=Iacc[:, gq * 1024:ce],
                                                                    scalar1=lo_[:, 0:1], scalar2=None, op0=ALU.is_ge),
                     reads=BI[2 * gq:min(2 * gq + 2, nkc)] + [Blo, BMall, BMall2], writes=[BM[gq]])

            def load_group(gq):
                vk = gq % 2
                S.dma("sp", lambda e, gq=gq, vk=vk: e.dma_start(out=vbs[vk][:], in_=C.vb_d[gq * 8:(gq + 1) * 8].rearrange("b p c -> p b c")),
                      writes=[Bvbs[vk]], owner=Bvbs[vk])
                for j in range(min(8, nkb - gq * 8)):
                    n = gq * 8 + j
                    S.op("pe", lambda e, j=j, n=n, vk=vk: e.transpose(out=mt_ps[vk][:, j * 128:(j + 1) * 128], in_=Mb[:, n * 128:(n + 1) * 128],
                                                                      identity=C.ident[:]),
                         reads=[BM[gq], C.Bident], writes=[Bmtps[vk]])
                nj = min(8, nkb - gq * 8)
                S.op("act", lambda e, vk=vk, nj=nj: e.activation(out=mts[vk][:, 0:nj * 128], in_=mt_ps[vk][:, 0:nj * 128], func=AF.Copy),
                     reads=[Bmtps[vk]], writes=[Bmts[vk]])

            def qk_B(n, sidx, qs=qs):
                for h in range(8):
                    rows = slice((h % 2) * 64, (h % 2) * 64 + 64)
                    pr = h // 2
                    S.op("pe", lambda e, h=h, rows=rows, pr=pr: e.matmul(
                        scst[sidx][h % 2][:, pr * 128:(pr + 1) * 128], lhsT=C.kbT[rows, n, pr * 128:(pr + 1) * 128],
                        rhs=qbT[qs][rows, pr * 128:(pr + 1) * 128], start=True, stop=True),
                        reads=[C.BkbT[n], BqbT[qs]], writes=[Bscst[sidx][h % 2]])

            load_group(0)
            qk_B(0, 0)
            qk_B(1, 1)
            for n in range(nkb):
                gq, j = n // 8, n % 8
                vk = gq % 2
                sidx = n % 2
                if j == 0 and gq + 1 < NG:
                    load_group(gq + 1)
                S.op("act", lambda e, sidx=sidx: e.activation(out=pe_[sidx][:, :], in_=PB[sidx][:, :], func=AF.Exp, scale=0.125),
                     reads=[Bscst[sidx][0], Bscst[sidx][1]], writes=[Bpe[sidx]])
                if n + 2 < nkb:
                    qk_B(n + 2, sidx)
                S.op("dve", lambda e, sidx=sidx, vk=vk, j=j: e.tensor_tensor(
                    out=pm[sidx][:].rearrange("p (h q) -> p h q", q=128), in0=pe_[sidx][:].rearrange("p (h q) -> p h q", q=128),
                    in1=mts[vk][:, j * 128:(j + 1) * 128].unsqueeze(1).to_broadcast([128, 8, 128]), op=ALU.mult),
                    reads=[Bpe[sidx], Bmts[vk]], writes=[Bpm[sidx]])
                for h in range(8):
                    gi = h // 4
                    cc = (h % 4) * 65
                    pcol = (h % 2) * 512 + (h // 2) * 128
                    S.op("pe", lambda e, h=h, gi=gi, cc=cc, pcol=pcol, sidx=sidx, vk=vk, j=j, n=n: e.matmul(
                        acc[gi][:, cc:cc + 65], lhsT=pm[sidx][:, pcol:pcol + 128], rhs=vbs[vk][:, j, h * 65:(h + 1) * 65],
                        start=(n == 0 and h % 4 == 0), stop=(n == nkb - 1), skip_group_check=True),
                        reads=[Bpm[sidx], Bvbs[vk]], writes=[Bacc[gi]])
            a = i % 2
            for gi in range(2):
                accv = acc[gi][:, 0:260].rearrange("p (h c) -> p h c", c=65)
                S.op("dve", lambda e, a=a, gi=gi, accv=accv: e.tensor_scalar(out=rd[a][:, gi * 4:(gi + 1) * 4].unsqueeze(2), in0=accv[:, :, 64:65],
                                                                             scalar1=1e-30, scalar2=None, op0=ALU.max),
                     reads=[Bacc[gi]], writes=[Brd[a]])
                S.op("dve", lambda e, a=a, gi=gi: e.reciprocal(out=rd[a][:, gi * 4:(gi + 1) * 4], in_=rd[a][:, gi * 4:(gi + 1) * 4]),
                     reads=[Brd[a]], writes=[Brd[a]])
                S.op("dve", lambda e, a=a, gi=gi, accv=accv: e.tensor_tensor(
                    out=mxb[a][:, gi * 256:(gi + 1) * 256].rearrange("p (h d) -> p h d", d=64), in0=accv[:, :, 0:64],
                    in1=rd[a][:, gi * 4:(gi + 1) * 4].unsqueeze(2).to_broadcast([128, 4, 64]), op=ALU.mult),
                    reads=[Bacc[gi], Brd[a]], writes=[Bmxb[a]])
            S.dma("sp", lambda e, a=a, i=i: e.dma_start(out=C.mix_d[i][:, 512:1024], in_=mxb[a][:]), reads=[Bmxb[a]], writes=[Buf()], owner=Bmxb[a])
        S.emit_phase()
        S.release(BqbT + BqiT + Blohi + Bvbs + Bmxb)
    if DBG and STOP_AFTER == "B":
        dump_mix(C)


def rms_to_bf16(S, xin_ap, Bx, sqj, Bsq, ss, Bss, rstd, Brs, gam, Bgam, out_ap, Bout):
    S.op("act", lambda e: e.activation(out=sqj[:], in_=xin_ap, func=AF.Square, accum_out=ss[:, 0:1]), reads=[Bx], writes=[Bsq, Bss])
    S.op("act", lambda e: e.activation(out=rstd[:], in_=ss[:], func=AF.Sqrt, scale=1.0 / D, bias=RMS_EPS), reads=[Bss], writes=[Brs])
    S.op("dve", lambda e: e.reciprocal(out=rstd[:], in_=rstd[:]), reads=[Brs], writes=[Brs])
    S.op("dve", lambda e: e.scalar_tensor_tensor(out=out_ap, in0=xin_ap, scalar=rstd[:, 0:1], in1=gam[:], op0=ALU.mult, op1=ALU.mult),
         reads=[Bx, Brs, Bgam], writes=[Bout])


def phaseC1(C):
    nc, S = C.nc, C.S
    TT = 256
    NT = OWN // TT
    with ExitStack() as es:
        sb = lambda n, s, d: es.enter_context(nc.sbuf_tensor(n, s, d))
        pst = lambda n, s, d: es.enter_context(nc.psum_tensor(n, s, d))
        wo = sb("wo", [128, 8, D], BF16)
        wup = sb("wup", [128, 8, 2 * D_FF], BF16)
        wdn = sb("wdn", [128, 22, D], BF16)
        gff = sb("gff", [128, D], F32)
        cw = sb("cw", [128, 4, NCH], F32)
        carry = sb("carry", [128, NCH, 2], F32)
        mixt = [sb("mixt0", [128, D], BF16)] * 2
        mixT = [sb("mixT0", [128, 8, 128], BF16)] * 2
        xio = [sb(f"xio{i}", [128, 2, D], F32) for i in range(2)]
        cwl = xio[0][0:NCH, 0, 0:512].rearrange("c (j p) -> c j p", j=4)
        ss = [sb(f"ssC{i}", [128, 1], F32) for i in range(2)]
        rstd = [sb(f"rstdC{i}", [128, 1], F32) for i in range(2)]
        h2 = [sb(f"h2{i}", [128, D], BF16) for i in range(2)]
        h2T = [sb(f"h2T{i}", [128, 8, TT], BF16) for i in range(2)]
        U = [sb(f"U{i}", [128, TT + 2], F32) for i in range(4)]
        y = [sb(f"y{i}", [128, TT], F32) for i in range(5)]
        sg = [sb("sg0", [128, TT], F32)] * 2
        aT = sb("aT", [128, 22, TT], BF16)
        tp_ps = [pst(f"tpps{i}", [128, 1024], BF16) for i in range(2)]
        wo_ps = [pst(f"wops{i}", [128, 512], F32) for i in range(2)]
        u_ps = [pst(f"ups{i}", [128, 512], F32) for i in range(4)]
        wd_ps = wo_ps

        Bwo, Bwup, Bwdn, Bgff, Bcw, Bsq = (Buf() for _ in range(6))
        Bcarry = [Buf() for _ in range(NCH)]
        Bmixt = [Buf()] * 2
        BmixT = [Buf()] * 2
        Bxio = [[Buf() for _ in range(2)] for _ in range(2)]
        Bcwl = Bxio[0][0]
        Bss = [Buf() for _ in range(2)]
        Brs = [Buf() for _ in range(2)]
        Bh2 = [Buf() for _ in range(2)]
        Bh2T = [[Buf() for _ in range(2)] for _ in range(2)]
        BU = [Buf() for _ in range(4)]
        By = [Buf() for _ in range(5)]
        Bsg = [Buf()] * 2
        BaT = [Buf() for _ in range(22)]
        Btp = [Buf() for _ in range(2)]
        Bwops = [Buf() for _ in range(2)]
        Bups = [Buf() for _ in range(4)]
        Bwdps = Bwops

        g = S.new_group("pC1init")
        gw = S.new_group("pC1w")
        S.dma("sp", lambda e: e.dma_start(out=gff[:], in_=C.ffn_norm.partition_broadcast(128)), writes=[Bgff], group=g)
        for j in range(3):
            S.dma("sp", lambda e, j=j: e.dma_start(out=cwl[:, j, :], in_=C.conv_w[j].rearrange("(c p) -> c p", p=128)), writes=[Bcwl], group=g)
        S.dma("sp", lambda e: e.dma_start(out=cwl[:, 3, :], in_=C.conv_b[0].rearrange("(c p) -> c p", p=128)), writes=[Bcwl], group=g)
        gwu = S.new_group("pC1wu")
        gwd = S.new_group("pC1wd")
        S.dma("sp", lambda e: e.dma_start(out=wo[:], in_=C.wob_d.rearrange("(m p) c -> p m c", p=128)), writes=[Bwo], group=gw)
        for m in range(8):
            S.dma("sp", lambda e, m=m: e.dma_start(out=wup[:, m, :], in_=C.wupb_d[m * 128:(m + 1) * 128, :]), writes=[Bwup], group=gwu)
        for k0 in range(0, 22, 11):
            S.dma("sp", lambda e, k0=k0: e.dma_start(out=wdn[:, k0:k0 + 11, :],
                                                     in_=C.wdnb_d[k0 * 128:(k0 + 11) * 128, :].rearrange("(m p) c -> p m c", p=128)),
                  writes=[Bwdn], group=gwd)
        for j in range(4):
            S.op("pe", lambda e, j=j: e.transpose(out=wo_ps[0][:, j * NCH:(j + 1) * NCH], in_=cwl[:, j, :], identity=C.identf[0:NCH, 0:NCH]),
                 reads=[Bcwl, C.Bident], writes=[Bwops[0]])
        S.op("dve", lambda e: e.tensor_copy(out=cw[:].rearrange("p j c -> p (j c)"), in_=wo_ps[0][:, 0:4 * NCH]), reads=[Bwops[0]], writes=[Bcw])

        cnt = {"tp": 0, "u": 0, "U": 0, "y": 0}

        def load_xrow(xa_blk, xdst, Bxdst):
            S.dma("sp", lambda e: e.dma_start(out=xdst, in_=C.xa[xa_blk * 128:(xa_blk + 1) * 128, :]), writes=[Bxdst], owner=Bxdst)

        def attn_out_block(mix_idx, xa_blk, xdst, Bxdst, slot):
            S.dma("sp", lambda e: e.dma_start(out=mixt[slot][:], in_=C.mix_d[mix_idx]), writes=[Bmixt[slot]], owner=Bmixt[slot])
            q = cnt["tp"] % 2
            cnt["tp"] += 1
            for m in range(8):
                S.op("pe", lambda e, m=m, q=q: e.transpose(out=tp_ps[q][:, m * 128:(m + 1) * 128], in_=mixt[slot][:, m * 128:(m + 1) * 128],
                                                           identity=C.ident[:]), reads=[Bmixt[slot], C.Bident], writes=[Btp[q]])
            S.op("act", lambda e, q=q: e.activation(out=mixT[slot][:].rearrange("p m t -> p (m t)"), in_=tp_ps[q][:], func=AF.Copy),
                 reads=[Btp[q]], writes=[BmixT[slot]])
            for hf in range(2):
                for m in range(8):
                    S.op("pe", lambda e, m=m, hf=hf: e.matmul(wo_ps[hf][:, :], lhsT=mixT[slot][:, m, :], rhs=wo[:, m, hf * 512:(hf + 1) * 512],
                                                              start=(m == 0), stop=(m == 7)), reads=[BmixT[slot], Bwo], writes=[Bwops[hf]])
                S.op("dve", lambda e, hf=hf: e.tensor_tensor(out=xdst[:, hf * 512:(hf + 1) * 512], in0=xdst[:, hf * 512:(hf + 1) * 512],
                                                             in1=wo_ps[hf][:, :], op=ALU.add), reads=[Bxdst, Bwops[hf]], writes=[Bxdst])

        def norm_T(xsrc, Bxsrc, slot, dstT_fn, BdstT):
            rms_to_bf16(S, xsrc, Bxsrc, h2[slot], Bh2[slot], ss[slot], Bss[slot], rstd[slot], Brs[slot], gff, Bgff, h2[slot][:], Bh2[slot])
            q = cnt["tp"] % 2
            cnt["tp"] += 1
            for m in range(8):
                S.op("pe", lambda e, m=m, q=q: e.transpose(out=tp_ps[q][:, m * 128:(m + 1) * 128], in_=h2[slot][:, m * 128:(m + 1) * 128],
                                                           identity=C.ident[:]), reads=[Bh2[slot], C.Bident], writes=[Btp[q]])
            S.op("act", lambda e, q=q: e.activation(out=dstT_fn(), in_=tp_ps[q][:].rearrange("p (m t) -> p m t", t=128), func=AF.Copy),
                 reads=[Btp[q]], writes=[BdstT])

        load_xrow(NB_A - NQ, xio[1][:, 1, :], Bxio[1][1])
        for blk in range(2):
            load_xrow(NB_A - NQ + 1 + blk, xio[0][:, blk, :], Bxio[0][blk])
        attn_out_block(0, NB_A - NQ, xio[1][:, 1, :], Bxio[1][1], 1)
        norm_T(xio[1][:, 1, :], Bxio[1][1], 1, lambda: h2T[1][:, :, 128:256], Bh2T[1][1])
        for cc in range(NCH):
            r = cnt["u"] % 4
            cnt["u"] += 1
            ub, uh = u_ps[r], 0
            for m in range(8):
                S.op("pe", lambda e, m=m, cc=cc, ub=ub, uh=uh: e.matmul(ub[:, uh * 256:uh * 256 + 2], lhsT=wup[:, m, cc * 128:(cc + 1) * 128],
                                                                        rhs=h2T[1][:, m, 254:256], start=(m == 0), stop=(m == 7)),
                     reads=[Bh2T[1][1], Bwup], writes=[Bups[r]])
            S.op("dve", lambda e, cc=cc, ub=ub, uh=uh: e.tensor_copy(out=carry[:, cc, :], in_=ub[:, uh * 256:uh * 256 + 2]),
                 reads=[Bups[r]], writes=[Bcarry[cc]])

        def prologue(t):
            xs = t % 2
            for blk in range(2):
                tb = 2 * t + blk
                attn_out_block(tb + 1, NB_A - NQ + 1 + tb, xio[xs][:, blk, :], Bxio[xs][blk], blk)
                norm_T(xio[xs][:, blk, :], Bxio[xs][blk], blk, lambda xs=xs, blk=blk: h2T[xs][:, :, blk * 128:(blk + 1) * 128], Bh2T[xs][blk])

        prologue(0)
        for t in range(NT):
            xs = t % 2
            if t + 1 < NT:
                for blk in range(2):
                    load_xrow(NB_A - NQ + 1 + 2 * (t + 1) + blk, xio[1 - xs][:, blk, :], Bxio[1 - xs][blk])
            for k in range(22):
                ys = []
                pair = []
                for cc in (k, k + 22):
                    r = cnt["u"] % 4
                    cnt["u"] += 1
                    ub = u_ps[r]
                    for m in range(8):
                        S.op("pe", lambda e, m=m, cc=cc, ub=ub, xs=xs: e.matmul(
                            ub[:, 0:TT], lhsT=wup[:, m, cc * 128:(cc + 1) * 128], rhs=h2T[xs][:, m, :],
                            start=(m == 0), stop=(m == 7)), reads=[Bh2T[xs][0], Bh2T[xs][1], Bwup], writes=[Bups[r]])
                    ui = cnt["U"] % 4
                    cnt["U"] += 1
                    yi = cnt["y"] % 5
                    cnt["y"] += 1
                    pair.append((cc, r, ub, ui, yi))
                    ys.append(yi)
                for cc, r, ub, ui, yi in pair:
                    S.op("pool", lambda e, ui=ui, cc=cc: e.tensor_copy(out=U[ui][:, 0:2], in_=carry[:, cc, :]), reads=[Bcarry[cc]], writes=[BU[ui]])
                for cc, r, ub, ui, yi in pair:
                    S.op("act", lambda e, ui=ui, ub=ub: e.activation(out=U[ui][:, 2:TT + 2], in_=ub[:, 0:TT], func=AF.Copy),
                         reads=[Bups[r]], writes=[BU[ui]])
                for cc, r, ub, ui, yi in pair:
                    S.op("pool", lambda e, ui=ui, cc=cc: e.tensor_copy(out=carry[:, cc, :], in_=U[ui][:, TT:TT + 2]), reads=[BU[ui]],
                         writes=[Bcarry[cc]])
                for pi, (cc, r, ub, ui, yi) in enumerate(pair):
                    if True:
                        S.op("act", lambda e, ui=ui, yi=yi, cc=cc: e.activation(out=y[yi][:], in_=U[ui][:, 2:TT + 2], func=AF.Identity,
                                                                                scale=cw[:, 2, cc:cc + 1], bias=cw[:, 3, cc:cc + 1]),
                             reads=[BU[ui], Bcw], writes=[By[yi]])
                    else:
                        S.op("pool", lambda e, ui=ui, yi=yi, cc=cc: e.tensor_scalar(out=y[yi][:], in0=U[ui][:, 2:TT + 2], scalar1=cw[:, 2, cc:cc + 1],
                                                                                    scalar2=cw[:, 3, cc:cc + 1], op0=ALU.mult, op1=ALU.add),
                             reads=[BU[ui], Bcw], writes=[By[yi]])
                for cc, r, ub, ui, yi in pair:
                    S.op("dve", lambda e, ui=ui, yi=yi, cc=cc: e.scalar_tensor_tensor(out=y[yi][:], in0=U[ui][:, 1:TT + 1], scalar=cw[:, 1, cc:cc + 1],
                                                                                      in1=y[yi][:], op0=ALU.mult, op1=ALU.add),
                         reads=[BU[ui], Bcw, By[yi]], writes=[By[yi]])
                for cc, r, ub, ui, yi in pair:
                    S.op("dve", lambda e, ui=ui, yi=yi, cc=cc: e.scalar_tensor_tensor(out=y[yi][:], in0=U[ui][:, 0:TT], scalar=cw[:, 0, cc:cc + 1],
                                                                                      in1=y[yi][:], op0=ALU.mult, op1=ALU.add),
                         reads=[BU[ui], Bcw, By[yi]], writes=[By[yi]])
                si = k % 2
                S.op("act", lambda e, si=si, yg=ys[0]: e.activation(out=sg[si][:], in_=y[yg][:], func=AF.Silu), reads=[By[ys[0]]], writes=[Bsg[si]])
                S.op("pool", lambda e, si=si, yu=ys[1], k=k: e.tensor_tensor(out=aT[:, k, :], in0=sg[si][:], in1=y[yu][:], op=ALU.mult),
                     reads=[Bsg[si], By[ys[1]]], writes=[BaT[k]])
            if t + 1 < NT:
                prologue(t + 1)
            for blk in range(2):
                tb = 2 * t + blk
                for hf in range(2):
                    for k in range(22):
                        S.op("pe", lambda e, k=k, hf=hf, blk=blk: e.matmul(wd_ps[hf][:, :], lhsT=aT[:, k, blk * 128:(blk + 1) * 128],
                                                                           rhs=wdn[:, k, hf * 512:(hf + 1) * 512], start=(k == 0), stop=(k == 21)),
                             reads=[BaT[k], Bwdn], writes=[Bwdps[hf]])
                    S.op("dve", lambda e, hf=hf, blk=blk, xs=xs: e.tensor_tensor(out=xio[xs][:, blk, hf * 512:(hf + 1) * 512],
                                                                                 in0=xio[xs][:, blk, hf * 512:(hf + 1) * 512],
                                                                                 in1=wd_ps[hf][:, :], op=ALU.add),
                         reads=[Bxio[xs][blk], Bwdps[hf]], writes=[Bxio[xs][blk]])
                S.dma("sp", lambda e, tb=tb, xs=xs, blk=blk: e.dma_start(out=C.x2_d[tb * 128:(tb + 1) * 128, :], in_=xio[xs][:, blk, :]),
                      reads=[Bxio[xs][blk]], writes=[Buf()], owner=Bxio[xs][blk])
        S.emit_phase()
        S.release([Bmixt[0]] + Bxio[0] + Bxio[1])
    if DBG:
        with nc.sbuf_tensor("dx0", [128, D], F32) as dx0, nc.sbuf_tensor("dx1", [128, D], F32) as dx1:
            dx = [dx0, dx1]
            Bd = [Buf(), Buf()]
            for i in range(OWN // 128):
                k = i % 2
                S.dma("sp", lambda e, i=i, k=k: e.dma_start(out=dx[k][:], in_=C.x2_d[i * 128:(i + 1) * 128, :]), writes=[Bd[k]], owner=Bd[k])
                S.dma("sp", lambda e, i=i, k=k: e.dma_start(out=C.dbg["x2"][i * 128:(i + 1) * 128, :], in_=dx[k][:]), reads=[Bd[k]], writes=[Buf()],
                      owner=Bd[k])
            S.emit_phase()
            S.release(Bd)


def phaseC2(C):
    nc, S = C.nc, C.S
    NBK = OWN // 128
    with ExitStack() as es:
        sb = lambda n, s, d: es.enter_context(nc.sbuf_tensor(n, s, d))
        pst = lambda n, s, d: es.enter_context(nc.psum_tensor(n, s, d))
        wg = sb("wg", [128, 8, D], BF16)
        wp = sb("wp", [128, 2, D], BF16)
        gpl = sb("gpl", [128, D], F32)
        gfin = sb("gfin", [128, D], F32)
        x2t = [sb(f"x2t{i}", [128, D], F32) for i in range(2)]
        pt = [sb(f"pt{i}", [128, 256], F32) for i in range(2)]
        pb = [sb(f"pb{i}", [128, 256], BF16) for i in range(2)]
        pT = [sb(f"pT{i}", [128, 2, 128], BF16) for i in range(2)]
        sqj = sb("sqjD", [128, D], BF16)
        sqj2 = sb("sqjD2", [128, D], BF16)
        Bsq2 = Buf()
        ss = [sb(f"ssD{i}", [128, 1], F32) for i in range(2)]
        rstd = [sb(f"rstdD{i}", [128, 1], F32) for i in range(2)]
        ss2 = [sb(f"ssE{i}", [128, 1], F32) for i in range(2)]
        rstd2 = [sb(f"rstdE{i}", [128, 1], F32) for i in range(2)]
        h3 = [sb(f"h3{i}", [128, D], BF16) for i in range(2)]
        h3T = [sb(f"h3T{i}", [128, 8, 128], BF16) for i in range(2)]
        gate = [sb(f"gate{i}", [128, D], F32) for i in range(2)]
        x3 = [sb(f"x3{i}", [128, D], F32) for i in range(2)]
        ot = [sb(f"ot{i}", [128, D], F32) for i in range(2)]
        tp_ps = pst("tpD", [128, 1024], BF16)
        pT_ps = pst("pTD", [128, 1024], BF16)
        g_ps = [pst(f"gps{i}", [128, 512], F32) for i in range(2)]
        pp_ps = [pst(f"ppps{i}", [128, 512], F32) for i in range(2)]
        Bwg, Bwp, Bgpl, Bgfin, Bsq, Btp, BpTps = (Buf() for _ in range(7))
        mk = lambda: [Buf() for _ in range(2)]
        Bx2t, Bpt, Bpb, BpT, Bss, Brs, Bss2, Brs2, Bh3, Bh3T, Bgate, Bx3, Bot, Bgps, Bppps = (mk() for _ in range(15))
        g = S.new_group("pC2init")
        gw = S.new_group("pC2w")
        S.dma("sp", lambda e: e.dma_start(out=gpl[:], in_=C.ple_norm.partition_broadcast(128)), writes=[Bgpl], group=g)
        S.dma("sp", lambda e: e.dma_start(out=gfin[:], in_=C.final_norm.partition_broadcast(128)), writes=[Bgfin], group=g)
        S.dma("sp", lambda e: e.dma_start(out=wg[:], in_=C.wgb_d.rearrange("(m p) c -> p m c", p=128)), writes=[Bwg], group=gw)
        S.dma("sp", lambda e: e.dma_start(out=wp[:], in_=C.wpb_d.rearrange("(m p) c -> p m c", p=128)), writes=[Bwp], group=gw)
        def load_c2(tb):
            s = tb % 2
            S.dma("sp", lambda e: e.dma_start(out=x2t[s][:], in_=C.x2_d[tb * 128:(tb + 1) * 128, :]), writes=[Bx2t[s]], owner=Bx2t[s])
            S.dma("sp", lambda e: e.dma_start(out=pt[s][:], in_=C.p_own[tb * 128:(tb + 1) * 128, :]), writes=[Bpt[s]], owner=Bpt[s])

        def x_norm(tb):
            s = tb % 2
            rms_to_bf16(S, x2t[s][:], Bx2t[s], sqj, Bsq, ss[s], Bss[s], rstd[s], Brs[s], gpl, Bgpl, h3[s][:], Bh3[s])
            S.op("pool", lambda e, s=s: e.tensor_copy(out=pb[s][:], in_=pt[s][:]), reads=[Bpt[s]], writes=[Bpb[s]])

        def x_transposes(tb):
            s = tb % 2
            for m in range(8):
                S.op("pe", lambda e, m=m, s=s: e.transpose(out=tp_ps[:, m * 128:(m + 1) * 128], in_=h3[s][:, m * 128:(m + 1) * 128],
                                                           identity=C.ident[:]), reads=[Bh3[s], C.Bident], writes=[Btp])
            for m in range(2):
                S.op("pe", lambda e, m=m, s=s: e.transpose(out=pT_ps[:, m * 128:(m + 1) * 128], in_=pb[s][:, m * 128:(m + 1) * 128],
                                                           identity=C.ident[:]), reads=[Bpb[s], C.Bident], writes=[BpTps])

        def x_copies(tb):
            s = tb % 2
            S.op("act", lambda e, s=s: e.activation(out=h3T[s][:].rearrange("p m t -> p (m t)"), in_=tp_ps[:], func=AF.Copy),
                 reads=[Btp], writes=[Bh3T[s]])
            S.op("act", lambda e, s=s: e.activation(out=pT[s][:].rearrange("p m t -> p (m t)"), in_=pT_ps[:, 0:256], func=AF.Copy),
                 reads=[BpTps], writes=[BpT[s]])

        load_c2(0)
        load_c2(1)
        x_norm(0)
        x_transposes(0)
        x_copies(0)
        for tb in range(NBK):
            s = tb % 2
            nxt = tb + 1 < NBK
            for hf in range(2):
                for m in range(8):
                    S.op("pe", lambda e, m=m, hf=hf, s=s: e.matmul(g_ps[hf][:, :], lhsT=h3T[s][:, m, :], rhs=wg[:, m, hf * 512:(hf + 1) * 512],
                                                                   start=(m == 0), stop=(m == 7)), reads=[Bh3T[s], Bwg], writes=[Bgps[hf]])
                for m in range(2):
                    S.op("pe", lambda e, m=m, hf=hf, s=s: e.matmul(pp_ps[hf][:, :], lhsT=pT[s][:, m, :], rhs=wp[:, m, hf * 512:(hf + 1) * 512],
                                                                   start=(m == 0), stop=(m == 1)), reads=[BpT[s], Bwp], writes=[Bppps[hf]])
            if nxt:
                x_norm(tb + 1)
                x_transposes(tb + 1)
            for hf in range(2):
                S.op("act", lambda e, hf=hf, s=s: e.activation(out=gate[s][:, hf * 512:(hf + 1) * 512], in_=g_ps[hf][:, :], func=AF.Sigmoid),
                     reads=[Bgps[hf]], writes=[Bgate[s]])
                S.op("dve", lambda e, hf=hf, s=s: e.tensor_tensor(out=gate[s][:, hf * 512:(hf + 1) * 512], in0=gate[s][:, hf * 512:(hf + 1) * 512],
                                                                  in1=pp_ps[hf][:, :], op=ALU.mult), reads=[Bgate[s], Bppps[hf]], writes=[Bgate[s]])
            if nxt:
                x_copies(tb + 1)
            S.op("pool", lambda e, s=s: e.tensor_tensor(out=x3[s][:], in0=gate[s][:], in1=x2t[s][:], op=ALU.add),
                 reads=[Bgate[s], Bx2t[s]], writes=[Bx3[s]])
            if tb + 2 < NBK:
                load_c2(tb + 2)
            S.op("act", lambda e, s=s: e.activation(out=sqj2[:], in_=x3[s][:], func=AF.Square, accum_out=ss2[s][:, 0:1]),
                 reads=[Bx3[s]], writes=[Bsq2, Bss2[s]])
            S.op("act", lambda e, s=s: e.activation(out=rstd2[s][:], in_=ss2[s][:], func=AF.Sqrt, scale=1.0 / D, bias=RMS_EPS),
                 reads=[Bss2[s]], writes=[Brs2[s]])
            S.op("dve", lambda e, s=s: e.reciprocal(out=rstd2[s][:], in_=rstd2[s][:]), reads=[Brs2[s]], writes=[Brs2[s]])
            S.op("dve", lambda e, s=s: e.scalar_tensor_tensor(out=ot[s][:], in0=x3[s][:], scalar=rstd2[s][:, 0:1], in1=gfin[:],
                                                              op0=ALU.mult, op1=ALU.mult), reads=[Bx3[s], Brs2[s], Bgfin], writes=[Bot[s]])
            S.dma("sp", lambda e, tb=tb, s=s: e.dma_start(out=C.out[tb * 128:(tb + 1) * 128, :], in_=ot[s][:]), reads=[Bot[s]], writes=[Buf()],
                  owner=Bot[s])
        S.emit_phase()


def _amask_const():
    s = np.arange(128)[:, None]
    qq = np.arange(128)[None, :]
    out = np.zeros((128, NQ, 128), np.float32)
    for r in range(NQ):
        delta = (16 - r) * 128 + qq - s
        m = ((delta >= 0) & (delta <= 128)).astype(np.float32)
        m += ((delta >= 0) & (delta <= 512) & (delta % 4 == 0)).astype(np.float32)
        m += ((delta >= 0) & (delta <= 2048) & (delta % 16 == 0)).astype(np.float32)
        out[:, r, :] = m
    return out.reshape(128, NQ * 128)


def make_in_maps(x, p, positions, attn_norm, w_in, w_o, ffn_norm, w_up, conv_w, conv_b, w_down, ple_norm,
                 w_ple_gate, w_ple_proj, final_norm):
    f32 = lambda a: np.ascontiguousarray(np.asarray(a, dtype=np.float32))
    x = f32(x)
    p = f32(p)
    positions = np.asarray(positions).astype(np.int32)
    invf = (1.0 / (10000.0 ** (np.arange(0, 64, 2, dtype=np.float32) / np.float32(64)))).astype(np.float32)[None]
    amask = _amask_const()
    shared = dict(invf=invf, amask=amask, w_in=f32(w_in[0]), w_o=f32(w_o[0]), w_up=f32(w_up[0]), w_down=f32(w_down[0]),
                  w_g=f32(w_ple_gate[0]), w_p=f32(w_ple_proj[0]), attn_norm=f32(attn_norm[0:1]), ffn_norm=f32(ffn_norm[0:1]),
                  ple_norm=f32(ple_norm[0:1]), final_norm=f32(final_norm[None]), conv_w=f32(conv_w[0]), conv_b=f32(conv_b[0:1]))
    maps = []
    for c in range(NCORE):
        b, q = c // 4, c % 4
        t_lo = OWN * q - (NB_A - 16) * 128 + 0
        t_lo = OWN * q - 2176
        idx = np.arange(t_lo, t_lo + NB_A * 128)
        valid = idx >= 0
        xa = np.zeros((NB_A * 128, D), np.float32)
        xa[valid] = x[b, idx[valid]]
        posa = np.zeros(NB_A * 128, np.int32)
        posa[valid] = positions[b, idx[valid]]
        pos = np.concatenate([positions[b].reshape(NB_ALL, 128).T, posa.reshape(NB_A, 128).T], axis=1)
        avalid = valid.astype(np.float32).reshape(NB_A, 128).T
        tqi = idx[(NB_A - NQ) * 128:].astype(np.float32)
        tqi = np.where(tqi >= 0, tqi, -1.0).reshape(NQ, 128).T
        m = dict(shared)
        m.update(xall=x[b], xa=xa, pos=np.ascontiguousarray(pos), avalid=np.ascontiguousarray(avalid),
                 tq=np.ascontiguousarray(tqi.astype(np.float32)), p_own=np.ascontiguousarray(p[0, b, OWN * q:OWN * (q + 1)]))
        maps.append(m)
    return maps


_NC_CACHE = {}


def kernel(**inputs):
    if "nc" not in _NC_CACHE:
        _NC_CACHE["nc"] = build_program()
    nc = _NC_CACHE["nc"]
    in_maps = make_in_maps(**inputs)
    res = run_bass_kernel_spmd(nc, in_maps, core_ids=list(range(NCORE)))
    outs = [r["out"] for r in res.results]
    full = np.stack(outs, 0).reshape(2, T, D).astype(np.float32)
    if DBG:
        kernel.last = res.results
    return full
```

```python
import os
from contextlib import ExitStack

import numpy as np
import concourse.bass as bass
import concourse.mybir as mybir
from concourse.bass_utils import run_bass_kernel_spmd

F32 = mybir.dt.float32
BF16 = mybir.dt.bfloat16
I32 = mybir.dt.int32
AF = mybir.ActivationFunctionType
ALU = mybir.AluOpType

ENGS = ("pe", "act", "dve", "pool", "sp")


class Buf:
    __slots__ = ("name", "last_w", "readers", "sem", "semcnt")

    def __init__(self, name=""):
        self.name = name
        self.last_w = None
        self.readers = []
        self.sem = None
        self.semcnt = 0


class Op:
    __slots__ = ("eng", "fn", "deps", "is_dma", "owner", "needs_inc", "semval", "phase", "group", "owner_sem")

    def __init__(self, eng, fn):
        self.eng = eng
        self.fn = fn
        self.deps = []
        self.is_dma = False
        self.owner = None
        self.needs_inc = False
        self.semval = None
        self.group = None
        self.phase = 0


class Group:
    def __init__(self, name):
        self.name = name
        self.sem = None
        self.base = 0
        self.n = 0


class Sched:
    def __init__(self, nc, sems):
        self.nc = nc
        self.free_sems = list(sems)
        self.ops = []
        self.eng_sem = {}
        self.eng_cnt = {}
        for e in ("pe", "act", "dve", "pool"):
            self.eng_sem[e] = self.free_sems.pop()
            self.eng_cnt[e] = 0
        self.dma_bufs = []
        self.groups = []
        self.phase = 0
        self.free_dma = []

    def _track(self, op, reads, writes):
        deps = []
        op.phase = self.phase
        for b in reads:
            if b.last_w is not None:
                deps.append(b.last_w)
        for b in writes:
            if b.last_w is not None:
                deps.append(b.last_w)
            for r in b.readers:
                deps.append(r)
        for b in writes:
            b.last_w = op
            b.readers = []
        for b in reads:
            if not op.is_dma:
                b.readers = [r for r in b.readers if r.is_dma or r.eng != op.eng]
            b.readers.append(op)
        seen = set()
        out = []
        for d in deps:
            if d is op or id(d) in seen or d.phase != self.phase:
                continue
            if op.is_dma and d.is_dma and op.group is not None and d.group is op.group:
                continue
            seen.add(id(d))
            out.append(d)
        op.deps = out

    def op(self, eng, fn, reads=(), writes=()):
        o = Op(eng, fn)
        self._track(o, reads, writes)
        self.ops.append(o)
        return o

    def dma(self, eng, fn, reads=(), writes=(), owner=None, group=None):
        o = Op(eng, fn)
        o.is_dma = True
        o.owner = owner
        o.group = group
        assert (owner is None) != (group is None)
        self._track(o, reads, writes)
        self.ops.append(o)
        return o

    def new_group(self, name):
        g = Group(name)
        g.sem = self.free_sems.pop()
        self.groups.append(g)
        return g

    def release(self, bufs):
        for b in bufs:
            if b.sem is not None:
                self.dma_bufs.remove(b)
                self.free_dma.append((b.sem, b.semcnt))
                b.sem = None

    def emit_phase(self):
        nc = self.nc
        ops = self.ops
        self.ops = []
        for o in ops:
            for d in o.deps:
                if d.is_dma:
                    continue
                if d.eng == o.eng and o.eng == "pe":
                    continue
                d.needs_inc = True
        for o in ops:
            if o.is_dma:
                if o.group is not None:
                    o.group.n += 1
                else:
                    b = o.owner
                    if b.sem is None:
                        if self.free_dma:
                            b.sem, b.semcnt = self.free_dma.pop()
                        else:
                            b.sem = self.free_sems.pop()
                            b.semcnt = 0
                        self.dma_bufs.append(b)
                    b.semcnt += 16
                    o.semval = b.semcnt
            elif o.needs_inc:
                self.eng_cnt[o.eng] += 1
                o.semval = self.eng_cnt[o.eng]
        per = {e: [] for e in ENGS}
        for o in ops:
            per[o.eng].append(o)
        waited = {e: {} for e in ENGS}
        final_waits = [(b.sem, b.semcnt) for b in self.dma_bufs]
        for g in self.groups:
            if g.n:
                final_waits.append((g.sem, g.base + 16 * g.n))

        def run(eng_name, eng):
            w = waited[eng_name]
            for o in per[eng_name]:
                for d in o.deps:
                    if d.is_dma:
                        if d.group is not None:
                            sem, val = d.group.sem, d.group.base + 16 * d.group.n
                        else:
                            sem, val = d.owner_sem, d.semval
                    else:
                        if d.eng == eng_name and eng_name == "pe":
                            continue
                        sem, val = self.eng_sem[d.eng], d.semval
                    key = id(sem)
                    if w.get(key, 0) >= val:
                        continue
                    w[key] = val
                    eng.wait_ge(sem, val)
                ins = o.fn(eng)
                if o.is_dma:
                    ins.then_inc(o.group.sem if o.group is not None else o.owner_sem, 16)
                elif o.needs_inc:
                    ins.then_inc(self.eng_sem[o.eng], 1)
            if eng_name == "sp":
                for sem, val in final_waits:
                    key = id(sem)
                    if w.get(key, 0) >= val:
                        continue
                    w[key] = val
                    eng.wait_ge(sem, val)

        for o in ops:
            if o.is_dma and o.group is None:
                o.owner_sem = o.owner.sem

        with nc.Block() as block:
            @block.tensor
            def _(e):
                run("pe", e)

            @block.scalar
            def _(e):
                run("act", e)

            @block.vector
            def _(e):
                run("dve", e)

            @block.gpsimd
            def _(e):
                run("pool", e)

            @block.sync
            def _(e):
                run("sp", e)
        for g in self.groups:
            g.base += 16 * g.n
            g.n = 0
        self.phase += 1


T = 8192
D = 1024
NCORE = 8
OWN = 2048
NB_ALL = 64
NB_A = 33
NQ = 17
IN_COLS = 3656
D_FF = 2816
NCH = 44
C_QA, C_KA, C_VA, C_QB, C_KB, C_VB, C_QI, C_KI, C_WI = 0, 512, 1024, 1536, 2048, 2560, 3072, 3584, 3648
RMS_EPS = 1e-6
BIG = 1.0e30
NBIS = 18
BIS_B = 16.0
DBG = os.environ.get("MK_DBG", "")
STOP_AFTER = os.environ.get("MK_STOP", "")


class Ctx:
    pass


def build_program():
    nc = bass.Bass("TRN2", target_bir_lowering=False)
    C = Ctx()
    C.nc = nc
    din = lambda name, shape, dt=F32: nc.dram_tensor(name, shape, dt, kind="ExternalInput").ap()
    C.xall = din("xall", [T, D])
    C.xa = din("xa", [NB_A * 128, D])
    C.pos = din("pos", [128, NB_ALL + NB_A], I32)
    C.avalid = din("avalid", [128, NB_A])
    C.tq = din("tq", [128, NQ])
    C.invf = din("invf", [1, 32])
    C.amask = din("amask", [128, NQ * 128])
    C.p_own = din("p_own", [OWN, 256])
    C.w_in = din("w_in", [D, IN_COLS])
    C.w_o = din("w_o", [D, D])
    C.w_up = din("w_up", [D, 2 * D_FF])
    C.w_down = din("w_down", [D_FF, D])
    C.w_g = din("w_g", [D, D])
    C.w_p = din("w_p", [256, D])
    C.attn_norm = din("attn_norm", [1, D])
    C.ffn_norm = din("ffn_norm", [1, D])
    C.ple_norm = din("ple_norm", [1, D])
    C.final_norm = din("final_norm", [1, D])
    C.conv_w = din("conv_w", [3, 2 * D_FF])
    C.conv_b = din("conv_b", [1, 2 * D_FF])
    C.out = nc.dram_tensor("out", [OWN, D], F32, kind="ExternalOutput").ap()
    C.vb_d = nc.dram_tensor("vb_d", [NB_ALL, 128, 520], BF16).ap()
    C.va_d = nc.dram_tensor("va_d", [NB_A, 128, 520], BF16).ap()
    C.kaT_d = nc.dram_tensor("kaT_d", [NB_A, 128, 512], BF16).ap()
    C.qaT_d = nc.dram_tensor("qaT_d", [NQ, 128, 512], BF16).ap()
    C.qbT_d = nc.dram_tensor("qbT_d", [NQ, 128, 512], BF16).ap()
    C.qiT_d = nc.dram_tensor("qiT_d", [NQ, 128, 512], BF16).ap()
    C.lohi_d = nc.dram_tensor("lohi_d", [NQ, 128, 16], F32).ap()
    C.mix_d = nc.dram_tensor("mix_d", [NQ, 128, 1024], BF16).ap()
    C.x2_d = nc.dram_tensor("x2_d", [OWN, D], F32).ap()
    C.wob_d = nc.dram_tensor("wob_d", [D, D], BF16).ap()
    C.wupb_d = nc.dram_tensor("wupb_d", [D, 2 * D_FF], BF16).ap()
    C.wdnb_d = nc.dram_tensor("wdnb_d", [D_FF, D], BF16).ap()
    C.wgb_d = nc.dram_tensor("wgb_d", [D, D], BF16).ap()
    C.wpb_d = nc.dram_tensor("wpb_d", [256, D], BF16).ap()
    C.dbg = {}
    if DBG:
        dout = lambda name, shape, dt=F32: nc.dram_tensor(name, shape, dt, kind="ExternalOutput").ap()
        C.dbg["kbT"] = dout("dbg_kbT", [128, NB_ALL * 512], BF16)
        C.dbg["kiT"] = dout("dbg_kiT", [128, T], BF16)
        C.dbg["mix"] = dout("dbg_mix", [NQ, 128, 1024], BF16)
        C.dbg["x2"] = dout("dbg_x2", [OWN, D])

    with ExitStack() as top:
        sems = [top.enter_context(nc.semaphore(f"s{i}")) for i in range(96)]
        S = Sched(nc, sems)
        C.S = S
        C.ident = top.enter_context(nc.sbuf_tensor("ident", [128, 128], BF16))
        C.identf = top.enter_context(nc.sbuf_tensor("identf", [128, 128], F32))
        C.Bident = Buf("ident")
        with ExitStack() as kv:
            C.kbT = kv.enter_context(nc.sbuf_tensor("kbT", [128, NB_ALL, 512], BF16))
            C.kiT = kv.enter_context(nc.sbuf_tensor("kiT", [128, T], BF16))
            C.BkbT = [Buf(f"kbT{i}") for i in range(NB_ALL)]
            C.BkiT = [Buf(f"kiT{i}") for i in range(NB_ALL)]
            phase1(C)
            if STOP_AFTER != "1":
                phaseA(C)
            if STOP_AFTER not in ("1", "A"):
                phaseB(C)
        if STOP_AFTER not in ("1", "A", "B"):
            phaseC1(C)
            if STOP_AFTER != "C1":
                phaseC2(C)
        if STOP_AFTER:
            final_dummy(C)
    return nc


def final_dummy(C):
    nc, S = C.nc, C.S
    with nc.sbuf_tensor("zz", [128, 1024], F32) as zz:
        Bz = Buf()
        S.op("dve", lambda e: e.memset(zz[:], 0.0), writes=[Bz])
        for i in range(16):
            S.dma("sp", lambda e, i=i: e.dma_start(out=C.out[i * 128:(i + 1) * 128, :], in_=zz[:]), reads=[Bz], owner=Bz)
        S.emit_phase()


def phase1(C):
    nc, S = C.nc, C.S
    NTB = NB_ALL + NB_A
    with ExitStack() as es:
        sb = lambda n, s, d: es.enter_context(nc.sbuf_tensor(n, s, d))
        pst = lambda n, s, d: es.enter_context(nc.psum_tensor(n, s, d))
        win = sb("win", [128, 8, IN_COLS], BF16)
        gat = sb("gat", [128, D], F32)
        posi = sb("posi", [128, NTB], I32)
        posf = sb("posf", [128, NTB], F32)
        avl = sb("avl", [128, NB_A], F32)
        invt = sb("invt", [128, 32], F32)
        cosT = sb("cosT", [128, NTB, 32], F32)
        sinT = sb("sinT", [128, NTB, 32], F32)
        identf = C.identf
        ones8 = sb("ones8", [128, 8], F32)
        xt = [sb(f"xt{i}", [128, D], F32) for i in range(2)]
        sqj = sb("sqj", [128, D], BF16)
        ss = [sb(f"ss{i}", [128, 1], F32) for i in range(2)]
        rstd = [sb(f"rstd{i}", [128, 1], F32) for i in range(2)]
        hb = [sb(f"hb{i}", [128, D], BF16) for i in range(2)]
        hT = [sb(f"hT{i}", [128, 8, 128], BF16) for i in range(2)]
        ta = [sb(f"ta{i}", [128, 512], F32) for i in range(2)]
        tb_ = [sb(f"tb{i}", [128, 512], F32) for i in range(2)]
        zb = [sb(f"zb{i}", [128, 512], BF16) for i in range(6)]
        zTs = [sb(f"zTs{i}", [128, 512], BF16) for i in range(3)]
        vsb = [sb(f"vsb{i}", [128, 520], BF16) for i in range(2)]
        vsa = [sb(f"vsa{i}", [128, 520], BF16) for i in range(2)]
        wsc = [sb(f"wsc{i}", [128, 8], F32) for i in range(2)]
        lohi = [sb(f"lohi{i}", [128, 16], F32) for i in range(2)]
        hT_ps = [pst(f"hTps{i}", [128, 1024], BF16) for i in range(2)]
        z_ps = [pst(f"zps{i}", [128, 512], F32) for i in range(4)]
        zT_ps = [pst(f"zTps{i}", [128, 1024], BF16) for i in range(2)]

        Bwin, Bgat, Bpos, Bavl, Binv, Bcs, Bidf, Bones = (Buf() for _ in range(8))
        Bxt = [Buf() for _ in range(2)]
        Bsq = Buf()
        Bss = [Buf() for _ in range(2)]
        Brs = [Buf() for _ in range(2)]
        Bhb = [Buf() for _ in range(2)]
        BhT = [Buf() for _ in range(2)]
        Bta = [Buf() for _ in range(2)]
        Btb = [Buf() for _ in range(2)]
        Bzb = [Buf() for _ in range(6)]
        BzTs = [Buf() for _ in range(3)]
        Bvsb = [Buf() for _ in range(2)]
        Bvsa = [Buf() for _ in range(2)]
        Bwsc = [Buf() for _ in range(2)]
        Blohi = [Buf() for _ in range(2)]
        BhTps = [Buf() for _ in range(2)]
        Bzps = [Buf() for _ in range(4)]
        BzTps = [Buf() for _ in range(2)]

        g0 = S.new_group("p1init")
        gw = S.new_group("p1w")
        S.dma("sp", lambda e: e.dma_start(out=posi[:], in_=C.pos), writes=[Bpos], group=g0)
        S.dma("sp", lambda e: e.dma_start(out=avl[:], in_=C.avalid), writes=[Bavl], group=g0)
        S.dma("sp", lambda e: e.dma_start(out=invt[:], in_=C.invf.partition_broadcast(128)), writes=[Binv], group=g0)
        S.dma("sp", lambda e: e.dma_start(out=gat[:], in_=C.attn_norm.partition_broadcast(128)), writes=[Bgat], group=g0)
        gw2 = S.new_group("p1w2")
        Bwin2 = Buf()
        for (c0, c1, grp, bw) in ((2048, IN_COLS, gw, Bwin), (0, 2048, gw2, Bwin2)):
            for m in range(8):
                S.dma("pool", lambda e, m=m, c0=c0, c1=c1: e.dma_start(out=win[:, m, c0:c1], in_=C.w_in[m * 128:(m + 1) * 128, c0:c1]),
                      writes=[bw], group=grp)
        S.op("pool", lambda e: e.memset(identf[:], 0.0), writes=[Bidf])
        S.op("pool", lambda e: e.affine_select(out=identf[:], in_=identf[:], compare_op=ALU.not_equal, fill=1.0,
                                               base=0, pattern=[[-1, 128]], channel_multiplier=1), reads=[Bidf], writes=[Bidf])
        S.op("dve", lambda e: e.tensor_copy(out=C.ident[:], in_=identf[:]), reads=[Bidf], writes=[C.Bident])
        S.op("dve", lambda e: e.memset(ones8[:], 1.0), writes=[Bones])
        for i in range(2):
            S.op("dve", lambda e, i=i: e.memset(vsb[i][:], 1.0), writes=[Bvsb[i]])
            S.op("dve", lambda e, i=i: e.memset(vsa[i][:], 1.0), writes=[Bvsa[i]])
        cnt = {"z": 0, "zT": 0, "zTs": 0, "vsb": 0, "vsa": 0, "zb": 0, "t": 0}

        def mm_group(s, c0, n):
            r = cnt["z"] % 4
            cnt["z"] += 1
            for m in range(8):
                S.op("pe", lambda e, m=m, r=r, s=s: e.matmul(z_ps[r][:, 0:n], lhsT=hT[s][:, m, :], rhs=win[:, m, c0:c0 + n],
                                                             start=(m == 0), stop=(m == 7)),
                     reads=[BhT[s], Bwin if c0 >= 2048 else Bwin2], writes=[Bzps[r]])
            return r

        def rope(r, n, tbk):
            zi = cnt["zb"] % 6
            cnt["zb"] += 1
            ti = cnt["t"] % 2
            cnt["t"] += 1
            H = n // 64
            zv = z_ps[r][:, 0:n].rearrange("p (h t d) -> p h t d", h=H, t=2)
            A4 = ta[ti][:, 0:n].rearrange("p (h t d) -> p h t d", h=H, t=2)
            B4 = tb_[ti][:, 0:n].rearrange("p (h t d) -> p h t d", h=H, t=2)
            Z4 = zb[zi][:, 0:n].rearrange("p (h t d) -> p h t d", h=H, t=2)
            cosb = cosT[:, tbk, :].unsqueeze(1).unsqueeze(1).to_broadcast([128, H, 2, 32])
            sinb = sinT[:, tbk, :].unsqueeze(1).to_broadcast([128, H, 32])
            S.op("dve", lambda e: e.tensor_tensor(out=A4, in0=zv, in1=cosb, op=ALU.mult), reads=[Bzps[r], Bcs], writes=[Bta[ti]])
            S.op("dve", lambda e: e.tensor_tensor(out=B4[:, :, 0, :], in0=zv[:, :, 1, :], in1=sinb, op=ALU.mult),
                 reads=[Bzps[r], Bcs], writes=[Btb[ti]])
            S.op("dve", lambda e: e.tensor_tensor(out=B4[:, :, 1, :], in0=zv[:, :, 0, :], in1=sinb, op=ALU.mult),
                 reads=[Bzps[r], Bcs], writes=[Btb[ti]])
            S.op("pool", lambda e: e.tensor_tensor(out=Z4[:, :, 0, :], in0=A4[:, :, 0, :], in1=B4[:, :, 0, :], op=ALU.subtract),
                 reads=[Bta[ti], Btb[ti]], writes=[Bzb[zi]])
            S.op("pool", lambda e: e.tensor_tensor(out=Z4[:, :, 1, :], in0=A4[:, :, 1, :], in1=B4[:, :, 1, :], op=ALU.add),
                 reads=[Bta[ti], Btb[ti]], writes=[Bzb[zi]])
            return zi

        pending = []

        def transposeT(zi, ncols, dst_fn, dst_bufs_w, then=None):
            def run():
                q = cnt["zT"] % 2
                cnt["zT"] += 1
                nt = ncols // 128
                for j in range(nt):
                    S.op("pe", lambda e, j=j, q=q: e.transpose(out=zT_ps[q][:, j * 128:(j + 1) * 128], in_=zb[zi][:, j * 128:(j + 1) * 128],
                                                               identity=C.ident[:]),
                         reads=[Bzb[zi], C.Bident], writes=[BzTps[q]])
                S.op("act", lambda e, q=q: e.activation(out=dst_fn(), in_=zT_ps[q][:, 0:ncols], func=AF.Copy),
                     reads=[BzTps[q]], writes=dst_bufs_w)
                if then is not None:
                    then()
            pending.append(run)

        def to_dram_T(zi, dram_ap):
            k = cnt["zTs"] % 3
            cnt["zTs"] += 1
            transposeT(zi, 512, lambda k=k: zTs[k][:], [BzTs[k]],
                       then=lambda k=k: S.dma("sp", lambda e, k=k: e.dma_start(out=dram_ap, in_=zTs[k][:]), reads=[BzTs[k]], writes=[Buf()],
                                              owner=BzTs[k]))

        def vcopy(r, dst, Bdst):
            S.op("act", lambda e: e.activation(out=dst[:].rearrange("p (h c) -> p h c", c=65)[:, :, 0:64],
                                               in_=z_ps[r][:, :].rearrange("p (h d) -> p h d", d=64), func=AF.Copy),
                 reads=[Bzps[r]], writes=[Bdst])

        def load_x(tbk):
            s = tbk % 2
            is_all = tbk < NB_ALL
            blk = tbk if is_all else tbk - NB_ALL
            src = C.xall if is_all else C.xa
            S.dma("sp", lambda e: e.dma_start(out=xt[s][:], in_=src[blk * 128:(blk + 1) * 128, :]), writes=[Bxt[s]], owner=Bxt[s])

        def stageA(tbk):
            s = tbk % 2
            if tbk + 1 < NTB:
                load_x(tbk + 1)
            S.op("act", lambda e, s=s: e.activation(out=sqj[:], in_=xt[s][:], func=AF.Square, accum_out=ss[s][:, 0:1]),
                 reads=[Bxt[s]], writes=[Bsq, Bss[s]])
            S.op("act", lambda e, s=s: e.activation(out=rstd[s][:], in_=ss[s][:], func=AF.Sqrt, scale=1.0 / D, bias=RMS_EPS),
                 reads=[Bss[s]], writes=[Brs[s]])
            S.op("dve", lambda e, s=s: e.reciprocal(out=rstd[s][:], in_=rstd[s][:]), reads=[Brs[s]], writes=[Brs[s]])
            S.op("dve", lambda e, s=s: e.scalar_tensor_tensor(out=hb[s][:], in0=xt[s][:], scalar=rstd[s][:, 0:1], in1=gat[:],
                                                              op0=ALU.mult, op1=ALU.mult),
                 reads=[Bxt[s], Brs[s], Bgat], writes=[Bhb[s]])
            for m in range(8):
                S.op("pe", lambda e, s=s, m=m: e.transpose(out=hT_ps[s][:, m * 128:(m + 1) * 128], in_=hb[s][:, m * 128:(m + 1) * 128],
                                                           identity=C.ident[:]),
                     reads=[Bhb[s], C.Bident], writes=[BhTps[s]])
            S.op("act", lambda e, s=s: e.activation(out=hT[s][:].rearrange("p m t -> p (m t)"), in_=hT_ps[s][:], func=AF.Copy),
                 reads=[BhTps[s]], writes=[BhT[s]])

        load_x(0)
        stageA(0)
        S.op("dve", lambda e: e.tensor_copy(out=posf[:], in_=posi[:]), reads=[Bpos], writes=[Bpos])
        MAGIC = 12582912.0
        TWO_PI = float(2 * np.pi)
        CH = 16
        for b0 in range(0, NTB, CH):
            nb_ = min(CH, NTB - b0)
            ang = ta[0][:, 0:nb_ * 32].rearrange("p (b d) -> p b d", d=32)
            kk = tb_[0][:, 0:nb_ * 32].rearrange("p (b d) -> p b d", d=32)
            Bang, Bkk = Bta[0], Btb[0]
            S.op("dve", lambda e, ang=ang, b0=b0, nb_=nb_: e.tensor_tensor(
                out=ang, in0=posf[:, b0:b0 + nb_].unsqueeze(2).to_broadcast([128, nb_, 32]),
                in1=invt[:, :].unsqueeze(1).to_broadcast([128, nb_, 32]), op=ALU.mult),
                reads=[Bpos, Binv], writes=[Bang])
            for dst, shift in ((sinT, 0.0), (cosT, float(np.pi / 2))):
                S.op("dve", lambda e, ang=ang, kk=kk, shift=shift: e.tensor_scalar(out=kk, in0=ang, scalar1=shift,
                                                                                   scalar2=1.0 / TWO_PI, op0=ALU.add, op1=ALU.mult),
                     reads=[Bang], writes=[Bkk])
                S.op("dve", lambda e, kk=kk: e.tensor_scalar(out=kk, in0=kk, scalar1=MAGIC, scalar2=None, op0=ALU.add),
                     reads=[Bkk], writes=[Bkk])
                S.op("dve", lambda e, kk=kk: e.tensor_scalar(out=kk, in0=kk, scalar1=MAGIC, scalar2=-TWO_PI,
                                                             op0=ALU.subtract, op1=ALU.mult), reads=[Bkk], writes=[Bkk])
                S.op("dve", lambda e, ang=ang, kk=kk, shift=shift: e.scalar_tensor_tensor(out=kk, in0=ang, scalar=shift, in1=kk,
                                                                                          op0=ALU.add, op1=ALU.add),
                     reads=[Bang, Bkk], writes=[Bkk])
                S.op("dve", lambda e, kk=kk: e.tensor_scalar(out=kk, in0=kk, scalar1=float(np.pi), scalar2=float(-np.pi),
                                                             op0=ALU.min, op1=ALU.max), reads=[Bkk], writes=[Bkk])
                S.op("act", lambda e, kk=kk, dst=dst, b0=b0, nb_=nb_: e.activation(out=dst[:, b0:b0 + nb_, :], in_=kk, func=AF.Sin),
                     reads=[Bkk], writes=[Bcs])

        for tbk in range(NTB):
            s = tbk % 2
            is_all = tbk < NB_ALL
            blk = tbk if is_all else tbk - NB_ALL
            isq = (not is_all) and blk >= NB_A - NQ
            qi_ = blk - (NB_A - NQ)
            if is_all:
                groups = [("kb", C_KB, 512), ("vb", C_VB, 512), ("ki", C_KI, 64)]
            elif not isq:
                groups = [("ka", C_KA, 512), ("va", C_VA, 512)]
            else:
                groups = [("ka", C_KA, 512), ("va", C_VA, 512), ("wi", C_WI, 8), ("qa", C_QA, 512), ("qb", C_QB, 512), ("qi", C_QI, 512)]
            for g0i in range(0, len(groups), 3):
                rnd = groups[g0i:g0i + 3]
                rs = [mm_group(s, c0, n) for (_, c0, n) in rnd]
                if g0i == 0 and tbk + 1 < NTB:
                    stageA(tbk + 1)
                for fn in pending:
                    fn()
                pending.clear()
                for (kind, c0, n), r in zip(rnd, rs):
                    if kind == "kb":
                        zi = rope(r, 512, tbk)
                        transposeT(zi, 512, lambda blk=blk: C.kbT[:, blk, :], [C.BkbT[blk]])
                    elif kind == "vb":
                        k = cnt["vsb"] % 2
                        cnt["vsb"] += 1
                        vcopy(r, vsb[k], Bvsb[k])
                        S.dma("sp", lambda e, k=k, blk=blk: e.dma_start(out=C.vb_d[blk], in_=vsb[k][:]), reads=[Bvsb[k]], writes=[Buf()],
                              owner=Bvsb[k])
                    elif kind == "ki":
                        zi = rope(r, 64, tbk)
                        S.op("pool", lambda e, zi=zi: e.tensor_copy(out=zb[zi][:, 64:128], in_=zb[zi][:, 0:64]), reads=[Bzb[zi]], writes=[Bzb[zi]])
                        transposeT(zi, 128, lambda blk=blk: C.kiT[:, blk * 128:(blk + 1) * 128], [C.BkiT[blk]])
                    elif kind == "ka":
                        zi = rope(r, 512, tbk)
                        to_dram_T(zi, C.kaT_d[blk])
                    elif kind == "va":
                        k = cnt["vsa"] % 2
                        cnt["vsa"] += 1
                        vcopy(r, vsa[k], Bvsa[k])
                        S.op("dve", lambda e, k=k, blk=blk: e.tensor_scalar(out=vsa[k][:].rearrange("p (h c) -> p h c", c=65)[:, :, 64],
                                                                            in0=ones8[:], scalar1=avl[:, blk:blk + 1], scalar2=None, op0=ALU.mult),
                             reads=[Bones, Bavl], writes=[Bvsa[k]])
                        S.dma("sp", lambda e, k=k, blk=blk: e.dma_start(out=C.va_d[blk], in_=vsa[k][:]), reads=[Bvsa[k]], writes=[Buf()],
                              owner=Bvsa[k])
                    elif kind == "qa":
                        zi = rope(r, 512, tbk)
                        to_dram_T(zi, C.qaT_d[qi_])
                    elif kind == "qb":
                        zi = rope(r, 512, tbk)
                        to_dram_T(zi, C.qbT_d[qi_])
                    elif kind == "wi":
                        S.op("dve", lambda e, r=r, s=s: e.tensor_scalar(out=wsc[s][:], in0=z_ps[r][:, 0:8], scalar1=float(1.0 / (8.0 * np.sqrt(8.0))),
                                                                        scalar2=None, op0=ALU.mult), reads=[Bzps[r]], writes=[Bwsc[s]])
                        S.op("dve", lambda e, s=s: e.tensor_scalar(out=lohi[s][:, 8:16], in0=wsc[s][:], scalar1=0.0, scalar2=2.0,
                                                                   op0=ALU.is_ge, op1=ALU.mult), reads=[Bwsc[s]], writes=[Blohi[s]])
                        S.op("dve", lambda e, s=s: e.tensor_scalar(out=lohi[s][:, 0:8], in0=lohi[s][:, 8:16], scalar1=-1.0, scalar2=None,
                                                                   op0=ALU.add), reads=[Blohi[s]], writes=[Blohi[s]])
                        S.dma("sp", lambda e, s=s, qi_=qi_: e.dma_start(out=C.lohi_d[qi_], in_=lohi[s][:]), reads=[Blohi[s]], writes=[Buf()],
                              owner=Blohi[s])
                    elif kind == "qi":
                        zi = rope(r, 512, tbk)
                        S.op("pool", lambda e, s=s, zi=zi: e.tensor_tensor(out=zb[zi][:].rearrange("p (h d) -> p h d", d=64),
                                                                           in0=zb[zi][:].rearrange("p (h d) -> p h d", d=64),
                                                                           in1=wsc[s][:, :].unsqueeze(2).to_broadcast([128, 8, 64]), op=ALU.mult),
                             reads=[Bzb[zi], Bwsc[s]], writes=[Bzb[zi]])
                        to_dram_T(zi, C.qiT_d[qi_])
        for fn in pending:
            fn()
        pending.clear()
        if DBG:
            gd = S.new_group("p1dbg")
            S.dma("sp", lambda e: e.dma_start(out=C.dbg["kbT"], in_=C.kbT[:].rearrange("p b c -> p (b c)")), reads=C.BkbT, writes=[Buf()],
                  group=gd)
            S.dma("sp", lambda e: e.dma_start(out=C.dbg["kiT"], in_=C.kiT[:]), reads=C.BkiT, writes=[Buf()], group=gd)
        S.emit_phase()
        S.release(Bxt + BzTs + Bvsb + Bvsa + Blohi)


def phaseA(C):
    nc, S = C.nc, C.S
    with ExitStack() as es:
        sb = lambda n, s, d: es.enter_context(nc.sbuf_tensor(n, s, d))
        pst = lambda n, s, d: es.enter_context(nc.psum_tensor(n, s, d))
        kaT = sb("kaT", [128, NB_A, 512], BF16)
        va = sb("va", [128, NB_A, 520], BF16)
        qaT = sb("qaT", [128, NQ, 512], BF16)
        amk = sb("amk", [128, NQ, 128], BF16)
        pe_ = [sb(f"pe{i}", [128, 1024], BF16) for i in range(2)]
        pm = [sb(f"pm{i}", [128, 1024], BF16) for i in range(2)]
        rd = [sb(f"rd{i}", [128, 8], F32) for i in range(2)]
        mxa = [sb(f"mxa{i}", [128, 512], BF16) for i in range(2)]
        STA = [pst(f"st{i}", [128, 1024], F32) for i in range(2)]
        st = [[STA[i][:, j * 512:(j + 1) * 512] for j in range(2)] for i in range(2)]
        acc = [[pst(f"acc{i}{j}", [128, 512], F32) for j in range(2)] for i in range(2)]
        BkaT, Bva, BqaT, Bamk = Buf(), Buf(), Buf(), Buf()
        Bpe = [Buf() for _ in range(2)]
        Bpm = [Buf() for _ in range(2)]
        Brd = [Buf() for _ in range(2)]
        Bmxa = [Buf() for _ in range(2)]
        Bst = [[Buf() for _ in range(2)] for _ in range(2)]
        Bacc = [[Buf() for _ in range(2)] for _ in range(2)]
        g = S.new_group("pAinit")
        for c0 in range(0, NB_A, 11):
            S.dma("sp", lambda e, c0=c0: e.dma_start(out=kaT[:, c0:c0 + 11, :], in_=C.kaT_d[c0:c0 + 11].rearrange("b p c -> p b c")),
                  writes=[BkaT], group=g)
            S.dma("sp", lambda e, c0=c0: e.dma_start(out=va[:, c0:c0 + 11, :], in_=C.va_d[c0:c0 + 11].rearrange("b p c -> p b c")),
                  writes=[Bva], group=g)
        S.dma("sp", lambda e: e.dma_start(out=qaT[:], in_=C.qaT_d.rearrange("b p c -> p b c")), writes=[BqaT], group=g)
        gp = S.new_group("pAinit_sw")
        S.dma("pool", lambda e: e.dma_start(out=amk[:].rearrange("p r q -> p (r q)"), in_=C.amask), writes=[Bamk], group=gp)
        def qk_A(i, r, sidx):
            kb = i + r
            for h in range(8):
                rows = slice((h % 2) * 64, (h % 2) * 64 + 64)
                pr = h // 2
                S.op("pe", lambda e, h=h, rows=rows, pr=pr: e.matmul(
                    st[sidx][h % 2][:, pr * 128:(pr + 1) * 128], lhsT=kaT[rows, kb, pr * 128:(pr + 1) * 128],
                    rhs=qaT[rows, i, pr * 128:(pr + 1) * 128], start=True, stop=True),
                    reads=[BkaT, BqaT], writes=[Bst[sidx][h % 2]])

        steps = [(i, r) for i in range(NQ) for r in range(NQ)]
        qk_A(0, 0, 0)
        qk_A(steps[1][0], steps[1][1], 1)
        for it, (i, r) in enumerate(steps):
            a = i % 2
            kb = i + r
            sidx = it % 2
            S.op("act", lambda e, sidx=sidx: e.activation(out=pe_[sidx][:, :], in_=STA[sidx][:, :], func=AF.Exp, scale=0.125),
                 reads=[Bst[sidx][0], Bst[sidx][1]], writes=[Bpe[sidx]])
            if it + 2 < len(steps):
                qk_A(steps[it + 2][0], steps[it + 2][1], sidx)
            S.op("dve", lambda e, sidx=sidx, r=r: e.tensor_tensor(out=pm[sidx][:].rearrange("p (h q) -> p h q", q=128),
                                                                  in0=pe_[sidx][:].rearrange("p (h q) -> p h q", q=128),
                                                                  in1=amk[:, r, :].unsqueeze(1).to_broadcast([128, 8, 128]), op=ALU.mult),
                 reads=[Bpe[sidx], Bamk], writes=[Bpm[sidx]])
            for h in range(8):
                gi = h // 4
                cc = (h % 4) * 65
                pcol = (h % 2) * 512 + (h // 2) * 128
                S.op("pe", lambda e, h=h, gi=gi, cc=cc, pcol=pcol, sidx=sidx, kb=kb, a=a, r=r: e.matmul(
                    acc[a][gi][:, cc:cc + 65], lhsT=pm[sidx][:, pcol:pcol + 128], rhs=va[:, kb, h * 65:(h + 1) * 65],
                    start=(r == 0 and h % 4 == 0), stop=(r == NQ - 1), skip_group_check=True),
                    reads=[Bpm[sidx], Bva], writes=[Bacc[a][gi]])
            if r != NQ - 1:
                continue
            for gi in range(2):
                accv = acc[a][gi][:, 0:260].rearrange("p (h c) -> p h c", c=65)
                S.op("dve", lambda e, a=a, gi=gi, accv=accv: e.tensor_scalar(out=rd[a][:, gi * 4:(gi + 1) * 4].unsqueeze(2), in0=accv[:, :, 64:65],
                                                                             scalar1=1e-30, scalar2=None, op0=ALU.max),
                     reads=[Bacc[a][gi]], writes=[Brd[a]])
                S.op("dve", lambda e, a=a, gi=gi: e.reciprocal(out=rd[a][:, gi * 4:(gi + 1) * 4], in_=rd[a][:, gi * 4:(gi + 1) * 4]),
                     reads=[Brd[a]], writes=[Brd[a]])
                S.op("dve", lambda e, a=a, gi=gi, accv=accv: e.tensor_tensor(
                    out=mxa[a][:, gi * 256:(gi + 1) * 256].rearrange("p (h d) -> p h d", d=64), in0=accv[:, :, 0:64],
                    in1=rd[a][:, gi * 4:(gi + 1) * 4].unsqueeze(2).to_broadcast([128, 4, 64]), op=ALU.mult),
                    reads=[Bacc[a][gi], Brd[a]], writes=[Bmxa[a]])
            S.dma("sp", lambda e, a=a, i=i: e.dma_start(out=C.mix_d[i][:, 0:512], in_=mxa[a][:]), reads=[Bmxa[a]], writes=[Buf()], owner=Bmxa[a])
        S.emit_phase()
        S.release(Bmxa)
        if DBG and STOP_AFTER == "A":
            dump_mix(C)


def dump_mix(C):
    nc, S = C.nc, C.S
    with nc.sbuf_tensor("dm0", [128, 1024], BF16) as dm0, nc.sbuf_tensor("dm1", [128, 1024], BF16) as dm1:
        dm = [dm0, dm1]
        Bd = [Buf(), Buf()]
        for i in range(NQ):
            k = i % 2
            c1 = 512 if STOP_AFTER == "A" else 1024
            S.dma("sp", lambda e, i=i, k=k, c1=c1: e.dma_start(out=dm[k][:, 0:c1], in_=C.mix_d[i][:, 0:c1]), writes=[Bd[k]], owner=Bd[k])
            S.dma("sp", lambda e, i=i, k=k, c1=c1: e.dma_start(out=C.dbg["mix"][i][:, 0:c1], in_=dm[k][:, 0:c1]), reads=[Bd[k]], writes=[Buf()],
                  owner=Bd[k])
        S.emit_phase()
        S.release(Bd)


def phaseB(C):
    nc, S = C.nc, C.S
    NCHK = T // 512
    with ExitStack() as es:
        sb = lambda n, s, d: es.enter_context(nc.sbuf_tensor(n, s, d))
        pst = lambda n, s, d: es.enter_context(nc.psum_tensor(n, s, d))
        iotaN = sb("iotaN", [128, T], F32)
        Iacc = sb("Iacc", [128, T], F32)
        Mb = sb("Mb", [128, T], BF16)
        qbT = [sb(f"qbT{i}", [128, 512], BF16) for i in range(2)]
        qiT = [sb(f"qiT{i}", [128, 512], BF16) for i in range(2)]
        lohi = [sb(f"lohiB{i}", [128, 16], F32) for i in range(2)]
        tqf = sb("tqf", [128, NQ], F32)
        cq = sb("cq", [128, NQ], F32)
        fh = [sb(f"fh{i}", [128, 1024], F32) for i in range(3)]
        lo_ = sb("lo_", [128, 1], F32)
        cand = sb("cand", [128, 1], F32)
        cntt = sb("cntt", [128, 1], F32)
        ind = sb("ind", [128, 1], F32)
        ncand = sb("ncand", [128, 1], F32)
        ssum = sb("ssum", [128, 1], F32)
        vbs = [sb(f"vbs{i}", [128, 8, 520], BF16) for i in range(2)]
        pe_ = [sb(f"peB{i}", [128, 1024], BF16) for i in range(2)]
        pm = [sb(f"pmB{i}", [128, 1024], BF16) for i in range(2)]
        mts = [sb(f"mts{i}", [128, 1024], BF16) for i in range(2)]
        rd = [sb(f"rdB{i}", [128, 8], F32) for i in range(2)]
        mxb = [sb(f"mxb{i}", [128, 512], BF16) for i in range(2)]
        PB = [pst(f"PB{i}", [128, 1024], F32) for i in range(4)]
        BP = [[Buf() for _ in range(2)] for _ in range(4)]
        scst = [[PB[i][:, j * 512:(j + 1) * 512] for j in range(2)] for i in range(2)]
        acc = [PB[2][:, j * 512:(j + 1) * 512] for j in range(2)]
        mt_ps = [PB[3][:, j * 512:(j + 1) * 512].bitcast(BF16) for j in range(2)]

        Biota, Btq, Bcq, Blo, Bcand, Bcnt, Bind, Bncand, Bssum, BMall2 = (Buf() for _ in range(10))
        BI = [Buf() for _ in range(NCHK)]
        BM = [Buf() for _ in range(NB_ALL // 8)]
        BMall = Buf()
        BqbT = [Buf() for _ in range(2)]
        BqiT = [Buf() for _ in range(2)]
        Blohi = [Buf() for _ in range(2)]
        Bfh = [Buf() for _ in range(3)]
        Bvbs = [Buf() for _ in range(2)]
        Bpe = [Buf() for _ in range(2)]
        Bpm = [Buf() for _ in range(2)]
        Bmts = [Buf() for _ in range(2)]
        Brd = [Buf() for _ in range(2)]
        Bmxb = [Buf() for _ in range(2)]
        Bscst = [[BP[i][j] for j in range(2)] for i in range(2)]
        Bacc = [BP[2][j] for j in range(2)]
        Bmtps = [BP[3][j] for j in range(2)]

        g = S.new_group("pBinit")
        S.dma("sp", lambda e: e.dma_start(out=tqf[:], in_=C.tq), writes=[Btq], group=g)
        S.op("dve", lambda e: e.tensor_scalar(out=cq[:], in0=tqf[:], scalar1=0.5, scalar2=-BIG, op0=ALU.add, op1=ALU.mult),
             reads=[Btq], writes=[Bcq])
        S.op("pool", lambda e: e.iota(Iacc[:].bitcast(I32), pattern=[[1, T]], base=0, channel_multiplier=0), writes=BI)
        S.op("dve", lambda e: e.tensor_scalar(out=iotaN[:], in0=Iacc[:].bitcast(I32), scalar1=-BIG, scalar2=None, op0=ALU.mult),
             reads=BI, writes=[Biota])

        gcast = S.new_group("pBcast")
        for (src, dst, rows, cols) in ((C.w_o, C.wob_d, D, D), (C.w_up, C.wupb_d, D, 2 * D_FF), (C.w_down, C.wdnb_d, D_FF, D),
                                       (C.w_g, C.wgb_d, D, D), (C.w_p, C.wpb_d, 256, D)):
            cw_ = 1408 if cols == 2 * D_FF else 1024
            for r0 in range(0, rows, 128):
                for c0 in range(0, cols, cw_):
                    S.dma("pool", lambda e, src=src, dst=dst, r0=r0, c0=c0, cw_=cw_: e.dma_start(out=dst[r0:r0 + 128, c0:c0 + cw_],
                                                                                         in_=src[r0:r0 + 128, c0:c0 + cw_]),
                          writes=[Buf()], group=gcast)
        fcnt = 0

        def load_q(i):
            qs = i % 2
            S.dma("sp", lambda e: e.dma_start(out=qbT[qs][:], in_=C.qbT_d[i]), writes=[BqbT[qs]], owner=BqbT[qs])
            S.dma("sp", lambda e: e.dma_start(out=qiT[qs][:], in_=C.qiT_d[i]), writes=[BqiT[qs]], owner=BqiT[qs])
            S.dma("sp", lambda e: e.dma_start(out=lohi[qs][:], in_=C.lohi_d[i]), writes=[Blohi[qs]], owner=Blohi[qs])

        load_q(0)
        for i in range(NQ):
            qs = i % 2
            if i + 1 < NQ:
                load_q(i + 1)
            nkc = min(NCHK, -(-(3 * (OWN // 128) + i) // 4))
            nkb = nkc * 4
            nkeys = nkb * 128
            HB = (nkc + 1) // 2
            h1 = HB * 512
            n_act = nkeys - h1
            cthr = float(256 - n_act // 2)
            NG = -(-nkb // 8)
            for cp in range((nkc + 1) // 2):
                nck = min(2, nkc - 2 * cp)
                w = nck * 512
                c0 = cp * 1024
                BIc = BI[2 * cp:2 * cp + nck]
                for h in range(8):
                    rows = slice((h % 2) * 64, (h % 2) * 64 + 64)
                    pr = h // 2
                    ti = (h % 2) + 2 * (pr % 2)
                    for u in range(nck):
                        S.op("pe", lambda e, rows=rows, pr=pr, ti=ti, qs=qs, u=u, c0=c0: e.matmul(
                            PB[ti][:, u * 512:(u + 1) * 512], lhsT=qiT[qs][rows, pr * 128:(pr + 1) * 128],
                            rhs=C.kiT[rows, c0 + u * 512:c0 + (u + 1) * 512], start=True, stop=True),
                            reads=[BqiT[qs]] + C.BkiT[(c0 // 128) + u * 4:(c0 // 128) + (u + 1) * 4], writes=[BP[ti][u]])
                    k = fcnt % 3
                    fcnt += 1
                    S.op("act", lambda e, ti=ti, k=k, qs=qs, h=h, w=w: e.activation(out=fh[k][:, 0:w], in_=PB[ti][:, 0:w], func=AF.Relu,
                                                                                 scale=lohi[qs][:, h:h + 1]),
                         reads=BP[ti][0:nck] + [Blohi[qs]], writes=[Bfh[k]])
                    if h == 0:
                        S.op("dve", lambda e, k=k, c0=c0, w=w, qs=qs, h=h: e.tensor_scalar(out=Iacc[:, c0:c0 + w], in0=fh[k][:, 0:w],
                                                                                       scalar1=lohi[qs][:, h:h + 1], scalar2=None, op0=ALU.mult),
                             reads=[Bfh[k], Blohi[qs]], writes=BIc)
                    else:
                        S.op("dve", lambda e, k=k, c0=c0, w=w, qs=qs, h=h: e.scalar_tensor_tensor(
                            out=Iacc[:, c0:c0 + w], in0=fh[k][:, 0:w], scalar=lohi[qs][:, h:h + 1], in1=Iacc[:, c0:c0 + w],
                            op0=ALU.mult, op1=ALU.add), reads=[Bfh[k], Blohi[qs]] + BIc, writes=BIc)
                S.op("dve", lambda e, c0=c0, w=w, i=i: e.scalar_tensor_tensor(out=Iacc[:, c0:c0 + w], in0=iotaN[:, c0:c0 + w],
                                                                              scalar=cq[:, i:i + 1], in1=Iacc[:, c0:c0 + w],
                                                                              op0=ALU.subtract, op1=ALU.min),
                     reads=[Biota, Bcq] + BIc, writes=BIc)
            S.op("dve", lambda e: e.memset(cand[:], 0.0), writes=[Bcand])
            for b in range(NBIS):
                step = float(BIS_B * 2.0 / (2 ** (b + 1)))
                nstep = step / 2.0
                S.op("dve", lambda e, h1=h1: e.tensor_scalar(out=Mb[:, 0:h1], in0=Iacc[:, 0:h1], scalar1=cand[:, 0:1], scalar2=None,
                                                      op0=ALU.is_ge, op1=ALU.add, accum_out=cntt[:, 0:1]),
                     reads=BI[:HB] + [Bcand], writes=[BMall, Bcnt])
                S.op("act", lambda e, h1=h1, nkeys=nkeys: e.activation(out=Mb[:, h1:nkeys], in_=Iacc[:, h1:nkeys], func=AF.Sign, scale=-1.0, bias=cand[:, 0:1],
                                                   accum_out=ssum[:, 0:1]),
                     reads=BI[HB:nkc] + [Bcand], writes=[BMall2, Bssum])
                S.op("dve", lambda e: e.scalar_tensor_tensor(out=ind[:], in0=ssum[:], scalar=-0.5, in1=cntt[:], op0=ALU.mult, op1=ALU.add),
                     reads=[Bssum, Bcnt], writes=[Bind])
                if b < NBIS - 1:
                    S.op("dve", lambda e, nstep=nstep, cthr=cthr: e.tensor_scalar(out=ind[:], in0=ind[:], scalar1=cthr, scalar2=2.0 * nstep,
                                                                       op0=ALU.is_ge, op1=ALU.mult), reads=[Bind], writes=[Bind])
                    S.op("dve", lambda e, nstep=nstep: e.scalar_tensor_tensor(out=cand[:], in0=ind[:], scalar=-nstep, in1=cand[:],
                                                                              op0=ALU.add, op1=ALU.add), reads=[Bind, Bcand], writes=[Bcand])
                else:
                    S.op("dve", lambda e, step=step, cthr=cthr: e.tensor_scalar(out=ind[:], in0=ind[:], scalar1=cthr, scalar2=step,
                                                                     op0=ALU.is_ge, op1=ALU.mult), reads=[Bind], writes=[Bind])
                    S.op("dve", lambda e, step=step: e.scalar_tensor_tensor(out=lo_[:], in0=ind[:], scalar=-step, in1=cand[:],
                                                                            op0=ALU.add, op1=ALU.add), reads=[Bind, Bcand], writes=[Blo])
            for gq in range(NG):
                ce = min((gq + 1) * 1024, nkeys)
                S.op("dve", lambda e, gq=gq, ce=ce: e.tensor_scalar(out=Mb[:, gq * 1024:ce], in0=Iacc[:, gq * 1024:ce],
                                                                    scalar1=lo_[:, 0:1], scalar2=None, op0=ALU.is_ge),
                     reads=BI[2 * gq:min(2 * gq + 2, nkc)] + [Blo, BMall, BMall2], writes=[BM[gq]])

            def load_group(gq):
                vk = gq % 2
                S.dma("sp", lambda e, gq=gq, vk=vk: e.dma_start(out=vbs[vk][:], in_=C.vb_d[gq * 8:(gq + 1) * 8].rearrange("b p c -> p b c")),
                      writes=[Bvbs[vk]], owner=Bvbs[vk])
                for j in range(min(8, nkb - gq * 8)):
                    n = gq * 8 + j
                    S.op("pe", lambda e, j=j, n=n, vk=vk: e.transpose(out=mt_ps[vk][:, j * 128:(j + 1) * 128], in_=Mb[:, n * 128:(n + 1) * 128],
                                                                      identity=C.ident[:]),
                         reads=[BM[gq], C.Bident], writes=[Bmtps[vk]])
                nj = min(8, nkb - gq * 8)
                S.op("act", lambda e, vk=vk, nj=nj: e.activation(out=mts[vk][:, 0:nj * 128], in_=mt_ps[vk][:, 0:nj * 128], func=AF.Copy),
                     reads=[Bmtps[vk]], writes=[Bmts[vk]])

            def qk_B(n, sidx, qs=qs):
                for h in range(8):
                    rows = slice((h % 2) * 64, (h % 2) * 64 + 64)
                    pr = h // 2
                    S.op("pe", lambda e, h=h, rows=rows, pr=pr: e.matmul(
                        scst[sidx][h % 2][:, pr * 128:(pr + 1) * 128], lhsT=C.kbT[rows, n, pr * 128:(pr + 1) * 128],
                        rhs=qbT[qs][rows, pr * 128:(pr + 1) * 128], start=True, stop=True),
                        reads=[C.BkbT[n], BqbT[qs]], writes=[Bscst[sidx][h % 2]])

            load_group(0)
            qk_B(0, 0)
            qk_B(1, 1)
            for n in range(nkb):
                gq, j = n // 8, n % 8
                vk = gq % 2
                sidx = n % 2
                if j == 0 and gq + 1 < NG:
                    load_group(gq + 1)
                S.op("act", lambda e, sidx=sidx: e.activation(out=pe_[sidx][:, :], in_=PB[sidx][:, :], func=AF.Exp, scale=0.125),
                     reads=[Bscst[sidx][0], Bscst[sidx][1]], writes=[Bpe[sidx]])
                if n + 2 < nkb:
                    qk_B(n + 2, sidx)
                S.op("dve", lambda e, sidx=sidx, vk=vk, j=j: e.tensor_tensor(
                    out=pm[sidx][:].rearrange("p (h q) -> p h q", q=128), in0=pe_[sidx][:].rearrange("p (h q) -> p h q", q=128),
                    in1=mts[vk][:, j * 128:(j + 1) * 128].unsqueeze(1).to_broadcast([128, 8, 128]), op=ALU.mult),
                    reads=[Bpe[sidx], Bmts[vk]], writes=[Bpm[sidx]])
                for h in range(8):
                    gi = h // 4
                    cc = (h % 4) * 65
                    pcol = (h % 2) * 512 + (h // 2) * 128
                    S.op("pe", lambda e, h=h, gi=gi, cc=cc, pcol=pcol, sidx=sidx, vk=vk, j=j, n=n: e.matmul(
                        acc[gi][:, cc:cc + 65], lhsT=pm[sidx][:, pcol:pcol + 128], rhs=vbs[vk][:, j, h * 65:(h + 1) * 65],
                        start=(n == 0 and h % 4 == 0), stop=(n == nkb - 1), skip_group_check=True),
                        reads=[Bpm[sidx], Bvbs[vk]], writes=[Bacc[gi]])
            a = i % 2
            for gi in range(2):
                accv = acc[gi][:, 0:260].rearrange("p (h c) -> p h c", c=65)
                S.op("dve", lambda e, a=a, gi=gi, accv=accv: e.tensor_scalar(out=rd[a][:, gi * 4:(gi + 1) * 4].unsqueeze(2), in0=accv[:, :, 64:65],
                                                                             scalar1=1e-30, scalar2=None, op0=ALU.max),
                     reads=[Bacc[gi]], writes=[Brd[a]])
                S.op("dve", lambda e, a=a, gi=gi: e.reciprocal(out=rd[a][:, gi * 4:(gi + 1) * 4], in_=rd[a][:, gi * 4:(gi + 1) * 4]),
                     reads=[Brd[a]], writes=[Brd[a]])
                S.op("dve", lambda e, a=a, gi=gi, accv=accv: e.tensor_tensor(
                    out=mxb[a][:, gi * 256:(gi + 1) * 256].rearrange("p (h d) -> p h d", d=64), in0=accv[:, :, 0:64],
                    in1=rd[a][:, gi * 4:(gi + 1) * 4].unsqueeze(2).to_broadcast([128, 4, 64]), op=ALU.mult),
                    reads=[Bacc[gi], Brd[a]], writes=[Bmxb[a]])
            S.dma("sp", lambda e, a=a, i=i: e.dma_start(out=C.mix_d[i][:, 512:1024], in_=mxb[a][:]), reads=[Bmxb[a]], writes=[Buf()], owner=Bmxb[a])
        S.emit_phase()
        S.release(BqbT + BqiT + Blohi + Bvbs + Bmxb)
    if DBG and STOP_AFTER == "B":
        dump_mix(C)


def rms_to_bf16(S, xin_ap, Bx, sqj, Bsq, ss, Bss, rstd, Brs, gam, Bgam, out_ap, Bout):
    S.op("act", lambda e: e.activation(out=sqj[:], in_=xin_ap, func=AF.Square, accum_out=ss[:, 0:1]), reads=[Bx], writes=[Bsq, Bss])
    S.op("act", lambda e: e.activation(out=rstd[:], in_=ss[:], func=AF.Sqrt, scale=1.0 / D, bias=RMS_EPS), reads=[Bss], writes=[Brs])
    S.op("dve", lambda e: e.reciprocal(out=rstd[:], in_=rstd[:]), reads=[Brs], writes=[Brs])
    S.op("dve", lambda e: e.scalar_tensor_tensor(out=out_ap, in0=xin_ap, scalar=rstd[:, 0:1], in1=gam[:], op0=ALU.mult, op1=ALU.mult),
         reads=[Bx, Brs, Bgam], writes=[Bout])


def phaseC1(C):
    nc, S = C.nc, C.S
    TT = 256
    NT = OWN // TT
    with ExitStack() as es:
        sb = lambda n, s, d: es.enter_context(nc.sbuf_tensor(n, s, d))
        pst = lambda n, s, d: es.enter_context(nc.psum_tensor(n, s, d))
        wo = sb("wo", [128, 8, D], BF16)
        wup = sb("wup", [128, 8, 2 * D_FF], BF16)
        wdn = sb("wdn", [128, 22, D], BF16)
        gff = sb("gff", [128, D], F32)
        cw = sb("cw", [128, 4, NCH], F32)
        carry = sb("carry", [128, NCH, 2], F32)
        mixt = [sb("mixt0", [128, D], BF16)] * 2
        mixT = [sb("mixT0", [128, 8, 128], BF16)] * 2
        xio = [sb(f"xio{i}", [128, 2, D], F32) for i in range(2)]
        cwl = xio[0][0:NCH, 0, 0:512].rearrange("c (j p) -> c j p", j=4)
        ss = [sb(f"ssC{i}", [128, 1], F32) for i in range(2)]
        rstd = [sb(f"rstdC{i}", [128, 1], F32) for i in range(2)]
        h2 = [sb(f"h2{i}", [128, D], BF16) for i in range(2)]
        h2T = [sb(f"h2T{i}", [128, 8, TT], BF16) for i in range(2)]
        U = [sb(f"U{i}", [128, TT + 2], F32) for i in range(4)]
        y = [sb(f"y{i}", [128, TT], F32) for i in range(5)]
        sg = [sb("sg0", [128, TT], F32)] * 2
        aT = sb("aT", [128, 22, TT], BF16)
        tp_ps = [pst(f"tpps{i}", [128, 1024], BF16) for i in range(2)]
        wo_ps = [pst(f"wops{i}", [128, 512], F32) for i in range(2)]
        u_ps = [pst(f"ups{i}", [128, 512], F32) for i in range(4)]
        wd_ps = wo_ps

        Bwo, Bwup, Bwdn, Bgff, Bcw, Bsq = (Buf() for _ in range(6))
        Bcarry = [Buf() for _ in range(NCH)]
        Bmixt = [Buf()] * 2
        BmixT = [Buf()] * 2
        Bxio = [[Buf() for _ in range(2)] for _ in range(2)]
        Bcwl = Bxio[0][0]
        Bss = [Buf() for _ in range(2)]
        Brs = [Buf() for _ in range(2)]
        Bh2 = [Buf() for _ in range(2)]
        Bh2T = [[Buf() for _ in range(2)] for _ in range(2)]
        BU = [Buf() for _ in range(4)]
        By = [Buf() for _ in range(5)]
        Bsg = [Buf()] * 2
        BaT = [Buf() for _ in range(22)]
        Btp = [Buf() for _ in range(2)]
        Bwops = [Buf() for _ in range(2)]
        Bups = [Buf() for _ in range(4)]
        Bwdps = Bwops

        g = S.new_group("pC1init")
        gw = S.new_group("pC1w")
        S.dma("sp", lambda e: e.dma_start(out=gff[:], in_=C.ffn_norm.partition_broadcast(128)), writes=[Bgff], group=g)
        for j in range(3):
            S.dma("sp", lambda e, j=j: e.dma_start(out=cwl[:, j, :], in_=C.conv_w[j].rearrange("(c p) -> c p", p=128)), writes=[Bcwl], group=g)
        S.dma("sp", lambda e: e.dma_start(out=cwl[:, 3, :], in_=C.conv_b[0].rearrange("(c p) -> c p", p=128)), writes=[Bcwl], group=g)
        gwu = S.new_group("pC1wu")
        gwd = S.new_group("pC1wd")
        S.dma("sp", lambda e: e.dma_start(out=wo[:], in_=C.wob_d.rearrange("(m p) c -> p m c", p=128)), writes=[Bwo], group=gw)
        for m in range(8):
            S.dma("sp", lambda e, m=m: e.dma_start(out=wup[:, m, :], in_=C.wupb_d[m * 128:(m + 1) * 128, :]), writes=[Bwup], group=gwu)
        for k0 in range(0, 22, 11):
            S.dma("sp", lambda e, k0=k0: e.dma_start(out=wdn[:, k0:k0 + 11, :],
                                                     in_=C.wdnb_d[k0 * 128:(k0 + 11) * 128, :].rearrange("(m p) c -> p m c", p=128)),
                  writes=[Bwdn], group=gwd)
        for j in range(4):
            S.op("pe", lambda e, j=j: e.transpose(out=wo_ps[0][:, j * NCH:(j + 1) * NCH], in_=cwl[:, j, :], identity=C.identf[0:NCH, 0:NCH]),
                 reads=[Bcwl, C.Bident], writes=[Bwops[0]])
        S.op("dve", lambda e: e.tensor_copy(out=cw[:].rearrange("p j c -> p (j c)"), in_=wo_ps[0][:, 0:4 * NCH]), reads=[Bwops[0]], writes=[Bcw])

        cnt = {"tp": 0, "u": 0, "U": 0, "y": 0}

        def load_xrow(xa_blk, xdst, Bxdst):
            S.dma("sp", lambda e: e.dma_start(out=xdst, in_=C.xa[xa_blk * 128:(xa_blk + 1) * 128, :]), writes=[Bxdst], owner=Bxdst)

        def attn_out_block(mix_idx, xa_blk, xdst, Bxdst, slot):
            S.dma("sp", lambda e: e.dma_start(out=mixt[slot][:], in_=C.mix_d[mix_idx]), writes=[Bmixt[slot]], owner=Bmixt[slot])
            q = cnt["tp"] % 2
            cnt["tp"] += 1
            for m in range(8):
                S.op("pe", lambda e, m=m, q=q: e.transpose(out=tp_ps[q][:, m * 128:(m + 1) * 128], in_=mixt[slot][:, m * 128:(m + 1) * 128],
                                                           identity=C.ident[:]), reads=[Bmixt[slot], C.Bident], writes=[Btp[q]])
            S.op("act", lambda e, q=q: e.activation(out=mixT[slot][:].rearrange("p m t -> p (m t)"), in_=tp_ps[q][:], func=AF.Copy),
                 reads=[Btp[q]], writes=[BmixT[slot]])
            for hf in range(2):
                for m in range(8):
                    S.op("pe", lambda e, m=m, hf=hf: e.matmul(wo_ps[hf][:, :], lhsT=mixT[slot][:, m, :], rhs=wo[:, m, hf * 512:(hf + 1) * 512],
                                                              start=(m == 0), stop=(m == 7)), reads=[BmixT[slot], Bwo], writes=[Bwops[hf]])
                S.op("dve", lambda e, hf=hf: e.tensor_tensor(out=xdst[:, hf * 512:(hf + 1) * 512], in0=xdst[:, hf * 512:(hf + 1) * 512],
                                                             in1=wo_ps[hf][:, :], op=ALU.add), reads=[Bxdst, Bwops[hf]], writes=[Bxdst])

        def norm_T(xsrc, Bxsrc, slot, dstT_fn, BdstT):
            rms_to_bf16(S, xsrc, Bxsrc, h2[slot], Bh2[slot], ss[slot], Bss[slot], rstd[slot], Brs[slot], gff, Bgff, h2[slot][:], Bh2[slot])
            q = cnt["tp"] % 2
            cnt["tp"] += 1
            for m in range(8):
                S.op("pe", lambda e, m=m, q=q: e.transpose(out=tp_ps[q][:, m * 128:(m + 1) * 128], in_=h2[slot][:, m * 128:(m + 1) * 128],
                                                           identity=C.ident[:]), reads=[Bh2[slot], C.Bident], writes=[Btp[q]])
            S.op("act", lambda e, q=q: e.activation(out=dstT_fn(), in_=tp_ps[q][:].rearrange("p (m t) -> p m t", t=128), func=AF.Copy),
                 reads=[Btp[q]], writes=[BdstT])

        load_xrow(NB_A - NQ, xio[1][:, 1, :], Bxio[1][1])
        for blk in range(2):
            load_xrow(NB_A - NQ + 1 + blk, xio[0][:, blk, :], Bxio[0][blk])
        attn_out_block(0, NB_A - NQ, xio[1][:, 1, :], Bxio[1][1], 1)
        norm_T(xio[1][:, 1, :], Bxio[1][1], 1, lambda: h2T[1][:, :, 128:256], Bh2T[1][1])
        for cc in range(NCH):
            r = cnt["u"] % 4
            cnt["u"] += 1
            ub, uh = u_ps[r], 0
            for m in range(8):
                S.op("pe", lambda e, m=m, cc=cc, ub=ub, uh=uh: e.matmul(ub[:, uh * 256:uh * 256 + 2], lhsT=wup[:, m, cc * 128:(cc + 1) * 128],
                                                                        rhs=h2T[1][:, m, 254:256], start=(m == 0), stop=(m == 7)),
                     reads=[Bh2T[1][1], Bwup], writes=[Bups[r]])
            S.op("dve", lambda e, cc=cc, ub=ub, uh=uh: e.tensor_copy(out=carry[:, cc, :], in_=ub[:, uh * 256:uh * 256 + 2]),
                 reads=[Bups[r]], writes=[Bcarry[cc]])

        def prologue(t):
            xs = t % 2
            for blk in range(2):
                tb = 2 * t + blk
                attn_out_block(tb + 1, NB_A - NQ + 1 + tb, xio[xs][:, blk, :], Bxio[xs][blk], blk)
                norm_T(xio[xs][:, blk, :], Bxio[xs][blk], blk, lambda xs=xs, blk=blk: h2T[xs][:, :, blk * 128:(blk + 1) * 128], Bh2T[xs][blk])

        prologue(0)
        for t in range(NT):
            xs = t % 2
            if t + 1 < NT:
                for blk in range(2):
                    load_xrow(NB_A - NQ + 1 + 2 * (t + 1) + blk, xio[1 - xs][:, blk, :], Bxio[1 - xs][blk])
            for k in range(22):
                ys = []
                pair = []
                for cc in (k, k + 22):
                    r = cnt["u"] % 4
                    cnt["u"] += 1
                    ub = u_ps[r]
                    for m in range(8):
                        S.op("pe", lambda e, m=m, cc=cc, ub=ub, xs=xs: e.matmul(
                            ub[:, 0:TT], lhsT=wup[:, m, cc * 128:(cc + 1) * 128], rhs=h2T[xs][:, m, :],
                            start=(m == 0), stop=(m == 7)), reads=[Bh2T[xs][0], Bh2T[xs][1], Bwup], writes=[Bups[r]])
                    ui = cnt["U"] % 4
                    cnt["U"] += 1
                    yi = cnt["y"] % 5
                    cnt["y"] += 1
                    pair.append((cc, r, ub, ui, yi))
                    ys.append(yi)
                for cc, r, ub, ui, yi in pair:
                    S.op("pool", lambda e, ui=ui, cc=cc: e.tensor_copy(out=U[ui][:, 0:2], in_=carry[:, cc, :]), reads=[Bcarry[cc]], writes=[BU[ui]])
                for cc, r, ub, ui, yi in pair:
                    S.op("act", lambda e, ui=ui, ub=ub: e.activation(out=U[ui][:, 2:TT + 2], in_=ub[:, 0:TT], func=AF.Copy),
                         reads=[Bups[r]], writes=[BU[ui]])
                for cc, r, ub, ui, yi in pair:
                    S.op("pool", lambda e, ui=ui, cc=cc: e.tensor_copy(out=carry[:, cc, :], in_=U[ui][:, TT:TT + 2]), reads=[BU[ui]],
                         writes=[Bcarry[cc]])
                for pi, (cc, r, ub, ui, yi) in enumerate(pair):
                    if pi == 0:
                        S.op("act", lambda e, ui=ui, yi=yi, cc=cc: e.activation(out=y[yi][:], in_=U[ui][:, 2:TT + 2], func=AF.Identity,
                                                                                scale=cw[:, 2, cc:cc + 1], bias=cw[:, 3, cc:cc + 1]),
                             reads=[BU[ui], Bcw], writes=[By[yi]])
                    else:
                        S.op("pool", lambda e, ui=ui, yi=yi, cc=cc: e.tensor_scalar(out=y[yi][:], in0=U[ui][:, 2:TT + 2], scalar1=cw[:, 2, cc:cc + 1],
                                                                                    scalar2=cw[:, 3, cc:cc + 1], op0=ALU.mult, op1=ALU.add),
                             reads=[BU[ui], Bcw], writes=[By[yi]])
                for cc, r, ub, ui, yi in pair:
                    S.op("dve", lambda e, ui=ui, yi=yi, cc=cc: e.scalar_tensor_tensor(out=y[yi][:], in0=U[ui][:, 1:TT + 1], scalar=cw[:, 1, cc:cc + 1],
                                                                                      in1=y[yi][:], op0=ALU.mult, op1=ALU.add),
                         reads=[BU[ui], Bcw, By[yi]], writes=[By[yi]])
                for cc, r, ub, ui, yi in pair:
                    S.op("dve", lambda e, ui=ui, yi=yi, cc=cc: e.scalar_tensor_tensor(out=y[yi][:], in0=U[ui][:, 0:TT], scalar=cw[:, 0, cc:cc + 1],
                                                                                      in1=y[yi][:], op0=ALU.mult, op1=ALU.add),
                         reads=[BU[ui], Bcw, By[yi]], writes=[By[yi]])
                si = k % 2
                S.op("act", lambda e, si=si, yg=ys[0]: e.activation(out=sg[si][:], in_=y[yg][:], func=AF.Silu), reads=[By[ys[0]]], writes=[Bsg[si]])
                S.op("dve", lambda e, si=si, yu=ys[1], k=k: e.tensor_tensor(out=aT[:, k, :], in0=sg[si][:], in1=y[yu][:], op=ALU.mult),
                     reads=[Bsg[si], By[ys[1]]], writes=[BaT[k]])
            if t + 1 < NT:
                prologue(t + 1)
            for blk in range(2):
                tb = 2 * t + blk
                for hf in range(2):
                    for k in range(22):
                        S.op("pe", lambda e, k=k, hf=hf, blk=blk: e.matmul(wd_ps[hf][:, :], lhsT=aT[:, k, blk * 128:(blk + 1) * 128],
                                                                           rhs=wdn[:, k, hf * 512:(hf + 1) * 512], start=(k == 0), stop=(k == 21)),
                             reads=[BaT[k], Bwdn], writes=[Bwdps[hf]])
                    S.op("dve", lambda e, hf=hf, blk=blk, xs=xs: e.tensor_tensor(out=xio[xs][:, blk, hf * 512:(hf + 1) * 512],
                                                                                 in0=xio[xs][:, blk, hf * 512:(hf + 1) * 512],
                                                                                 in1=wd_ps[hf][:, :], op=ALU.add),
                         reads=[Bxio[xs][blk], Bwdps[hf]], writes=[Bxio[xs][blk]])
                S.dma("sp", lambda e, tb=tb, xs=xs, blk=blk: e.dma_start(out=C.x2_d[tb * 128:(tb + 1) * 128, :], in_=xio[xs][:, blk, :]),
                      reads=[Bxio[xs][blk]], writes=[Buf()], owner=Bxio[xs][blk])
        S.emit_phase()
        S.release([Bmixt[0]] + Bxio[0] + Bxio[1])
    if DBG:
        with nc.sbuf_tensor("dx0", [128, D], F32) as dx0, nc.sbuf_tensor("dx1", [128, D], F32) as dx1:
            dx = [dx0, dx1]
            Bd = [Buf(), Buf()]
            for i in range(OWN // 128):
                k = i % 2
                S.dma("sp", lambda e, i=i, k=k: e.dma_start(out=dx[k][:], in_=C.x2_d[i * 128:(i + 1) * 128, :]), writes=[Bd[k]], owner=Bd[k])
                S.dma("sp", lambda e, i=i, k=k: e.dma_start(out=C.dbg["x2"][i * 128:(i + 1) * 128, :], in_=dx[k][:]), reads=[Bd[k]], writes=[Buf()],
                      owner=Bd[k])
            S.emit_phase()
            S.release(Bd)


def phaseC2(C):
    nc, S = C.nc, C.S
    NBK = OWN // 128
    with ExitStack() as es:
        sb = lambda n, s, d: es.enter_context(nc.sbuf_tensor(n, s, d))
        pst = lambda n, s, d: es.enter_context(nc.psum_tensor(n, s, d))
        wg = sb("wg", [128, 8, D], BF16)
        wp = sb("wp", [128, 2, D], BF16)
        gpl = sb("gpl", [128, D], F32)
        gfin = sb("gfin", [128, D], F32)
        x2t = [sb(f"x2t{i}", [128, D], F32) for i in range(2)]
        pt = [sb(f"pt{i}", [128, 256], F32) for i in range(2)]
        pb = [sb(f"pb{i}", [128, 256], BF16) for i in range(2)]
        pT = [sb(f"pT{i}", [128, 2, 128], BF16) for i in range(2)]
        sqj = sb("sqjD", [128, D], BF16)
        sqj2 = sb("sqjD2", [128, D], BF16)
        Bsq2 = Buf()
        ss = [sb(f"ssD{i}", [128, 1], F32) for i in range(2)]
        rstd = [sb(f"rstdD{i}", [128, 1], F32) for i in range(2)]
        ss2 = [sb(f"ssE{i}", [128, 1], F32) for i in range(2)]
        rstd2 = [sb(f"rstdE{i}", [128, 1], F32) for i in range(2)]
        h3 = [sb(f"h3{i}", [128, D], BF16) for i in range(2)]
        h3T = [sb(f"h3T{i}", [128, 8, 128], BF16) for i in range(2)]
        gate = [sb(f"gate{i}", [128, D], F32) for i in range(2)]
        x3 = [sb(f"x3{i}", [128, D], F32) for i in range(2)]
        ot = [sb(f"ot{i}", [128, D], F32) for i in range(2)]
        tp_ps = pst("tpD", [128, 1024], BF16)
        pT_ps = pst("pTD", [128, 1024], BF16)
        g_ps = [pst(f"gps{i}", [128, 512], F32) for i in range(2)]
        pp_ps = [pst(f"ppps{i}", [128, 512], F32) for i in range(2)]
        Bwg, Bwp, Bgpl, Bgfin, Bsq, Btp, BpTps = (Buf() for _ in range(7))
        mk = lambda: [Buf() for _ in range(2)]
        Bx2t, Bpt, Bpb, BpT, Bss, Brs, Bss2, Brs2, Bh3, Bh3T, Bgate, Bx3, Bot, Bgps, Bppps = (mk() for _ in range(15))
        g = S.new_group("pC2init")
        gw = S.new_group("pC2w")
        S.dma("sp", lambda e: e.dma_start(out=gpl[:], in_=C.ple_norm.partition_broadcast(128)), writes=[Bgpl], group=g)
        S.dma("sp", lambda e: e.dma_start(out=gfin[:], in_=C.final_norm.partition_broadcast(128)), writes=[Bgfin], group=g)
        S.dma("sp", lambda e: e.dma_start(out=wg[:], in_=C.wgb_d.rearrange("(m p) c -> p m c", p=128)), writes=[Bwg], group=gw)
        S.dma("sp", lambda e: e.dma_start(out=wp[:], in_=C.wpb_d.rearrange("(m p) c -> p m c", p=128)), writes=[Bwp], group=gw)
        def load_c2(tb):
            s = tb % 2
            S.dma("sp", lambda e: e.dma_start(out=x2t[s][:], in_=C.x2_d[tb * 128:(tb + 1) * 128, :]), writes=[Bx2t[s]], owner=Bx2t[s])
            S.dma("sp", lambda e: e.dma_start(out=pt[s][:], in_=C.p_own[tb * 128:(tb + 1) * 128, :]), writes=[Bpt[s]], owner=Bpt[s])

        def x_norm(tb):
            s = tb % 2
            rms_to_bf16(S, x2t[s][:], Bx2t[s], sqj, Bsq, ss[s], Bss[s], rstd[s], Brs[s], gpl, Bgpl, h3[s][:], Bh3[s])
            S.op("pool", lambda e, s=s: e.tensor_copy(out=pb[s][:], in_=pt[s][:]), reads=[Bpt[s]], writes=[Bpb[s]])

        def x_transposes(tb):
            s = tb % 2
            for m in range(8):
                S.op("pe", lambda e, m=m, s=s: e.transpose(out=tp_ps[:, m * 128:(m + 1) * 128], in_=h3[s][:, m * 128:(m + 1) * 128],
                                                           identity=C.ident[:]), reads=[Bh3[s], C.Bident], writes=[Btp])
            for m in range(2):
                S.op("pe", lambda e, m=m, s=s: e.transpose(out=pT_ps[:, m * 128:(m + 1) * 128], in_=pb[s][:, m * 128:(m + 1) * 128],
                                                           identity=C.ident[:]), reads=[Bpb[s], C.Bident], writes=[BpTps])

        def x_copies(tb):
            s = tb % 2
            S.op("act", lambda e, s=s: e.activation(out=h3T[s][:].rearrange("p m t -> p (m t)"), in_=tp_ps[:], func=AF.Copy),
                 reads=[Btp], writes=[Bh3T[s]])
            S.op("act", lambda e, s=s: e.activation(out=pT[s][:].rearrange("p m t -> p (m t)"), in_=pT_ps[:, 0:256], func=AF.Copy),
                 reads=[BpTps], writes=[BpT[s]])

        load_c2(0)
        load_c2(1)
        x_norm(0)
        x_transposes(0)
        x_copies(0)
        for tb in range(NBK):
            s = tb % 2
            nxt = tb + 1 < NBK
            for hf in range(2):
                for m in range(8):
                    S.op("pe", lambda e, m=m, hf=hf, s=s: e.matmul(g_ps[hf][:, :], lhsT=h3T[s][:, m, :], rhs=wg[:, m, hf * 512:(hf + 1) * 512],
                                                                   start=(m == 0), stop=(m == 7)), reads=[Bh3T[s], Bwg], writes=[Bgps[hf]])
                for m in range(2):
                    S.op("pe", lambda e, m=m, hf=hf, s=s: e.matmul(pp_ps[hf][:, :], lhsT=pT[s][:, m, :], rhs=wp[:, m, hf * 512:(hf + 1) * 512],
                                                                   start=(m == 0), stop=(m == 1)), reads=[BpT[s], Bwp], writes=[Bppps[hf]])
            if nxt:
                x_norm(tb + 1)
                x_transposes(tb + 1)
            for hf in range(2):
                S.op("act", lambda e, hf=hf, s=s: e.activation(out=gate[s][:, hf * 512:(hf + 1) * 512], in_=g_ps[hf][:, :], func=AF.Sigmoid),
                     reads=[Bgps[hf]], writes=[Bgate[s]])
                S.op("dve", lambda e, hf=hf, s=s: e.tensor_tensor(out=gate[s][:, hf * 512:(hf + 1) * 512], in0=gate[s][:, hf * 512:(hf + 1) * 512],
                                                                  in1=pp_ps[hf][:, :], op=ALU.mult), reads=[Bgate[s], Bppps[hf]], writes=[Bgate[s]])
            if nxt:
                x_copies(tb + 1)
            S.op("dve", lambda e, s=s: e.tensor_tensor(out=x3[s][:], in0=gate[s][:], in1=x2t[s][:], op=ALU.add),
                 reads=[Bgate[s], Bx2t[s]], writes=[Bx3[s]])
            if tb + 2 < NBK:
                load_c2(tb + 2)
            S.op("act", lambda e, s=s: e.activation(out=sqj2[:], in_=x3[s][:], func=AF.Square, accum_out=ss2[s][:, 0:1]),
                 reads=[Bx3[s]], writes=[Bsq2, Bss2[s]])
            S.op("act", lambda e, s=s: e.activation(out=rstd2[s][:], in_=ss2[s][:], func=AF.Sqrt, scale=1.0 / D, bias=RMS_EPS),
                 reads=[Bss2[s]], writes=[Brs2[s]])
            S.op("dve", lambda e, s=s: e.reciprocal(out=rstd2[s][:], in_=rstd2[s][:]), reads=[Brs2[s]], writes=[Brs2[s]])
            S.op("dve", lambda e, s=s: e.scalar_tensor_tensor(out=ot[s][:], in0=x3[s][:], scalar=rstd2[s][:, 0:1], in1=gfin[:],
                                                              op0=ALU.mult, op1=ALU.mult), reads=[Bx3[s], Brs2[s], Bgfin], writes=[Bot[s]])
            S.dma("sp", lambda e, tb=tb, s=s: e.dma_start(out=C.out[tb * 128:(tb + 1) * 128, :], in_=ot[s][:]), reads=[Bot[s]], writes=[Buf()],
                  owner=Bot[s])
        S.emit_phase()


def _amask_const():
    s = np.arange(128)[:, None]
    qq = np.arange(128)[None, :]
    out = np.zeros((128, NQ, 128), np.float32)
    for r in range(NQ):
        delta = (16 - r) * 128 + qq - s
        m = ((delta >= 0) & (delta <= 128)).astype(np.float32)
        m += ((delta >= 0) & (delta <= 512) & (delta % 4 == 0)).astype(np.float32)
        m += ((delta >= 0) & (delta <= 2048) & (delta % 16 == 0)).astype(np.float32)
        out[:, r, :] = m
    return out.reshape(128, NQ * 128)


def make_in_maps(x, p, positions, attn_norm, w_in, w_o, ffn_norm, w_up, conv_w, conv_b, w_down, ple_norm,
                 w_ple_gate, w_ple_proj, final_norm):
    f32 = lambda a: np.ascontiguousarray(np.asarray(a, dtype=np.float32))
    x = f32(x)
    p = f32(p)
    positions = np.asarray(positions).astype(np.int32)
    invf = (1.0 / (10000.0 ** (np.arange(0, 64, 2, dtype=np.float32) / np.float32(64)))).astype(np.float32)[None]
    amask = _amask_const()
    shared = dict(invf=invf, amask=amask, w_in=f32(w_in[0]), w_o=f32(w_o[0]), w_up=f32(w_up[0]), w_down=f32(w_down[0]),
                  w_g=f32(w_ple_gate[0]), w_p=f32(w_ple_proj[0]), attn_norm=f32(attn_norm[0:1]), ffn_norm=f32(ffn_norm[0:1]),
                  ple_norm=f32(ple_norm[0:1]), final_norm=f32(final_norm[None]), conv_w=f32(conv_w[0]), conv_b=f32(conv_b[0:1]))
    maps = []
    for c in range(NCORE):
        b, q = c // 4, c % 4
        t_lo = OWN * q - (NB_A - 16) * 128 + 0
        t_lo = OWN * q - 2176
        idx = np.arange(t_lo, t_lo + NB_A * 128)
        valid = idx >= 0
        xa = np.zeros((NB_A * 128, D), np.float32)
        xa[valid] = x[b, idx[valid]]
        posa = np.zeros(NB_A * 128, np.int32)
        posa[valid] = positions[b, idx[valid]]
        pos = np.concatenate([positions[b].reshape(NB_ALL, 128).T, posa.reshape(NB_A, 128).T], axis=1)
        avalid = valid.astype(np.float32).reshape(NB_A, 128).T
        tqi = idx[(NB_A - NQ) * 128:].astype(np.float32)
        tqi = np.where(tqi >= 0, tqi, -1.0).reshape(NQ, 128).T
        m = dict(shared)
        m.update(xall=x[b], xa=xa, pos=np.ascontiguousarray(pos), avalid=np.ascontiguousarray(avalid),
                 tq=np.ascontiguousarray(tqi.astype(np.float32)), p_own=np.ascontiguousarray(p[0, b, OWN * q:OWN * (q + 1)]))
        maps.append(m)
    return maps


_NC_CACHE = {}


def kernel(**inputs):
    if "nc" not in _NC_CACHE:
        _NC_CACHE["nc"] = build_program()
    nc = _NC_CACHE["nc"]
    in_maps = make_in_maps(**inputs)
    res = run_bass_kernel_spmd(nc, in_maps, core_ids=list(range(NCORE)))
    outs = [r["out"] for r in res.results]
    full = np.stack(outs, 0).reshape(2, T, D).astype(np.float32)
    if DBG:
        kernel.last = res.results
    return full
```

```python
import os
from contextlib import ExitStack

import numpy as np
import concourse.bass as bass
import concourse.mybir as mybir
from concourse.bass_utils import run_bass_kernel_spmd

F32 = mybir.dt.float32
BF16 = mybir.dt.bfloat16
I32 = mybir.dt.int32
AF = mybir.ActivationFunctionType
ALU = mybir.AluOpType

ENGS = ("pe", "act", "dve", "pool", "sp")


class Buf:
    __slots__ = ("name", "last_w", "readers", "sem", "semcnt")

    def __init__(self, name=""):
        self.name = name
        self.last_w = None
        self.readers = []
        self.sem = None
        self.semcnt = 0


class Op:
    __slots__ = ("eng", "fn", "deps", "is_dma", "owner", "needs_inc", "semval", "phase", "group", "owner_sem")

    def __init__(self, eng, fn):
        self.eng = eng
        self.fn = fn
        self.deps = []
        self.is_dma = False
        self.owner = None
        self.needs_inc = False
        self.semval = None
        self.group = None
        self.phase = 0


class Group:
    def __init__(self, name):
        self.name = name
        self.sem = None
        self.base = 0
        self.n = 0


class Sched:
    def __init__(self, nc, sems):
        self.nc = nc
        self.free_sems = list(sems)
        self.ops = []
        self.eng_sem = {}
        self.eng_cnt = {}
        for e in ("pe", "act", "dve", "pool"):
            self.eng_sem[e] = self.free_sems.pop()
            self.eng_cnt[e] = 0
        self.dma_bufs = []
        self.groups = []
        self.phase = 0
        self.free_dma = []

    def _track(self, op, reads, writes):
        deps = []
        op.phase = self.phase
        for b in reads:
            if b.last_w is not None:
                deps.append(b.last_w)
        for b in writes:
            if b.last_w is not None:
                deps.append(b.last_w)
            for r in b.readers:
                deps.append(r)
        for b in writes:
            b.last_w = op
            b.readers = []
        for b in reads:
            if not op.is_dma:
                b.readers = [r for r in b.readers if r.is_dma or r.eng != op.eng]
            b.readers.append(op)
        seen = set()
        out = []
        for d in deps:
            if d is op or id(d) in seen or d.phase != self.phase:
                continue
            if op.is_dma and d.is_dma and op.group is not None and d.group is op.group:
                continue
            seen.add(id(d))
            out.append(d)
        op.deps = out

    def op(self, eng, fn, reads=(), writes=()):
        o = Op(eng, fn)
        self._track(o, reads, writes)
        self.ops.append(o)
        return o

    def dma(self, eng, fn, reads=(), writes=(), owner=None, group=None):
        o = Op(eng, fn)
        o.is_dma = True
        o.owner = owner
        o.group = group
        assert (owner is None) != (group is None)
        self._track(o, reads, writes)
        self.ops.append(o)
        return o

    def new_group(self, name):
        g = Group(name)
        g.sem = self.free_sems.pop()
        self.groups.append(g)
        return g

    def release(self, bufs):
        for b in bufs:
            if b.sem is not None:
                self.dma_bufs.remove(b)
                self.free_dma.append((b.sem, b.semcnt))
                b.sem = None

    def emit_phase(self):
        nc = self.nc
        ops = self.ops
        self.ops = []
        for o in ops:
            for d in o.deps:
                if d.is_dma:
                    continue
                if d.eng == o.eng and o.eng == "pe":
                    continue
                d.needs_inc = True
        for o in ops:
            if o.is_dma:
                if o.group is not None:
                    o.group.n += 1
                else:
                    b = o.owner
                    if b.sem is None:
                        if self.free_dma:
                            b.sem, b.semcnt = self.free_dma.pop()
                        else:
                            b.sem = self.free_sems.pop()
                            b.semcnt = 0
                        self.dma_bufs.append(b)
                    b.semcnt += 16
                    o.semval = b.semcnt
            elif o.needs_inc:
                self.eng_cnt[o.eng] += 1
                o.semval = self.eng_cnt[o.eng]
        per = {e: [] for e in ENGS}
        for o in ops:
            per[o.eng].append(o)
        waited = {e: {} for e in ENGS}
        final_waits = [(b.sem, b.semcnt) for b in self.dma_bufs]
        for g in self.groups:
            if g.n:
                final_waits.append((g.sem, g.base + 16 * g.n))

        def run(eng_name, eng):
            w = waited[eng_name]
            for o in per[eng_name]:
                for d in o.deps:
                    if d.is_dma:
                        if d.group is not None:
                            sem, val = d.group.sem, d.group.base + 16 * d.group.n
                        else:
                            sem, val = d.owner_sem, d.semval
                    else:
                        if d.eng == eng_name and eng_name == "pe":
                            continue
                        sem, val = self.eng_sem[d.eng], d.semval
                    key = id(sem)
                    if w.get(key, 0) >= val:
                        continue
                    w[key] = val
                    eng.wait_ge(sem, val)
                ins = o.fn(eng)
                if o.is_dma:
                    ins.then_inc(o.group.sem if o.group is not None else o.owner_sem, 16)
                elif o.needs_inc:
                    ins.then_inc(self.eng_sem[o.eng], 1)
            if eng_name == "sp":
                for sem, val in final_waits:
                    key = id(sem)
                    if w.get(key, 0) >= val:
                        continue
                    w[key] = val
                    eng.wait_ge(sem, val)

        for o in ops:
            if o.is_dma and o.group is None:
                o.owner_sem = o.owner.sem

        with nc.Block() as block:
            @block.tensor
            def _(e):
                run("pe", e)

            @block.scalar
            def _(e):
                run("act", e)

            @block.vector
            def _(e):
                run("dve", e)

            @block.gpsimd
            def _(e):
                run("pool", e)

            @block.sync
            def _(e):
                run("sp", e)
        for g in self.groups:
            g.base += 16 * g.n
            g.n = 0
        self.phase += 1


T = 8192
D = 1024
NCORE = 8
OWN = 2048
NB_ALL = 64
NB_A = 33
NQ = 17
IN_COLS = 3656
D_FF = 2816
NCH = 44
C_QA, C_KA, C_VA, C_QB, C_KB, C_VB, C_QI, C_KI, C_WI = 0, 512, 1024, 1536, 2048, 2560, 3072, 3584, 3648
RMS_EPS = 1e-6
BIG = 1.0e30
NBIS = 18
BIS_B = 16.0
DBG = os.environ.get("MK_DBG", "")
STOP_AFTER = os.environ.get("MK_STOP", "")


class Ctx:
    pass


def build_program():
    nc = bass.Bass("TRN2", target_bir_lowering=False)
    C = Ctx()
    C.nc = nc
    din = lambda name, shape, dt=F32: nc.dram_tensor(name, shape, dt, kind="ExternalInput").ap()
    C.xall = din("xall", [T, D])
    C.xa = din("xa", [NB_A * 128, D])
    C.pos = din("pos", [128, NB_ALL + NB_A], I32)
    C.avalid = din("avalid", [128, NB_A])
    C.tq = din("tq", [128, NQ])
    C.invf = din("invf", [1, 32])
    C.amask = din("amask", [128, NQ * 128])
    C.p_own = din("p_own", [OWN, 256])
    C.w_in = din("w_in", [D, IN_COLS])
    C.w_o = din("w_o", [D, D])
    C.w_up = din("w_up", [D, 2 * D_FF])
    C.w_down = din("w_down", [D_FF, D])
    C.w_g = din("w_g", [D, D])
    C.w_p = din("w_p", [256, D])
    C.attn_norm = din("attn_norm", [1, D])
    C.ffn_norm = din("ffn_norm", [1, D])
    C.ple_norm = din("ple_norm", [1, D])
    C.final_norm = din("final_norm", [1, D])
    C.conv_w = din("conv_w", [3, 2 * D_FF])
    C.conv_b = din("conv_b", [1, 2 * D_FF])
    C.out = nc.dram_tensor("out", [OWN, D], F32, kind="ExternalOutput").ap()
    C.vb_d = nc.dram_tensor("vb_d", [NB_ALL, 128, 520], BF16).ap()
    C.va_d = nc.dram_tensor("va_d", [NB_A, 128, 520], BF16).ap()
    C.kaT_d = nc.dram_tensor("kaT_d", [NB_A, 128, 512], BF16).ap()
    C.qaT_d = nc.dram_tensor("qaT_d", [NQ, 128, 512], BF16).ap()
    C.qbT_d = nc.dram_tensor("qbT_d", [NQ, 128, 512], BF16).ap()
    C.qiT_d = nc.dram_tensor("qiT_d", [NQ, 128, 512], BF16).ap()
    C.lohi_d = nc.dram_tensor("lohi_d", [NQ, 128, 16], F32).ap()
    C.mix_d = nc.dram_tensor("mix_d", [NQ, 128, 1024], BF16).ap()
    C.x2_d = nc.dram_tensor("x2_d", [OWN, D], F32).ap()
    C.wob_d = nc.dram_tensor("wob_d", [D, D], BF16).ap()
    C.wupb_d = nc.dram_tensor("wupb_d", [D, 2 * D_FF], BF16).ap()
    C.wdnb_d = nc.dram_tensor("wdnb_d", [D_FF, D], BF16).ap()
    C.wgb_d = nc.dram_tensor("wgb_d", [D, D], BF16).ap()
    C.wpb_d = nc.dram_tensor("wpb_d", [256, D], BF16).ap()
    C.dbg = {}
    if DBG:
        dout = lambda name, shape, dt=F32: nc.dram_tensor(name, shape, dt, kind="ExternalOutput").ap()
        C.dbg["kbT"] = dout("dbg_kbT", [128, NB_ALL * 512], BF16)
        C.dbg["kiT"] = dout("dbg_kiT", [128, T], BF16)
        C.dbg["mix"] = dout("dbg_mix", [NQ, 128, 1024], BF16)
        C.dbg["x2"] = dout("dbg_x2", [OWN, D])

    with ExitStack() as top:
        sems = [top.enter_context(nc.semaphore(f"s{i}")) for i in range(96)]
        S = Sched(nc, sems)
        C.S = S
        C.ident = top.enter_context(nc.sbuf_tensor("ident", [128, 128], BF16))
        C.identf = top.enter_context(nc.sbuf_tensor("identf", [128, 128], F32))
        C.Bident = Buf("ident")
        with ExitStack() as kv:
            C.kbT = kv.enter_context(nc.sbuf_tensor("kbT", [128, NB_ALL, 512], BF16))
            C.kiT = kv.enter_context(nc.sbuf_tensor("kiT", [128, T], BF16))
            C.BkbT = [Buf(f"kbT{i}") for i in range(NB_ALL)]
            C.BkiT = [Buf(f"kiT{i}") for i in range(NB_ALL)]
            phase1(C)
            if STOP_AFTER != "1":
                phaseA(C)
            if STOP_AFTER not in ("1", "A"):
                phaseB(C)
        if STOP_AFTER not in ("1", "A", "B"):
            phaseC1(C)
            if STOP_AFTER != "C1":
                phaseC2(C)
        if STOP_AFTER:
            final_dummy(C)
    return nc


def final_dummy(C):
    nc, S = C.nc, C.S
    with nc.sbuf_tensor("zz", [128, 1024], F32) as zz:
        Bz = Buf()
        S.op("dve", lambda e: e.memset(zz[:], 0.0), writes=[Bz])
        for i in range(16):
            S.dma("sp", lambda e, i=i: e.dma_start(out=C.out[i * 128:(i + 1) * 128, :], in_=zz[:]), reads=[Bz], owner=Bz)
        S.emit_phase()


def phase1(C):
    nc, S = C.nc, C.S
    NTB = NB_ALL + NB_A
    with ExitStack() as es:
        sb = lambda n, s, d: es.enter_context(nc.sbuf_tensor(n, s, d))
        pst = lambda n, s, d: es.enter_context(nc.psum_tensor(n, s, d))
        win = sb("win", [128, 8, IN_COLS], BF16)
        gat = sb("gat", [128, D], F32)
        posi = sb("posi", [128, NTB], I32)
        posf = sb("posf", [128, NTB], F32)
        avl = sb("avl", [128, NB_A], F32)
        invt = sb("invt", [128, 32], F32)
        cosT = sb("cosT", [128, NTB, 32], F32)
        sinT = sb("sinT", [128, NTB, 32], F32)
        identf = C.identf
        ones8 = sb("ones8", [128, 8], F32)
        xt = [sb(f"xt{i}", [128, D], F32) for i in range(2)]
        sqj = sb("sqj", [128, D], BF16)
        ss = [sb(f"ss{i}", [128, 1], F32) for i in range(2)]
        rstd = [sb(f"rstd{i}", [128, 1], F32) for i in range(2)]
        hb = [sb(f"hb{i}", [128, D], BF16) for i in range(2)]
        hT = [sb(f"hT{i}", [128, 8, 128], BF16) for i in range(2)]
        ta = [sb(f"ta{i}", [128, 512], F32) for i in range(2)]
        tb_ = [sb(f"tb{i}", [128, 512], F32) for i in range(2)]
        zb = [sb(f"zb{i}", [128, 512], BF16) for i in range(6)]
        zTs = [sb(f"zTs{i}", [128, 512], BF16) for i in range(3)]
        vsb = [sb(f"vsb{i}", [128, 520], BF16) for i in range(2)]
        vsa = [sb(f"vsa{i}", [128, 520], BF16) for i in range(2)]
        wsc = [sb(f"wsc{i}", [128, 8], F32) for i in range(2)]
        lohi = [sb(f"lohi{i}", [128, 16], F32) for i in range(2)]
        hT_ps = [pst(f"hTps{i}", [128, 1024], BF16) for i in range(2)]
        z_ps = [pst(f"zps{i}", [128, 512], F32) for i in range(4)]
        zT_ps = [pst(f"zTps{i}", [128, 1024], BF16) for i in range(2)]

        Bwin, Bgat, Bpos, Bavl, Binv, Bcs, Bidf, Bones = (Buf() for _ in range(8))
        Bxt = [Buf() for _ in range(2)]
        Bsq = Buf()
        Bss = [Buf() for _ in range(2)]
        Brs = [Buf() for _ in range(2)]
        Bhb = [Buf() for _ in range(2)]
        BhT = [Buf() for _ in range(2)]
        Bta = [Buf() for _ in range(2)]
        Btb = [Buf() for _ in range(2)]
        Bzb = [Buf() for _ in range(6)]
        BzTs = [Buf() for _ in range(3)]
        Bvsb = [Buf() for _ in range(2)]
        Bvsa = [Buf() for _ in range(2)]
        Bwsc = [Buf() for _ in range(2)]
        Blohi = [Buf() for _ in range(2)]
        BhTps = [Buf() for _ in range(2)]
        Bzps = [Buf() for _ in range(4)]
        BzTps = [Buf() for _ in range(2)]

        g0 = S.new_group("p1init")
        gw = S.new_group("p1w")
        S.dma("sp", lambda e: e.dma_start(out=posi[:], in_=C.pos), writes=[Bpos], group=g0)
        S.dma("sp", lambda e: e.dma_start(out=avl[:], in_=C.avalid), writes=[Bavl], group=g0)
        S.dma("sp", lambda e: e.dma_start(out=invt[:], in_=C.invf.partition_broadcast(128)), writes=[Binv], group=g0)
        S.dma("sp", lambda e: e.dma_start(out=gat[:], in_=C.attn_norm.partition_broadcast(128)), writes=[Bgat], group=g0)
        gw2 = S.new_group("p1w2")
        Bwin2 = Buf()
        for (c0, c1, grp, bw) in ((2048, IN_COLS, gw, Bwin), (0, 2048, gw2, Bwin2)):
            for m in range(8):
                S.dma("pool", lambda e, m=m, c0=c0, c1=c1: e.dma_start(out=win[:, m, c0:c1], in_=C.w_in[m * 128:(m + 1) * 128, c0:c1]),
                      writes=[bw], group=grp)
        S.op("pool", lambda e: e.memset(identf[:], 0.0), writes=[Bidf])
        S.op("pool", lambda e: e.affine_select(out=identf[:], in_=identf[:], compare_op=ALU.not_equal, fill=1.0,
                                               base=0, pattern=[[-1, 128]], channel_multiplier=1), reads=[Bidf], writes=[Bidf])
        S.op("dve", lambda e: e.tensor_copy(out=C.ident[:], in_=identf[:]), reads=[Bidf], writes=[C.Bident])
        S.op("dve", lambda e: e.memset(ones8[:], 1.0), writes=[Bones])
        for i in range(2):
            S.op("dve", lambda e, i=i: e.memset(vsb[i][:], 1.0), writes=[Bvsb[i]])
            S.op("dve", lambda e, i=i: e.memset(vsa[i][:], 1.0), writes=[Bvsa[i]])
        cnt = {"z": 0, "zT": 0, "zTs": 0, "vsb": 0, "vsa": 0, "zb": 0, "t": 0}

        def mm_group(s, c0, n):
            r = cnt["z"] % 4
            cnt["z"] += 1
            for m in range(8):
                S.op("pe", lambda e, m=m, r=r, s=s: e.matmul(z_ps[r][:, 0:n], lhsT=hT[s][:, m, :], rhs=win[:, m, c0:c0 + n],
                                                             start=(m == 0), stop=(m == 7)),
                     reads=[BhT[s], Bwin if c0 >= 2048 else Bwin2], writes=[Bzps[r]])
            return r

        def rope(r, n, tbk):
            zi = cnt["zb"] % 6
            cnt["zb"] += 1
            ti = cnt["t"] % 2
            cnt["t"] += 1
            H = n // 64
            zv = z_ps[r][:, 0:n].rearrange("p (h t d) -> p h t d", h=H, t=2)
            A4 = ta[ti][:, 0:n].rearrange("p (h t d) -> p h t d", h=H, t=2)
            B4 = tb_[ti][:, 0:n].rearrange("p (h t d) -> p h t d", h=H, t=2)
            Z4 = zb[zi][:, 0:n].rearrange("p (h t d) -> p h t d", h=H, t=2)
            cosb = cosT[:, tbk, :].unsqueeze(1).unsqueeze(1).to_broadcast([128, H, 2, 32])
            sinb = sinT[:, tbk, :].unsqueeze(1).to_broadcast([128, H, 32])
            S.op("dve", lambda e: e.tensor_tensor(out=A4, in0=zv, in1=cosb, op=ALU.mult), reads=[Bzps[r], Bcs], writes=[Bta[ti]])
            S.op("dve", lambda e: e.tensor_tensor(out=B4[:, :, 0, :], in0=zv[:, :, 1, :], in1=sinb, op=ALU.mult),
                 reads=[Bzps[r], Bcs], writes=[Btb[ti]])
            S.op("dve", lambda e: e.tensor_tensor(out=B4[:, :, 1, :], in0=zv[:, :, 0, :], in1=sinb, op=ALU.mult),
                 reads=[Bzps[r], Bcs], writes=[Btb[ti]])
            S.op("pool", lambda e: e.tensor_tensor(out=Z4[:, :, 0, :], in0=A4[:, :, 0, :], in1=B4[:, :, 0, :], op=ALU.subtract),
                 reads=[Bta[ti], Btb[ti]], writes=[Bzb[zi]])
            S.op("pool", lambda e: e.tensor_tensor(out=Z4[:, :, 1, :], in0=A4[:, :, 1, :], in1=B4[:, :, 1, :], op=ALU.add),
                 reads=[Bta[ti], Btb[ti]], writes=[Bzb[zi]])
            return zi

        pending = []

        def transposeT(zi, ncols, dst_fn, dst_bufs_w, then=None):
            def run():
                q = cnt["zT"] % 2
                cnt["zT"] += 1
                nt = ncols // 128
                for j in range(nt):
                    S.op("pe", lambda e, j=j, q=q: e.transpose(out=zT_ps[q][:, j * 128:(j + 1) * 128], in_=zb[zi][:, j * 128:(j + 1) * 128],
                                                               identity=C.ident[:]),
                         reads=[Bzb[zi], C.Bident], writes=[BzTps[q]])
                S.op("act", lambda e, q=q: e.activation(out=dst_fn(), in_=zT_ps[q][:, 0:ncols], func=AF.Copy),
                     reads=[BzTps[q]], writes=dst_bufs_w)
                if then is not None:
                    then()
            pending.append(run)

        def to_dram_T(zi, dram_ap):
            k = cnt["zTs"] % 3
            cnt["zTs"] += 1
            transposeT(zi, 512, lambda k=k: zTs[k][:], [BzTs[k]],
                       then=lambda k=k: S.dma("sp", lambda e, k=k: e.dma_start(out=dram_ap, in_=zTs[k][:]), reads=[BzTs[k]], writes=[Buf()],
                                              owner=BzTs[k]))

        def vcopy(r, dst, Bdst):
            S.op("act", lambda e: e.activation(out=dst[:].rearrange("p (h c) -> p h c", c=65)[:, :, 0:64],
                                               in_=z_ps[r][:, :].rearrange("p (h d) -> p h d", d=64), func=AF.Copy),
                 reads=[Bzps[r]], writes=[Bdst])

        def load_x(tbk):
            s = tbk % 2
            is_all = tbk < NB_ALL
            blk = tbk if is_all else tbk - NB_ALL
            src = C.xall if is_all else C.xa
            S.dma("sp", lambda e: e.dma_start(out=xt[s][:], in_=src[blk * 128:(blk + 1) * 128, :]), writes=[Bxt[s]], owner=Bxt[s])

        def stageA(tbk):
            s = tbk % 2
            if tbk + 1 < NTB:
                load_x(tbk + 1)
            S.op("act", lambda e, s=s: e.activation(out=sqj[:], in_=xt[s][:], func=AF.Square, accum_out=ss[s][:, 0:1]),
                 reads=[Bxt[s]], writes=[Bsq, Bss[s]])
            S.op("act", lambda e, s=s: e.activation(out=rstd[s][:], in_=ss[s][:], func=AF.Sqrt, scale=1.0 / D, bias=RMS_EPS),
                 reads=[Bss[s]], writes=[Brs[s]])
            S.op("dve", lambda e, s=s: e.reciprocal(out=rstd[s][:], in_=rstd[s][:]), reads=[Brs[s]], writes=[Brs[s]])
            S.op("dve", lambda e, s=s: e.scalar_tensor_tensor(out=hb[s][:], in0=xt[s][:], scalar=rstd[s][:, 0:1], in1=gat[:],
                                                              op0=ALU.mult, op1=ALU.mult),
                 reads=[Bxt[s], Brs[s], Bgat], writes=[Bhb[s]])
            for m in range(8):
                S.op("pe", lambda e, s=s, m=m: e.transpose(out=hT_ps[s][:, m * 128:(m + 1) * 128], in_=hb[s][:, m * 128:(m + 1) * 128],
                                                           identity=C.ident[:]),
                     reads=[Bhb[s], C.Bident], writes=[BhTps[s]])
            S.op("act", lambda e, s=s: e.activation(out=hT[s][:].rearrange("p m t -> p (m t)"), in_=hT_ps[s][:], func=AF.Copy),
                 reads=[BhTps[s]], writes=[BhT[s]])

        load_x(0)
        stageA(0)
        S.op("dve", lambda e: e.tensor_copy(out=posf[:], in_=posi[:]), reads=[Bpos], writes=[Bpos])
        MAGIC = 12582912.0
        TWO_PI = float(2 * np.pi)
        CH = 16
        for b0 in range(0, NTB, CH):
            nb_ = min(CH, NTB - b0)
            ang = ta[0][:, 0:nb_ * 32].rearrange("p (b d) -> p b d", d=32)
            kk = tb_[0][:, 0:nb_ * 32].rearrange("p (b d) -> p b d", d=32)
            Bang, Bkk = Bta[0], Btb[0]
            S.op("dve", lambda e, ang=ang, b0=b0, nb_=nb_: e.tensor_tensor(
                out=ang, in0=posf[:, b0:b0 + nb_].unsqueeze(2).to_broadcast([128, nb_, 32]),
                in1=invt[:, :].unsqueeze(1).to_broadcast([128, nb_, 32]), op=ALU.mult),
                reads=[Bpos, Binv], writes=[Bang])
            for dst, shift in ((sinT, 0.0), (cosT, float(np.pi / 2))):
                S.op("dve", lambda e, ang=ang, kk=kk, shift=shift: e.tensor_scalar(out=kk, in0=ang, scalar1=shift,
                                                                                   scalar2=1.0 / TWO_PI, op0=ALU.add, op1=ALU.mult),
                     reads=[Bang], writes=[Bkk])
                S.op("dve", lambda e, kk=kk: e.tensor_scalar(out=kk, in0=kk, scalar1=MAGIC, scalar2=None, op0=ALU.add),
                     reads=[Bkk], writes=[Bkk])
                S.op("dve", lambda e, kk=kk: e.tensor_scalar(out=kk, in0=kk, scalar1=MAGIC, scalar2=-TWO_PI,
                                                             op0=ALU.subtract, op1=ALU.mult), reads=[Bkk], writes=[Bkk])
                S.op("dve", lambda e, ang=ang, kk=kk, shift=shift: e.scalar_tensor_tensor(out=kk, in0=ang, scalar=shift, in1=kk,
                                                                                          op0=ALU.add, op1=ALU.add),
                     reads=[Bang, Bkk], writes=[Bkk])
                S.op("dve", lambda e, kk=kk: e.tensor_scalar(out=kk, in0=kk, scalar1=float(np.pi), scalar2=float(-np.pi),
                                                             op0=ALU.min, op1=ALU.max), reads=[Bkk], writes=[Bkk])
                S.op("act", lambda e, kk=kk, dst=dst, b0=b0, nb_=nb_: e.activation(out=dst[:, b0:b0 + nb_, :], in_=kk, func=AF.Sin),
                     reads=[Bkk], writes=[Bcs])

        for tbk in range(NTB):
            s = tbk % 2
            is_all = tbk < NB_ALL
            blk = tbk if is_all else tbk - NB_ALL
            isq = (not is_all) and blk >= NB_A - NQ
            qi_ = blk - (NB_A - NQ)
            if is_all:
                groups = [("kb", C_KB, 512), ("vb", C_VB, 512), ("ki", C_KI, 64)]
            elif not isq:
                groups = [("ka", C_KA, 512), ("va", C_VA, 512)]
            else:
                groups = [("ka", C_KA, 512), ("va", C_VA, 512), ("wi", C_WI, 8), ("qa", C_QA, 512), ("qb", C_QB, 512), ("qi", C_QI, 512)]
            for g0i in range(0, len(groups), 3):
                rnd = groups[g0i:g0i + 3]
                rs = [mm_group(s, c0, n) for (_, c0, n) in rnd]
                if g0i == 0 and tbk + 1 < NTB:
                    stageA(tbk + 1)
                for fn in pending:
                    fn()
                pending.clear()
                for (kind, c0, n), r in zip(rnd, rs):
                    if kind == "kb":
                        zi = rope(r, 512, tbk)
                        transposeT(zi, 512, lambda blk=blk: C.kbT[:, blk, :], [C.BkbT[blk]])
                    elif kind == "vb":
                        k = cnt["vsb"] % 2
                        cnt["vsb"] += 1
                        vcopy(r, vsb[k], Bvsb[k])
                        S.dma("sp", lambda e, k=k, blk=blk: e.dma_start(out=C.vb_d[blk], in_=vsb[k][:]), reads=[Bvsb[k]], writes=[Buf()],
                              owner=Bvsb[k])
                    elif kind == "ki":
                        zi = rope(r, 64, tbk)
                        S.op("pool", lambda e, zi=zi: e.tensor_copy(out=zb[zi][:, 64:128], in_=zb[zi][:, 0:64]), reads=[Bzb[zi]], writes=[Bzb[zi]])
                        transposeT(zi, 128, lambda blk=blk: C.kiT[:, blk * 128:(blk + 1) * 128], [C.BkiT[blk]])
                    elif kind == "ka":
                        zi = rope(r, 512, tbk)
                        to_dram_T(zi, C.kaT_d[blk])
                    elif kind == "va":
                        k = cnt["vsa"] % 2
                        cnt["vsa"] += 1
                        vcopy(r, vsa[k], Bvsa[k])
                        S.op("dve", lambda e, k=k, blk=blk: e.tensor_scalar(out=vsa[k][:].rearrange("p (h c) -> p h c", c=65)[:, :, 64],
                                                                            in0=ones8[:], scalar1=avl[:, blk:blk + 1], scalar2=None, op0=ALU.mult),
                             reads=[Bones, Bavl], writes=[Bvsa[k]])
                        S.dma("sp", lambda e, k=k, blk=blk: e.dma_start(out=C.va_d[blk], in_=vsa[k][:]), reads=[Bvsa[k]], writes=[Buf()],
                              owner=Bvsa[k])
                    elif kind == "qa":
                        zi = rope(r, 512, tbk)
                        to_dram_T(zi, C.qaT_d[qi_])
                    elif kind == "qb":
                        zi = rope(r, 512, tbk)
                        to_dram_T(zi, C.qbT_d[qi_])
                    elif kind == "wi":
                        S.op("dve", lambda e, r=r, s=s: e.tensor_scalar(out=wsc[s][:], in0=z_ps[r][:, 0:8], scalar1=float(1.0 / (8.0 * np.sqrt(8.0))),
                                                                        scalar2=None, op0=ALU.mult), reads=[Bzps[r]], writes=[Bwsc[s]])
                        S.op("dve", lambda e, s=s: e.tensor_scalar(out=lohi[s][:, 8:16], in0=wsc[s][:], scalar1=0.0, scalar2=2.0,
                                                                   op0=ALU.is_ge, op1=ALU.mult), reads=[Bwsc[s]], writes=[Blohi[s]])
                        S.op("dve", lambda e, s=s: e.tensor_scalar(out=lohi[s][:, 0:8], in0=lohi[s][:, 8:16], scalar1=-1.0, scalar2=None,
                                                                   op0=ALU.add), reads=[Blohi[s]], writes=[Blohi[s]])
                        S.dma("sp", lambda e, s=s, qi_=qi_: e.dma_start(out=C.lohi_d[qi_], in_=lohi[s][:]), reads=[Blohi[s]], writes=[Buf()],
                              owner=Blohi[s])
                    elif kind == "qi":
                        zi = rope(r, 512, tbk)
                        S.op("pool", lambda e, s=s, zi=zi: e.tensor_tensor(out=zb[zi][:].rearrange("p (h d) -> p h d", d=64),
                                                                           in0=zb[zi][:].rearrange("p (h d) -> p h d", d=64),
                                                                           in1=wsc[s][:, :].unsqueeze(2).to_broadcast([128, 8, 64]), op=ALU.mult),
                             reads=[Bzb[zi], Bwsc[s]], writes=[Bzb[zi]])
                        to_dram_T(zi, C.qiT_d[qi_])
        for fn in pending:
            fn()
        pending.clear()
        if DBG:
            gd = S.new_group("p1dbg")
            S.dma("sp", lambda e: e.dma_start(out=C.dbg["kbT"], in_=C.kbT[:].rearrange("p b c -> p (b c)")), reads=C.BkbT, writes=[Buf()],
                  group=gd)
            S.dma("sp", lambda e: e.dma_start(out=C.dbg["kiT"], in_=C.kiT[:]), reads=C.BkiT, writes=[Buf()], group=gd)
        S.emit_phase()
        S.release(Bxt + BzTs + Bvsb + Bvsa + Blohi)


def phaseA(C):
    nc, S = C.nc, C.S
    with ExitStack() as es:
        sb = lambda n, s, d: es.enter_context(nc.sbuf_tensor(n, s, d))
        pst = lambda n, s, d: es.enter_context(nc.psum_tensor(n, s, d))
        kaT = sb("kaT", [128, NB_A, 512], BF16)
        va = sb("va", [128, NB_A, 520], BF16)
        qaT = sb("qaT", [128, NQ, 512], BF16)
        amk = sb("amk", [128, NQ, 128], BF16)
        pe_ = [sb(f"pe{i}", [128, 1024], BF16) for i in range(2)]
        pm = [sb(f"pm{i}", [128, 1024], BF16) for i in range(2)]
        rd = [sb(f"rd{i}", [128, 8], F32) for i in range(2)]
        mxa = [sb(f"mxa{i}", [128, 512], BF16) for i in range(2)]
        STA = [pst(f"st{i}", [128, 1024], F32) for i in range(2)]
        st = [[STA[i][:, j * 512:(j + 1) * 512] for j in range(2)] for i in range(2)]
        acc = [[pst(f"acc{i}{j}", [128, 512], F32) for j in range(2)] for i in range(2)]
        BkaT, Bva, BqaT, Bamk = Buf(), Buf(), Buf(), Buf()
        Bpe = [Buf() for _ in range(2)]
        Bpm = [Buf() for _ in range(2)]
        Brd = [Buf() for _ in range(2)]
        Bmxa = [Buf() for _ in range(2)]
        Bst = [[Buf() for _ in range(2)] for _ in range(2)]
        Bacc = [[Buf() for _ in range(2)] for _ in range(2)]
        g = S.new_group("pAinit")
        for c0 in range(0, NB_A, 11):
            S.dma("sp", lambda e, c0=c0: e.dma_start(out=kaT[:, c0:c0 + 11, :], in_=C.kaT_d[c0:c0 + 11].rearrange("b p c -> p b c")),
                  writes=[BkaT], group=g)
            S.dma("sp", lambda e, c0=c0: e.dma_start(out=va[:, c0:c0 + 11, :], in_=C.va_d[c0:c0 + 11].rearrange("b p c -> p b c")),
                  writes=[Bva], group=g)
        S.dma("sp", lambda e: e.dma_start(out=qaT[:], in_=C.qaT_d.rearrange("b p c -> p b c")), writes=[BqaT], group=g)
        gp = S.new_group("pAinit_sw")
        S.dma("pool", lambda e: e.dma_start(out=amk[:].rearrange("p r q -> p (r q)"), in_=C.amask), writes=[Bamk], group=gp)
        def qk_A(i, r, sidx):
            kb = i + r
            for h in range(8):
                rows = slice((h % 2) * 64, (h % 2) * 64 + 64)
                pr = h // 2
                S.op("pe", lambda e, h=h, rows=rows, pr=pr: e.matmul(
                    st[sidx][h % 2][:, pr * 128:(pr + 1) * 128], lhsT=kaT[rows, kb, pr * 128:(pr + 1) * 128],
                    rhs=qaT[rows, i, pr * 128:(pr + 1) * 128], start=True, stop=True),
                    reads=[BkaT, BqaT], writes=[Bst[sidx][h % 2]])

        steps = [(i, r) for i in range(NQ) for r in range(NQ)]
        qk_A(0, 0, 0)
        qk_A(steps[1][0], steps[1][1], 1)
        for it, (i, r) in enumerate(steps):
            a = i % 2
            kb = i + r
            sidx = it % 2
            S.op("act", lambda e, sidx=sidx: e.activation(out=pe_[sidx][:, :], in_=STA[sidx][:, :], func=AF.Exp, scale=0.125),
                 reads=[Bst[sidx][0], Bst[sidx][1]], writes=[Bpe[sidx]])
            if it + 2 < len(steps):
                qk_A(steps[it + 2][0], steps[it + 2][1], sidx)
            S.op("dve", lambda e, sidx=sidx, r=r: e.tensor_tensor(out=pm[sidx][:].rearrange("p (h q) -> p h q", q=128),
                                                                  in0=pe_[sidx][:].rearrange("p (h q) -> p h q", q=128),
                                                                  in1=amk[:, r, :].unsqueeze(1).to_broadcast([128, 8, 128]), op=ALU.mult),
                 reads=[Bpe[sidx], Bamk], writes=[Bpm[sidx]])
            for h in range(8):
                gi = h // 4
                cc = (h % 4) * 65
                pcol = (h % 2) * 512 + (h // 2) * 128
                S.op("pe", lambda e, h=h, gi=gi, cc=cc, pcol=pcol, sidx=sidx, kb=kb, a=a, r=r: e.matmul(
                    acc[a][gi][:, cc:cc + 65], lhsT=pm[sidx][:, pcol:pcol + 128], rhs=va[:, kb, h * 65:(h + 1) * 65],
                    start=(r == 0 and h % 4 == 0), stop=(r == NQ - 1), skip_group_check=True),
                    reads=[Bpm[sidx], Bva], writes=[Bacc[a][gi]])
            if r != NQ - 1:
                continue
            for gi in range(2):
                accv = acc[a][gi][:, 0:260].rearrange("p (h c) -> p h c", c=65)
                S.op("dve", lambda e, a=a, gi=gi, accv=accv: e.tensor_scalar(out=rd[a][:, gi * 4:(gi + 1) * 4].unsqueeze(2), in0=accv[:, :, 64:65],
                                                                             scalar1=1e-30, scalar2=None, op0=ALU.max),
                     reads=[Bacc[a][gi]], writes=[Brd[a]])
                S.op("dve", lambda e, a=a, gi=gi: e.reciprocal(out=rd[a][:, gi * 4:(gi + 1) * 4], in_=rd[a][:, gi * 4:(gi + 1) * 4]),
                     reads=[Brd[a]], writes=[Brd[a]])
                S.op("dve", lambda e, a=a, gi=gi, accv=accv: e.tensor_tensor(
                    out=mxa[a][:, gi * 256:(gi + 1) * 256].rearrange("p (h d) -> p h d", d=64), in0=accv[:, :, 0:64],
                    in1=rd[a][:, gi * 4:(gi + 1) * 4].unsqueeze(2).to_broadcast([128, 4, 64]), op=ALU.mult),
                    reads=[Bacc[a][gi], Brd[a]], writes=[Bmxa[a]])
            S.dma("sp", lambda e, a=a, i=i: e.dma_start(out=C.mix_d[i][:, 0:512], in_=mxa[a][:]), reads=[Bmxa[a]], writes=[Buf()], owner=Bmxa[a])
        S.emit_phase()
        S.release(Bmxa)
        if DBG and STOP_AFTER == "A":
            dump_mix(C)


def dump_mix(C):
    nc, S = C.nc, C.S
    with nc.sbuf_tensor("dm0", [128, 1024], BF16) as dm0, nc.sbuf_tensor("dm1", [128, 1024], BF16) as dm1:
        dm = [dm0, dm1]
        Bd = [Buf(), Buf()]
        for i in range(NQ):
            k = i % 2
            c1 = 512 if STOP_AFTER == "A" else 1024
            S.dma("sp", lambda e, i=i, k=k, c1=c1: e.dma_start(out=dm[k][:, 0:c1], in_=C.mix_d[i][:, 0:c1]), writes=[Bd[k]], owner=Bd[k])
            S.dma("sp", lambda e, i=i, k=k, c1=c1: e.dma_start(out=C.dbg["mix"][i][:, 0:c1], in_=dm[k][:, 0:c1]), reads=[Bd[k]], writes=[Buf()],
                  owner=Bd[k])
        S.emit_phase()
        S.release(Bd)


def phaseB(C):
    nc, S = C.nc, C.S
    NCHK = T // 512
    with ExitStack() as es:
        sb = lambda n, s, d: es.enter_context(nc.sbuf_tensor(n, s, d))
        pst = lambda n, s, d: es.enter_context(nc.psum_tensor(n, s, d))
        iotaN = sb("iotaN", [128, T], F32)
        Iacc = sb("Iacc", [128, T], F32)
        Mb = sb("Mb", [128, T], BF16)
        qbT = [sb(f"qbT{i}", [128, 512], BF16) for i in range(2)]
        qiT = [sb(f"qiT{i}", [128, 512], BF16) for i in range(2)]
        lohi = [sb(f"lohiB{i}", [128, 16], F32) for i in range(2)]
        tqf = sb("tqf", [128, NQ], F32)
        cq = sb("cq", [128, NQ], F32)
        fh = [sb(f"fh{i}", [128, 1024], F32) for i in range(3)]
        lo_ = sb("lo_", [128, 1], F32)
        cand = sb("cand", [128, 1], F32)
        cntt = sb("cntt", [128, 1], F32)
        ind = sb("ind", [128, 1], F32)
        ncand = sb("ncand", [128, 1], F32)
        ssum = sb("ssum", [128, 1], F32)
        vbs = [sb(f"vbs{i}", [128, 8, 520], BF16) for i in range(2)]
        pe_ = [sb(f"peB{i}", [128, 1024], BF16) for i in range(2)]
        pm = [sb(f"pmB{i}", [128, 1024], BF16) for i in range(2)]
        mts = [sb(f"mts{i}", [128, 1024], BF16) for i in range(2)]
        rd = [sb(f"rdB{i}", [128, 8], F32) for i in range(2)]
        mxb = [sb(f"mxb{i}", [128, 512], BF16) for i in range(2)]
        PB = [pst(f"PB{i}", [128, 1024], F32) for i in range(4)]
        BP = [[Buf() for _ in range(2)] for _ in range(4)]
        scst = [[PB[i][:, j * 512:(j + 1) * 512] for j in range(2)] for i in range(2)]
        acc = [PB[2][:, j * 512:(j + 1) * 512] for j in range(2)]
        mt_ps = [PB[3][:, j * 512:(j + 1) * 512].bitcast(BF16) for j in range(2)]

        Biota, Btq, Bcq, Blo, Bcand, Bcnt, Bind, Bncand, Bssum, BMall2 = (Buf() for _ in range(10))
        BI = [Buf() for _ in range(NCHK)]
        BM = [Buf() for _ in range(NB_ALL // 8)]
        BMall = Buf()
        BqbT = [Buf() for _ in range(2)]
        BqiT = [Buf() for _ in range(2)]
        Blohi = [Buf() for _ in range(2)]
        Bfh = [Buf() for _ in range(3)]
        Bvbs = [Buf() for _ in range(2)]
        Bpe = [Buf() for _ in range(2)]
        Bpm = [Buf() for _ in range(2)]
        Bmts = [Buf() for _ in range(2)]
        Brd = [Buf() for _ in range(2)]
        Bmxb = [Buf() for _ in range(2)]
        Bscst = [[BP[i][j] for j in range(2)] for i in range(2)]
        Bacc = [BP[2][j] for j in range(2)]
        Bmtps = [BP[3][j] for j in range(2)]

        g = S.new_group("pBinit")
        S.dma("sp", lambda e: e.dma_start(out=tqf[:], in_=C.tq), writes=[Btq], group=g)
        S.op("dve", lambda e: e.tensor_scalar(out=cq[:], in0=tqf[:], scalar1=0.5, scalar2=-BIG, op0=ALU.add, op1=ALU.mult),
             reads=[Btq], writes=[Bcq])
        S.op("pool", lambda e: e.iota(Iacc[:].bitcast(I32), pattern=[[1, T]], base=0, channel_multiplier=0), writes=BI)
        S.op("dve", lambda e: e.tensor_scalar(out=iotaN[:], in0=Iacc[:].bitcast(I32), scalar1=-BIG, scalar2=None, op0=ALU.mult),
             reads=BI, writes=[Biota])

        gcast = S.new_group("pBcast")
        for (src, dst, rows, cols) in ((C.w_o, C.wob_d, D, D), (C.w_up, C.wupb_d, D, 2 * D_FF), (C.w_down, C.wdnb_d, D_FF, D),
                                       (C.w_g, C.wgb_d, D, D), (C.w_p, C.wpb_d, 256, D)):
            cw_ = 1408 if cols == 2 * D_FF else 1024
            for r0 in range(0, rows, 128):
                for c0 in range(0, cols, cw_):
                    S.dma("pool", lambda e, src=src, dst=dst, r0=r0, c0=c0, cw_=cw_: e.dma_start(out=dst[r0:r0 + 128, c0:c0 + cw_],
                                                                                         in_=src[r0:r0 + 128, c0:c0 + cw_]),
                          writes=[Buf()], group=gcast)
        fcnt = 0

        def load_q(i):
            qs = i % 2
            S.dma("sp", lambda e: e.dma_start(out=qbT[qs][:], in_=C.qbT_d[i]), writes=[BqbT[qs]], owner=BqbT[qs])
            S.dma("sp", lambda e: e.dma_start(out=qiT[qs][:], in_=C.qiT_d[i]), writes=[BqiT[qs]], owner=BqiT[qs])
            S.dma("sp", lambda e: e.dma_start(out=lohi[qs][:], in_=C.lohi_d[i]), writes=[Blohi[qs]], owner=Blohi[qs])

        load_q(0)
        for i in range(NQ):
            qs = i % 2
            if i + 1 < NQ:
                load_q(i + 1)
            nkc = min(NCHK, -(-(3 * (OWN // 128) + i) // 4))
            nkb = nkc * 4
            nkeys = nkb * 128
            HB = max(1, (nkc * 7) // 16)
            h1 = HB * 512
            n_act = nkeys - h1
            cthr = float(256 - n_act // 2)
            NG = -(-nkb // 8)
            for cp in range((nkc + 1) // 2):
                nck = min(2, nkc - 2 * cp)
                w = nck * 512
                c0 = cp * 1024
                BIc = BI[2 * cp:2 * cp + nck]
                for h in range(8):
                    rows = slice((h % 2) * 64, (h % 2) * 64 + 64)
                    pr = h // 2
                    ti = (h % 2) + 2 * (pr % 2)
                    for u in range(nck):
                        S.op("pe", lambda e, rows=rows, pr=pr, ti=ti, qs=qs, u=u, c0=c0: e.matmul(
                            PB[ti][:, u * 512:(u + 1) * 512], lhsT=qiT[qs][rows, pr * 128:(pr + 1) * 128],
                            rhs=C.kiT[rows, c0 + u * 512:c0 + (u + 1) * 512], start=True, stop=True),
                            reads=[BqiT[qs]] + C.BkiT[(c0 // 128) + u * 4:(c0 // 128) + (u + 1) * 4], writes=[BP[ti][u]])
                    k = fcnt % 3
                    fcnt += 1
                    S.op("act", lambda e, ti=ti, k=k, qs=qs, h=h, w=w: e.activation(out=fh[k][:, 0:w], in_=PB[ti][:, 0:w], func=AF.Relu,
                                                                                 scale=lohi[qs][:, h:h + 1]),
                         reads=BP[ti][0:nck] + [Blohi[qs]], writes=[Bfh[k]])
                    if h == 0:
                        S.op("dve", lambda e, k=k, c0=c0, w=w, qs=qs, h=h: e.tensor_scalar(out=Iacc[:, c0:c0 + w], in0=fh[k][:, 0:w],
                                                                                       scalar1=lohi[qs][:, h:h + 1], scalar2=None, op0=ALU.mult),
                             reads=[Bfh[k], Blohi[qs]], writes=BIc)
                    else:
                        S.op("dve", lambda e, k=k, c0=c0, w=w, qs=qs, h=h: e.scalar_tensor_tensor(
                            out=Iacc[:, c0:c0 + w], in0=fh[k][:, 0:w], scalar=lohi[qs][:, h:h + 1], in1=Iacc[:, c0:c0 + w],
                            op0=ALU.mult, op1=ALU.add), reads=[Bfh[k], Blohi[qs]] + BIc, writes=BIc)
                S.op("dve", lambda e, c0=c0, w=w, i=i: e.scalar_tensor_tensor(out=Iacc[:, c0:c0 + w], in0=iotaN[:, c0:c0 + w],
                                                                              scalar=cq[:, i:i + 1], in1=Iacc[:, c0:c0 + w],
                                                                              op0=ALU.subtract, op1=ALU.min),
                     reads=[Biota, Bcq] + BIc, writes=BIc)
            S.op("dve", lambda e: e.memset(cand[:], 0.0), writes=[Bcand])
            for b in range(NBIS):
                step = float(BIS_B * 2.0 / (2 ** (b + 1)))
                nstep = step / 2.0
                S.op("dve", lambda e, h1=h1: e.tensor_scalar(out=Mb[:, 0:h1], in0=Iacc[:, 0:h1], scalar1=cand[:, 0:1], scalar2=None,
                                                      op0=ALU.is_ge, op1=ALU.add, accum_out=cntt[:, 0:1]),
                     reads=BI[:HB] + [Bcand], writes=[BMall, Bcnt])
                S.op("act", lambda e, h1=h1, nkeys=nkeys: e.activation(out=Mb[:, h1:nkeys], in_=Iacc[:, h1:nkeys], func=AF.Sign, scale=-1.0, bias=cand[:, 0:1],
                                                   accum_out=ssum[:, 0:1]),
                     reads=BI[HB:nkc] + [Bcand], writes=[BMall2, Bssum])
                S.op("dve", lambda e: e.scalar_tensor_tensor(out=ind[:], in0=ssum[:], scalar=-0.5, in1=cntt[:], op0=ALU.mult, op1=ALU.add),
                     reads=[Bssum, Bcnt], writes=[Bind])
                if b < NBIS - 1:
                    S.op("dve", lambda e, nstep=nstep, cthr=cthr: e.tensor_scalar(out=ind[:], in0=ind[:], scalar1=cthr, scalar2=2.0 * nstep,
                                                                       op0=ALU.is_ge, op1=ALU.mult), reads=[Bind], writes=[Bind])
                    S.op("dve", lambda e, nstep=nstep: e.scalar_tensor_tensor(out=cand[:], in0=ind[:], scalar=-nstep, in1=cand[:],
                                                                              op0=ALU.add, op1=ALU.add), reads=[Bind, Bcand], writes=[Bcand])
                else:
                    S.op("dve", lambda e, step=step, cthr=cthr: e.tensor_scalar(out=ind[:], in0=ind[:], scalar1=cthr, scalar2=step,
                                                                     op0=ALU.is_ge, op1=ALU.mult), reads=[Bind], writes=[Bind])
                    S.op("dve", lambda e, step=step: e.scalar_tensor_tensor(out=lo_[:], in0=ind[:], scalar=-step, in1=cand[:],
                                                                            op0=ALU.add, op1=ALU.add), reads=[Bind, Bcand], writes=[Blo])
            for gq in range(NG):
                ce = min((gq + 1) * 1024, nkeys)
                S.op("dve", lambda e, gq=gq, ce=ce: e.tensor_scalar(out=Mb[:, gq * 1024:ce], in0=Iacc[:, gq * 1024:ce],
                                                                    scalar1=lo_[:, 0:1], scalar2=None, op0=ALU.is_ge),
                     reads=BI[2 * gq:min(2 * gq + 2, nkc)] + [Blo, BMall, BMall2], writes=[BM[gq]])

            def load_group(gq):
                vk = gq % 2
                S.dma("sp", lambda e, gq=gq, vk=vk: e.dma_start(out=vbs[vk][:], in_=C.vb_d[gq * 8:(gq + 1) * 8].rearrange("b p c -> p b c")),
                      writes=[Bvbs[vk]], owner=Bvbs[vk])
                for j in range(min(8, nkb - gq * 8)):
                    n = gq * 8 + j
                    S.op("pe", lambda e, j=j, n=n, vk=vk: e.transpose(out=mt_ps[vk][:, j * 128:(j + 1) * 128], in_=Mb[:, n * 128:(n + 1) * 128],
                                                                      identity=C.ident[:]),
                         reads=[BM[gq], C.Bident], writes=[Bmtps[vk]])
                nj = min(8, nkb - gq * 8)
                S.op("act", lambda e, vk=vk, nj=nj: e.activation(out=mts[vk][:, 0:nj * 128], in_=mt_ps[vk][:, 0:nj * 128], func=AF.Copy),
                     reads=[Bmtps[vk]], writes=[Bmts[vk]])

            def qk_B(n, sidx, qs=qs):
                for h in range(8):
                    rows = slice((h % 2) * 64, (h % 2) * 64 + 64)
                    pr = h // 2
                    S.op("pe", lambda e, h=h, rows=rows, pr=pr: e.matmul(
                        scst[sidx][h % 2][:, pr * 128:(pr + 1) * 128], lhsT=C.kbT[rows, n, pr * 128:(pr + 1) * 128],
                        rhs=qbT[qs][rows, pr * 128:(pr + 1) * 128], start=True, stop=True),
                        reads=[C.BkbT[n], BqbT[qs]], writes=[Bscst[sidx][h % 2]])

            load_group(0)
            qk_B(0, 0)
            qk_B(1, 1)
            for n in range(nkb):
                gq, j = n // 8, n % 8
                vk = gq % 2
                sidx = n % 2
                if j == 0 and gq + 1 < NG:
                    load_group(gq + 1)
                S.op("act", lambda e, sidx=sidx: e.activation(out=pe_[sidx][:, :], in_=PB[sidx][:, :], func=AF.Exp, scale=0.125),
                     reads=[Bscst[sidx][0], Bscst[sidx][1]], writes=[Bpe[sidx]])
                if n + 2 < nkb:
                    qk_B(n + 2, sidx)
                S.op("dve", lambda e, sidx=sidx, vk=vk, j=j: e.tensor_tensor(
                    out=pm[sidx][:].rearrange("p (h q) -> p h q", q=128), in0=pe_[sidx][:].rearrange("p (h q) -> p h q", q=128),
                    in1=mts[vk][:, j * 128:(j + 1) * 128].unsqueeze(1).to_broadcast([128, 8, 128]), op=ALU.mult),
                    reads=[Bpe[sidx], Bmts[vk]], writes=[Bpm[sidx]])
                for h in range(8):
                    gi = h // 4
                    cc = (h % 4) * 65
                    pcol = (h % 2) * 512 + (h // 2) * 128
                    S.op("pe", lambda e, h=h, gi=gi, cc=cc, pcol=pcol, sidx=sidx, vk=vk, j=j, n=n: e.matmul(
                        acc[gi][:, cc:cc + 65], lhsT=pm[sidx][:, pcol:pcol + 128], rhs=vbs[vk][:, j, h * 65:(h + 1) * 65],
                        start=(n == 0 and h % 4 == 0), stop=(n == nkb - 1), skip_group_check=True),
                        reads=[Bpm[sidx], Bvbs[vk]], writes=[Bacc[gi]])
            a = i % 2
            for gi in range(2):
                accv = acc[gi][:, 0:260].rearrange("p (h c) -> p h c", c=65)
                S.op("dve", lambda e, a=a, gi=gi, accv=accv: e.tensor_scalar(out=rd[a][:, gi * 4:(gi + 1) * 4].unsqueeze(2), in0=accv[:, :, 64:65],
                                                                             scalar1=1e-30, scalar2=None, op0=ALU.max),
                     reads=[Bacc[gi]], writes=[Brd[a]])
                S.op("dve", lambda e, a=a, gi=gi: e.reciprocal(out=rd[a][:, gi * 4:(gi + 1) * 4], in_=rd[a][:, gi * 4:(gi + 1) * 4]),
                     reads=[Brd[a]], writes=[Brd[a]])
                S.op("dve", lambda e, a=a, gi=gi, accv=accv: e.tensor_tensor(
                    out=mxb[a][:, gi * 256:(gi + 1) * 256].rearrange("p (h d) -> p h d", d=64), in0=accv[:, :, 0:64],
                    in1=rd[a][:, gi * 4:(gi + 1) * 4].unsqueeze(2).to_broadcast([128, 4, 64]), op=ALU.mult),
                    reads=[Bacc[gi], Brd[a]], writes=[Bmxb[a]])
            S.dma("sp", lambda e, a=a, i=i: e.dma_start(out=C.mix_d[i][:, 512:1024], in_=mxb[a][:]), reads=[Bmxb[a]], writes=[Buf()], owner=Bmxb[a])
        S.emit_phase()
        S.release(BqbT + BqiT + Blohi + Bvbs + Bmxb)
    if DBG and STOP_AFTER == "B":
        dump_mix(C)


def rms_to_bf16(S, xin_ap, Bx, sqj, Bsq, ss, Bss, rstd, Brs, gam, Bgam, out_ap, Bout):
    S.op("act", lambda e: e.activation(out=sqj[:], in_=xin_ap, func=AF.Square, accum_out=ss[:, 0:1]), reads=[Bx], writes=[Bsq, Bss])
    S.op("act", lambda e: e.activation(out=rstd[:], in_=ss[:], func=AF.Sqrt, scale=1.0 / D, bias=RMS_EPS), reads=[Bss], writes=[Brs])
    S.op("dve", lambda e: e.reciprocal(out=rstd[:], in_=rstd[:]), reads=[Brs], writes=[Brs])
    S.op("dve", lambda e: e.scalar_tensor_tensor(out=out_ap, in0=xin_ap, scalar=rstd[:, 0:1], in1=gam[:], op0=ALU.mult, op1=ALU.mult),
         reads=[Bx, Brs, Bgam], writes=[Bout])


def phaseC1(C):
    nc, S = C.nc, C.S
    TT = 256
    NT = OWN // TT
    with ExitStack() as es:
        sb = lambda n, s, d: es.enter_context(nc.sbuf_tensor(n, s, d))
        pst = lambda n, s, d: es.enter_context(nc.psum_tensor(n, s, d))
        wo = sb("wo", [128, 8, D], BF16)
        wup = sb("wup", [128, 8, 2 * D_FF], BF16)
        wdn = sb("wdn", [128, 22, D], BF16)
        gff = sb("gff", [128, D], F32)
        cw = sb("cw", [128, 4, NCH], F32)
        carry = sb("carry", [128, NCH, 2], F32)
        mixt = [sb("mixt0", [128, D], BF16)] * 2
        mixT = [sb("mixT0", [128, 8, 128], BF16)] * 2
        xio = [sb(f"xio{i}", [128, 2, D], F32) for i in range(2)]
        cwl = xio[0][0:NCH, 0, 0:512].rearrange("c (j p) -> c j p", j=4)
        ss = [sb(f"ssC{i}", [128, 1], F32) for i in range(2)]
        rstd = [sb(f"rstdC{i}", [128, 1], F32) for i in range(2)]
        h2 = [sb(f"h2{i}", [128, D], BF16) for i in range(2)]
        h2T = [sb(f"h2T{i}", [128, 8, TT], BF16) for i in range(2)]
        U = [sb(f"U{i}", [128, TT + 2], F32) for i in range(4)]
        y = [sb(f"y{i}", [128, TT], F32) for i in range(5)]
        sg = [sb("sg0", [128, TT], F32)] * 2
        aT = sb("aT", [128, 22, TT], BF16)
        tp_ps = [pst(f"tpps{i}", [128, 1024], BF16) for i in range(2)]
        wo_ps = [pst(f"wops{i}", [128, 512], F32) for i in range(2)]
        u_ps = [pst(f"ups{i}", [128, 512], F32) for i in range(4)]
        wd_ps = wo_ps

        Bwo, Bwup, Bwdn, Bgff, Bcw, Bsq = (Buf() for _ in range(6))
        Bcarry = [Buf() for _ in range(NCH)]
        Bmixt = [Buf()] * 2
        BmixT = [Buf()] * 2
        Bxio = [[Buf() for _ in range(2)] for _ in range(2)]
        Bcwl = Bxio[0][0]
        Bss = [Buf() for _ in range(2)]
        Brs = [Buf() for _ in range(2)]
        Bh2 = [Buf() for _ in range(2)]
        Bh2T = [[Buf() for _ in range(2)] for _ in range(2)]
        BU = [Buf() for _ in range(4)]
        By = [Buf() for _ in range(5)]
        Bsg = [Buf()] * 2
        BaT = [Buf() for _ in range(22)]
        Btp = [Buf() for _ in range(2)]
        Bwops = [Buf() for _ in range(2)]
        Bups = [Buf() for _ in range(4)]
        Bwdps = Bwops

        g = S.new_group("pC1init")
        gw = S.new_group("pC1w")
        S.dma("sp", lambda e: e.dma_start(out=gff[:], in_=C.ffn_norm.partition_broadcast(128)), writes=[Bgff], group=g)
        for j in range(3):
            S.dma("sp", lambda e, j=j: e.dma_start(out=cwl[:, j, :], in_=C.conv_w[j].rearrange("(c p) -> c p", p=128)), writes=[Bcwl], group=g)
        S.dma("sp", lambda e: e.dma_start(out=cwl[:, 3, :], in_=C.conv_b[0].rearrange("(c p) -> c p", p=128)), writes=[Bcwl], group=g)
        gwu = S.new_group("pC1wu")
        gwd = S.new_group("pC1wd")
        S.dma("sp", lambda e: e.dma_start(out=wo[:], in_=C.wob_d.rearrange("(m p) c -> p m c", p=128)), writes=[Bwo], group=gw)
        for m in range(8):
            S.dma("sp", lambda e, m=m: e.dma_start(out=wup[:, m, :], in_=C.wupb_d[m * 128:(m + 1) * 128, :]), writes=[Bwup], group=gwu)
        for k0 in range(0, 22, 11):
            S.dma("sp", lambda e, k0=k0: e.dma_start(out=wdn[:, k0:k0 + 11, :],
                                                     in_=C.wdnb_d[k0 * 128:(k0 + 11) * 128, :].rearrange("(m p) c -> p m c", p=128)),
                  writes=[Bwdn], group=gwd)
        for j in range(4):
            S.op("pe", lambda e, j=j: e.transpose(out=wo_ps[0][:, j * NCH:(j + 1) * NCH], in_=cwl[:, j, :], identity=C.identf[0:NCH, 0:NCH]),
                 reads=[Bcwl, C.Bident], writes=[Bwops[0]])
        S.op("dve", lambda e: e.tensor_copy(out=cw[:].rearrange("p j c -> p (j c)"), in_=wo_ps[0][:, 0:4 * NCH]), reads=[Bwops[0]], writes=[Bcw])

        cnt = {"tp": 0, "u": 0, "U": 0, "y": 0}

        def load_xrow(xa_blk, xdst, Bxdst):
            S.dma("sp", lambda e: e.dma_start(out=xdst, in_=C.xa[xa_blk * 128:(xa_blk + 1) * 128, :]), writes=[Bxdst], owner=Bxdst)

        def attn_out_block(mix_idx, xa_blk, xdst, Bxdst, slot):
            S.dma("sp", lambda e: e.dma_start(out=mixt[slot][:], in_=C.mix_d[mix_idx]), writes=[Bmixt[slot]], owner=Bmixt[slot])
            q = cnt["tp"] % 2
            cnt["tp"] += 1
            for m in range(8):
                S.op("pe", lambda e, m=m, q=q: e.transpose(out=tp_ps[q][:, m * 128:(m + 1) * 128], in_=mixt[slot][:, m * 128:(m + 1) * 128],
                                                           identity=C.ident[:]), reads=[Bmixt[slot], C.Bident], writes=[Btp[q]])
            S.op("act", lambda e, q=q: e.activation(out=mixT[slot][:].rearrange("p m t -> p (m t)"), in_=tp_ps[q][:], func=AF.Copy),
                 reads=[Btp[q]], writes=[BmixT[slot]])
            for hf in range(2):
                for m in range(8):
                    S.op("pe", lambda e, m=m, hf=hf: e.matmul(wo_ps[hf][:, :], lhsT=mixT[slot][:, m, :], rhs=wo[:, m, hf * 512:(hf + 1) * 512],
                                                              start=(m == 0), stop=(m == 7)), reads=[BmixT[slot], Bwo], writes=[Bwops[hf]])
                S.op("dve", lambda e, hf=hf: e.tensor_tensor(out=xdst[:, hf * 512:(hf + 1) * 512], in0=xdst[:, hf * 512:(hf + 1) * 512],
                                                             in1=wo_ps[hf][:, :], op=ALU.add), reads=[Bxdst, Bwops[hf]], writes=[Bxdst])

        def norm_T(xsrc, Bxsrc, slot, dstT_fn, BdstT):
            rms_to_bf16(S, xsrc, Bxsrc, h2[slot], Bh2[slot], ss[slot], Bss[slot], rstd[slot], Brs[slot], gff, Bgff, h2[slot][:], Bh2[slot])
            q = cnt["tp"] % 2
            cnt["tp"] += 1
            for m in range(8):
                S.op("pe", lambda e, m=m, q=q: e.transpose(out=tp_ps[q][:, m * 128:(m + 1) * 128], in_=h2[slot][:, m * 128:(m + 1) * 128],
                                                           identity=C.ident[:]), reads=[Bh2[slot], C.Bident], writes=[Btp[q]])
            S.op("act", lambda e, q=q: e.activation(out=dstT_fn(), in_=tp_ps[q][:].rearrange("p (m t) -> p m t", t=128), func=AF.Copy),
                 reads=[Btp[q]], writes=[BdstT])

        load_xrow(NB_A - NQ, xio[1][:, 1, :], Bxio[1][1])
        for blk in range(2):
            load_xrow(NB_A - NQ + 1 + blk, xio[0][:, blk, :], Bxio[0][blk])
        attn_out_block(0, NB_A - NQ, xio[1][:, 1, :], Bxio[1][1], 1)
        norm_T(xio[1][:, 1, :], Bxio[1][1], 1, lambda: h2T[1][:, :, 128:256], Bh2T[1][1])
        for cc in range(NCH):
            r = cnt["u"] % 4
            cnt["u"] += 1
            ub, uh = u_ps[r], 0
            for m in range(8):
                S.op("pe", lambda e, m=m, cc=cc, ub=ub, uh=uh: e.matmul(ub[:, uh * 256:uh * 256 + 2], lhsT=wup[:, m, cc * 128:(cc + 1) * 128],
                                                                        rhs=h2T[1][:, m, 254:256], start=(m == 0), stop=(m == 7)),
                     reads=[Bh2T[1][1], Bwup], writes=[Bups[r]])
            S.op("dve", lambda e, cc=cc, ub=ub, uh=uh: e.tensor_copy(out=carry[:, cc, :], in_=ub[:, uh * 256:uh * 256 + 2]),
                 reads=[Bups[r]], writes=[Bcarry[cc]])

        def prologue(t):
            xs = t % 2
            for blk in range(2):
                tb = 2 * t + blk
                attn_out_block(tb + 1, NB_A - NQ + 1 + tb, xio[xs][:, blk, :], Bxio[xs][blk], blk)
                norm_T(xio[xs][:, blk, :], Bxio[xs][blk], blk, lambda xs=xs, blk=blk: h2T[xs][:, :, blk * 128:(blk + 1) * 128], Bh2T[xs][blk])

        prologue(0)
        for t in range(NT):
            xs = t % 2
            if t + 1 < NT:
                for blk in range(2):
                    load_xrow(NB_A - NQ + 1 + 2 * (t + 1) + blk, xio[1 - xs][:, blk, :], Bxio[1 - xs][blk])
            for k in range(22):
                ys = []
                pair = []
                for cc in (k, k + 22):
                    r = cnt["u"] % 4
                    cnt["u"] += 1
                    ub = u_ps[r]
                    for m in range(8):
                        S.op("pe", lambda e, m=m, cc=cc, ub=ub, xs=xs: e.matmul(
                            ub[:, 0:TT], lhsT=wup[:, m, cc * 128:(cc + 1) * 128], rhs=h2T[xs][:, m, :],
                            start=(m == 0), stop=(m == 7)), reads=[Bh2T[xs][0], Bh2T[xs][1], Bwup], writes=[Bups[r]])
                    ui = cnt["U"] % 4
                    cnt["U"] += 1
                    yi = cnt["y"] % 5
                    cnt["y"] += 1
                    pair.append((cc, r, ub, ui, yi))
                    ys.append(yi)
                for cc, r, ub, ui, yi in pair:
                    S.op("pool", lambda e, ui=ui, cc=cc: e.tensor_copy(out=U[ui][:, 0:2], in_=carry[:, cc, :]), reads=[Bcarry[cc]], writes=[BU[ui]])
                for cc, r, ub, ui, yi in pair:
                    S.op("act", lambda e, ui=ui, ub=ub: e.activation(out=U[ui][:, 2:TT + 2], in_=ub[:, 0:TT], func=AF.Copy),
                         reads=[Bups[r]], writes=[BU[ui]])
                for cc, r, ub, ui, yi in pair:
                    S.op("pool", lambda e, ui=ui, cc=cc: e.tensor_copy(out=carry[:, cc, :], in_=U[ui][:, TT:TT + 2]), reads=[BU[ui]],
                         writes=[Bcarry[cc]])
                for pi, (cc, r, ub, ui, yi) in enumerate(pair):
                    if pi == 0:
                        S.op("act", lambda e, ui=ui, yi=yi, cc=cc: e.activation(out=y[yi][:], in_=U[ui][:, 2:TT + 2], func=AF.Identity,
                                                                                scale=cw[:, 2, cc:cc + 1], bias=cw[:, 3, cc:cc + 1]),
                             reads=[BU[ui], Bcw], writes=[By[yi]])
                    else:
                        S.op("pool", lambda e, ui=ui, yi=yi, cc=cc: e.tensor_scalar(out=y[yi][:], in0=U[ui][:, 2:TT + 2], scalar1=cw[:, 2, cc:cc + 1],
                                                                                    scalar2=cw[:, 3, cc:cc + 1], op0=ALU.mult, op1=ALU.add),
                             reads=[BU[ui], Bcw], writes=[By[yi]])
                for cc, r, ub, ui, yi in pair:
                    S.op("dve", lambda e, ui=ui, yi=yi, cc=cc: e.scalar_tensor_tensor(out=y[yi][:], in0=U[ui][:, 1:TT + 1], scalar=cw[:, 1, cc:cc + 1],
                                                                                      in1=y[yi][:], op0=ALU.mult, op1=ALU.add),
                         reads=[BU[ui], Bcw, By[yi]], writes=[By[yi]])
                for cc, r, ub, ui, yi in pair:
                    S.op("dve", lambda e, ui=ui, yi=yi, cc=cc: e.scalar_tensor_tensor(out=y[yi][:], in0=U[ui][:, 0:TT], scalar=cw[:, 0, cc:cc + 1],
                                                                                      in1=y[yi][:], op0=ALU.mult, op1=ALU.add),
                         reads=[BU[ui], Bcw, By[yi]], writes=[By[yi]])
                si = k % 2
                S.op("act", lambda e, si=si, yg=ys[0]: e.activation(out=sg[si][:], in_=y[yg][:], func=AF.Silu), reads=[By[ys[0]]], writes=[Bsg[si]])
                S.op("dve", lambda e, si=si, yu=ys[1], k=k: e.tensor_tensor(out=aT[:, k, :], in0=sg[si][:], in1=y[yu][:], op=ALU.mult),
                     reads=[Bsg[si], By[ys[1]]], writes=[BaT[k]])
            if t + 1 < NT:
                prologue(t + 1)
            for blk in range(2):
                tb = 2 * t + blk
                for hf in range(2):
                    for k in range(22):
                        S.op("pe", lambda e, k=k, hf=hf, blk=blk: e.matmul(wd_ps[hf][:, :], lhsT=aT[:, k, blk * 128:(blk + 1) * 128],
                                                                           rhs=wdn[:, k, hf * 512:(hf + 1) * 512], start=(k == 0), stop=(k == 21)),
                             reads=[BaT[k], Bwdn], writes=[Bwdps[hf]])
                    S.op("dve", lambda e, hf=hf, blk=blk, xs=xs: e.tensor_tensor(out=xio[xs][:, blk, hf * 512:(hf + 1) * 512],
                                                                                 in0=xio[xs][:, blk, hf * 512:(hf + 1) * 512],
                                                                                 in1=wd_ps[hf][:, :], op=ALU.add),
                         reads=[Bxio[xs][blk], Bwdps[hf]], writes=[Bxio[xs][blk]])
                S.dma("sp", lambda e, tb=tb, xs=xs, blk=blk: e.dma_start(out=C.x2_d[tb * 128:(tb + 1) * 128, :], in_=xio[xs][:, blk, :]),
                      reads=[Bxio[xs][blk]], writes=[Buf()], owner=Bxio[xs][blk])
        S.emit_phase()
        S.release([Bmixt[0]] + Bxio[0] + Bxio[1])
    if DBG:
        with nc.sbuf_tensor("dx0", [128, D], F32) as dx0, nc.sbuf_tensor("dx1", [128, D], F32) as dx1:
            dx = [dx0, dx1]
            Bd = [Buf(), Buf()]
            for i in range(OWN // 128):
                k = i % 2
                S.dma("sp", lambda e, i=i, k=k: e.dma_start(out=dx[k][:], in_=C.x2_d[i * 128:(i + 1) * 128, :]), writes=[Bd[k]], owner=Bd[k])
                S.dma("sp", lambda e, i=i, k=k: e.dma_start(out=C.dbg["x2"][i * 128:(i + 1) * 128, :], in_=dx[k][:]), reads=[Bd[k]], writes=[Buf()],
                      owner=Bd[k])
            S.emit_phase()
            S.release(Bd)


def phaseC2(C):
    nc, S = C.nc, C.S
    NBK = OWN // 128
    with ExitStack() as es:
        sb = lambda n, s, d: es.enter_context(nc.sbuf_tensor(n, s, d))
        pst = lambda n, s, d: es.enter_context(nc.psum_tensor(n, s, d))
        wg = sb("wg", [128, 8, D], BF16)
        wp = sb("wp", [128, 2, D], BF16)
        gpl = sb("gpl", [128, D], F32)
        gfin = sb("gfin", [128, D], F32)
        x2t = [sb(f"x2t{i}", [128, D], F32) for i in range(2)]
        pt = [sb(f"pt{i}", [128, 256], F32) for i in range(2)]
        pb = [sb(f"pb{i}", [128, 256], BF16) for i in range(2)]
        pT = [sb(f"pT{i}", [128, 2, 128], BF16) for i in range(2)]
        sqj = sb("sqjD", [128, D], BF16)
        sqj2 = sb("sqjD2", [128, D], BF16)
        Bsq2 = Buf()
        ss = [sb(f"ssD{i}", [128, 1], F32) for i in range(2)]
        rstd = [sb(f"rstdD{i}", [128, 1], F32) for i in range(2)]
        ss2 = [sb(f"ssE{i}", [128, 1], F32) for i in range(2)]
        rstd2 = [sb(f"rstdE{i}", [128, 1], F32) for i in range(2)]
        h3 = [sb(f"h3{i}", [128, D], BF16) for i in range(2)]
        h3T = [sb(f"h3T{i}", [128, 8, 128], BF16) for i in range(2)]
        gate = [sb(f"gate{i}", [128, D], F32) for i in range(2)]
        x3 = [sb(f"x3{i}", [128, D], F32) for i in range(2)]
        ot = [sb(f"ot{i}", [128, D], F32) for i in range(2)]
        tp_ps = pst("tpD", [128, 1024], BF16)
        pT_ps = pst("pTD", [128, 1024], BF16)
        g_ps = [pst(f"gps{i}", [128, 512], F32) for i in range(2)]
        pp_ps = [pst(f"ppps{i}", [128, 512], F32) for i in range(2)]
        Bwg, Bwp, Bgpl, Bgfin, Bsq, Btp, BpTps = (Buf() for _ in range(7))
        mk = lambda: [Buf() for _ in range(2)]
        Bx2t, Bpt, Bpb, BpT, Bss, Brs, Bss2, Brs2, Bh3, Bh3T, Bgate, Bx3, Bot, Bgps, Bppps = (mk() for _ in range(15))
        g = S.new_group("pC2init")
        gw = S.new_group("pC2w")
        S.dma("sp", lambda e: e.dma_start(out=gpl[:], in_=C.ple_norm.partition_broadcast(128)), writes=[Bgpl], group=g)
        S.dma("sp", lambda e: e.dma_start(out=gfin[:], in_=C.final_norm.partition_broadcast(128)), writes=[Bgfin], group=g)
        S.dma("sp", lambda e: e.dma_start(out=wg[:], in_=C.wgb_d.rearrange("(m p) c -> p m c", p=128)), writes=[Bwg], group=gw)
        S.dma("sp", lambda e: e.dma_start(out=wp[:], in_=C.wpb_d.rearrange("(m p) c -> p m c", p=128)), writes=[Bwp], group=gw)
        def load_c2(tb):
            s = tb % 2
            S.dma("sp", lambda e: e.dma_start(out=x2t[s][:], in_=C.x2_d[tb * 128:(tb + 1) * 128, :]), writes=[Bx2t[s]], owner=Bx2t[s])
            S.dma("sp", lambda e: e.dma_start(out=pt[s][:], in_=C.p_own[tb * 128:(tb + 1) * 128, :]), writes=[Bpt[s]], owner=Bpt[s])

        def x_norm(tb):
            s = tb % 2
            rms_to_bf16(S, x2t[s][:], Bx2t[s], sqj, Bsq, ss[s], Bss[s], rstd[s], Brs[s], gpl, Bgpl, h3[s][:], Bh3[s])
            S.op("pool", lambda e, s=s: e.tensor_copy(out=pb[s][:], in_=pt[s][:]), reads=[Bpt[s]], writes=[Bpb[s]])

        def x_transposes(tb):
            s = tb % 2
            for m in range(8):
                S.op("pe", lambda e, m=m, s=s: e.transpose(out=tp_ps[:, m * 128:(m + 1) * 128], in_=h3[s][:, m * 128:(m + 1) * 128],
                                                           identity=C.ident[:]), reads=[Bh3[s], C.Bident], writes=[Btp])
            for m in range(2):
                S.op("pe", lambda e, m=m, s=s: e.transpose(out=pT_ps[:, m * 128:(m + 1) * 128], in_=pb[s][:, m * 128:(m + 1) * 128],
                                                           identity=C.ident[:]), reads=[Bpb[s], C.Bident], writes=[BpTps])

        def x_copies(tb):
            s = tb % 2
            S.op("act", lambda e, s=s: e.activation(out=h3T[s][:].rearrange("p m t -> p (m t)"), in_=tp_ps[:], func=AF.Copy),
                 reads=[Btp], writes=[Bh3T[s]])
            S.op("act", lambda e, s=s: e.activation(out=pT[s][:].rearrange("p m t -> p (m t)"), in_=pT_ps[:, 0:256], func=AF.Copy),
                 reads=[BpTps], writes=[BpT[s]])

        load_c2(0)
        load_c2(1)
        x_norm(0)
        x_transposes(0)
        x_copies(0)
        for tb in range(NBK):
            s = tb % 2
            nxt = tb + 1 < NBK
            for hf in range(2):
                for m in range(8):
                    S.op("pe", lambda e, m=m, hf=hf, s=s: e.matmul(g_ps[hf][:, :], lhsT=h3T[s][:, m, :], rhs=wg[:, m, hf * 512:(hf + 1) * 512],
                                                                   start=(m == 0), stop=(m == 7)), reads=[Bh3T[s], Bwg], writes=[Bgps[hf]])
                for m in range(2):
                    S.op("pe", lambda e, m=m, hf=hf, s=s: e.matmul(pp_ps[hf][:, :], lhsT=pT[s][:, m, :], rhs=wp[:, m, hf * 512:(hf + 1) * 512],
                                                                   start=(m == 0), stop=(m == 1)), reads=[BpT[s], Bwp], writes=[Bppps[hf]])
            if nxt:
                x_norm(tb + 1)
                x_transposes(tb + 1)
            for hf in range(2):
                S.op("act", lambda e, hf=hf, s=s: e.activation(out=gate[s][:, hf * 512:(hf + 1) * 512], in_=g_ps[hf][:, :], func=AF.Sigmoid),
                     reads=[Bgps[hf]], writes=[Bgate[s]])
                S.op("dve", lambda e, hf=hf, s=s: e.tensor_tensor(out=gate[s][:, hf * 512:(hf + 1) * 512], in0=gate[s][:, hf * 512:(hf + 1) * 512],
                                                                  in1=pp_ps[hf][:, :], op=ALU.mult), reads=[Bgate[s], Bppps[hf]], writes=[Bgate[s]])
            if nxt:
                x_copies(tb + 1)
            S.op("pool", lambda e, s=s: e.tensor_tensor(out=x3[s][:], in0=gate[s][:], in1=x2t[s][:], op=ALU.add),
                 reads=[Bgate[s], Bx2t[s]], writes=[Bx3[s]])
            if tb + 2 < NBK:
                load_c2(tb + 2)
            S.op("act", lambda e, s=s: e.activation(out=sqj2[:], in_=x3[s][:], func=AF.Square, accum_out=ss2[s][:, 0:1]),
                 reads=[Bx3[s]], writes=[Bsq2, Bss2[s]])
            S.op("act", lambda e, s=s: e.activation(out=rstd2[s][:], in_=ss2[s][:], func=AF.Sqrt, scale=1.0 / D, bias=RMS_EPS),
                 reads=[Bss2[s]], writes=[Brs2[s]])
            S.op("dve", lambda e, s=s: e.reciprocal(out=rstd2[s][:], in_=rstd2[s][:]), reads=[Brs2[s]], writes=[Brs2[s]])
            S.op("dve", lambda e, s=s: e.scalar_tensor_tensor(out=ot[s][:], in0=x3[s][:], scalar=rstd2[s][:, 0:1], in1=gfin[:],
                                                              op0=ALU.mult, op1=ALU.mult), reads=[Bx3[s], Brs2[s], Bgfin], writes=[Bot[s]])
            S.dma("sp", lambda e, tb=tb, s=s: e.dma_start(out=C.out[tb * 128:(tb + 1) * 128, :], in_=ot[s][:]), reads=[Bot[s]], writes=[Buf()],
                  owner=Bot[s])
        S.emit_phase()


def _amask_const():
    s = np.arange(128)[:, None]
    qq = np.arange(128)[None, :]
    out = np.zeros((128, NQ, 128), np.float32)
    for r in range(NQ):
        delta = (16 - r) * 128 + qq - s
        m = ((delta >= 0) & (delta <= 128)).astype(np.float32)
        m += ((delta >= 0) & (delta <= 512) & (delta % 4 == 0)).astype(np.float32)
        m += ((delta >= 0) & (delta <= 2048) & (delta % 16 == 0)).astype(np.float32)
        out[:, r, :] = m
    return out.reshape(128, NQ * 128)


def make_in_maps(x, p, positions, attn_norm, w_in, w_o, ffn_norm, w_up, conv_w, conv_b, w_down, ple_norm,
                 w_ple_gate, w_ple_proj, final_norm):
    f32 = lambda a: np.ascontiguousarray(np.asarray(a, dtype=np.float32))
    x = f32(x)
    p = f32(p)
    positions = np.asarray(positions).astype(np.int32)
    invf = (1.0 / (10000.0 ** (np.arange(0, 64, 2, dtype=np.float32) / np.float32(64)))).astype(np.float32)[None]
    amask = _amask_const()
    shared = dict(invf=invf, amask=amask, w_in=f32(w_in[0]), w_o=f32(w_o[0]), w_up=f32(w_up[0]), w_down=f32(w_down[0]),
                  w_g=f32(w_ple_gate[0]), w_p=f32(w_ple_proj[0]), attn_norm=f32(attn_norm[0:1]), ffn_norm=f32(ffn_norm[0:1]),
                  ple_norm=f32(ple_norm[0:1]), final_norm=f32(final_norm[None]), conv_w=f32(conv_w[0]), conv_b=f32(conv_b[0:1]))
    maps = []
    for c in range(NCORE):
        b, q = c // 4, c % 4
        t_lo = OWN * q - (NB_A - 16) * 128 + 0
        t_lo = OWN * q - 2176
        idx = np.arange(t_lo, t_lo + NB_A * 128)
        valid = idx >= 0
        xa = np.zeros((NB_A * 128, D), np.float32)
        xa[valid] = x[b, idx[valid]]
        posa = np.zeros(NB_A * 128, np.int32)
        posa[valid] = positions[b, idx[valid]]
        pos = np.concatenate([positions[b].reshape(NB_ALL, 128).T, posa.reshape(NB_A, 128).T], axis=1)
        avalid = valid.astype(np.float32).reshape(NB_A, 128).T
        tqi = idx[(NB_A - NQ) * 128:].astype(np.float32)
        tqi = np.where(tqi >= 0, tqi, -1.0).reshape(NQ, 128).T
        m = dict(shared)
        m.update(xall=x[b], xa=xa, pos=np.ascontiguousarray(pos), avalid=np.ascontiguousarray(avalid),
                 tq=np.ascontiguousarray(tqi.astype(np.float32)), p_own=np.ascontiguousarray(p[0, b, OWN * q:OWN * (q + 1)]))
        maps.append(m)
    return maps


_NC_CACHE = {}


def kernel(**inputs):
    if "nc" not in _NC_CACHE:
        _NC_CACHE["nc"] = build_program()
    nc = _NC_CACHE["nc"]
    in_maps = make_in_maps(**inputs)
    res = run_bass_kernel_spmd(nc, in_maps, core_ids=list(range(NCORE)))
    outs = [r["out"] for r in res.results]
    full = np.stack(outs, 0).reshape(2, T, D).astype(np.float32)
    if DBG:
        kernel.last = res.results
    return full
```

```python
import os
from contextlib import ExitStack

import numpy as np
import concourse.bass as bass
import concourse.mybir as mybir
from concourse.bass_utils import run_bass_kernel_spmd

F32 = mybir.dt.float32
BF16 = mybir.dt.bfloat16
I32 = mybir.dt.int32
AF = mybir.ActivationFunctionType
ALU = mybir.AluOpType

ENGS = ("pe", "act", "dve", "pool", "sp")


class Buf:
    __slots__ = ("name", "last_w", "readers", "sem", "semcnt")

    def __init__(self, name=""):
        self.name = name
        self.last_w = None
        self.readers = []
        self.sem = None
        self.semcnt = 0


class Op:
    __slots__ = ("eng", "fn", "deps", "is_dma", "owner", "needs_inc", "semval", "phase", "group", "owner_sem")

    def __init__(self, eng, fn):
        self.eng = eng
        self.fn = fn
        self.deps = []
        self.is_dma = False
        self.owner = None
        self.needs_inc = False
        self.semval = None
        self.group = None
        self.phase = 0


class Group:
    def __init__(self, name):
        self.name = name
        self.sem = None
        self.base = 0
        self.n = 0


class Sched:
    def __init__(self, nc, sems):
        self.nc = nc
        self.free_sems = list(sems)
        self.ops = []
        self.eng_sem = {}
        self.eng_cnt = {}
        for e in ("pe", "act", "dve", "pool"):
            self.eng_sem[e] = self.free_sems.pop()
            self.eng_cnt[e] = 0
        self.dma_bufs = []
        self.groups = []
        self.phase = 0
        self.free_dma = []

    def _track(self, op, reads, writes):
        deps = []
        op.phase = self.phase
        for b in reads:
            if b.last_w is not None:
                deps.append(b.last_w)
        for b in writes:
            if b.last_w is not None:
                deps.append(b.last_w)
            for r in b.readers:
                deps.append(r)
        for b in writes:
            b.last_w = op
            b.readers = []
        for b in reads:
            if not op.is_dma:
                b.readers = [r for r in b.readers if r.is_dma or r.eng != op.eng]
            b.readers.append(op)
        seen = set()
        out = []
        for d in deps:
            if d is op or id(d) in seen or d.phase != self.phase:
                continue
            if op.is_dma and d.is_dma and op.group is not None and d.group is op.group:
                continue
            seen.add(id(d))
            out.append(d)
        op.deps = out

    def op(self, eng, fn, reads=(), writes=()):
        o = Op(eng, fn)
        self._track(o, reads, writes)
        self.ops.append(o)
        return o

    def dma(self, eng, fn, reads=(), writes=(), owner=None, group=None):
        o = Op(eng, fn)
        o.is_dma = True
        o.owner = owner
        o.group = group
        assert (owner is None) != (group is None)
        self._track(o, reads, writes)
        self.ops.append(o)
        return o

    def new_group(self, name):
        g = Group(name)
        g.sem = self.free_sems.pop()
        self.groups.append(g)
        return g

    def release(self, bufs):
        for b in bufs:
            if b.sem is not None:
                self.dma_bufs.remove(b)
                self.free_dma.append((b.sem, b.semcnt))
                b.sem = None

    def emit_phase(self):
        nc = self.nc
        ops = self.ops
        self.ops = []
        for o in ops:
            for d in o.deps:
                if d.is_dma:
                    continue
                if d.eng == o.eng and o.eng == "pe":
                    continue
                d.needs_inc = True
        for o in ops:
            if o.is_dma:
                if o.group is not None:
                    o.group.n += 1
                else:
                    b = o.owner
                    if b.sem is None:
                        if self.free_dma:
                            b.sem, b.semcnt = self.free_dma.pop()
                        else:
                            b.sem = self.free_sems.pop()
                            b.semcnt = 0
                        self.dma_bufs.append(b)
                    b.semcnt += 16
                    o.semval = b.semcnt
            elif o.needs_inc:
                self.eng_cnt[o.eng] += 1
                o.semval = self.eng_cnt[o.eng]
        per = {e: [] for e in ENGS}
        for o in ops:
            per[o.eng].append(o)
        waited = {e: {} for e in ENGS}
        final_waits = [(b.sem, b.semcnt) for b in self.dma_bufs]
        for g in self.groups:
            if g.n:
                final_waits.append((g.sem, g.base + 16 * g.n))

        def run(eng_name, eng):
            w = waited[eng_name]
            for o in per[eng_name]:
                for d in o.deps:
                    if d.is_dma:
                        if d.group is not None:
                            sem, val = d.group.sem, d.group.base + 16 * d.group.n
                        else:
                            sem, val = d.owner_sem, d.semval
                    else:
                        if d.eng == eng_name and eng_name == "pe":
                            continue
                        sem, val = self.eng_sem[d.eng], d.semval
                    key = id(sem)
                    if w.get(key, 0) >= val:
                        continue
                    w[key] = val
                    eng.wait_ge(sem, val)
                ins = o.fn(eng)
                if o.is_dma:
                    ins.then_inc(o.group.sem if o.group is not None else o.owner_sem, 16)
                elif o.needs_inc:
                    ins.then_inc(self.eng_sem[o.eng], 1)
            if eng_name == "sp":
                for sem, val in final_waits:
                    key = id(sem)
                    if w.get(key, 0) >= val:
                        continue
                    w[key] = val
                    eng.wait_ge(sem, val)

        for o in ops:
            if o.is_dma and o.group is None:
                o.owner_sem = o.owner.sem

        with nc.Block() as block:
            @block.tensor
            def _(e):
                run("pe", e)

            @block.scalar
            def _(e):
                run("act", e)

            @block.vector
            def _(e):
                run("dve", e)

            @block.gpsimd
            def _(e):
                run("pool", e)

            @block.sync
            def _(e):
                run("sp", e)
        for g in self.groups:
            g.base += 16 * g.n
            g.n = 0
        self.phase += 1


T = 8192
D = 1024
NCORE = 8
OWN = 2048
NB_ALL = 64
NB_A = 33
NQ = 17
IN_COLS = 3656
D_FF = 2816
NCH = 44
C_QA, C_KA, C_VA, C_QB, C_KB, C_VB, C_QI, C_KI, C_WI = 0, 512, 1024, 1536, 2048, 2560, 3072, 3584, 3648
RMS_EPS = 1e-6
BIG = 1.0e30
NBIS = 18
BIS_B = 16.0
DBG = os.environ.get("MK_DBG", "")
STOP_AFTER = os.environ.get("MK_STOP", "")


class Ctx:
    pass


def build_program():
    nc = bass.Bass("TRN2", target_bir_lowering=False)
    C = Ctx()
    C.nc = nc
    din = lambda name, shape, dt=F32: nc.dram_tensor(name, shape, dt, kind="ExternalInput").ap()
    C.xall = din("xall", [T, D])
    C.xa = din("xa", [NB_A * 128, D])
    C.pos = din("pos", [128, NB_ALL + NB_A], I32)
    C.avalid = din("avalid", [128, NB_A])
    C.tq = din("tq", [128, NQ])
    C.invf = din("invf", [1, 32])
    C.amask = din("amask", [128, NQ * 128])
    C.p_own = din("p_own", [OWN, 256])
    C.w_in = din("w_in", [D, IN_COLS])
    C.w_o = din("w_o", [D, D])
    C.w_up = din("w_up", [D, 2 * D_FF])
    C.w_down = din("w_down", [D_FF, D])
    C.w_g = din("w_g", [D, D])
    C.w_p = din("w_p", [256, D])
    C.attn_norm = din("attn_norm", [1, D])
    C.ffn_norm = din("ffn_norm", [1, D])
    C.ple_norm = din("ple_norm", [1, D])
    C.final_norm = din("final_norm", [1, D])
    C.conv_w = din("conv_w", [3, 2 * D_FF])
    C.conv_b = din("conv_b", [1, 2 * D_FF])
    C.out = nc.dram_tensor("out", [OWN, D], F32, kind="ExternalOutput").ap()
    C.vb_d = nc.dram_tensor("vb_d", [NB_ALL, 128, 520], BF16).ap()
    C.va_d = nc.dram_tensor("va_d", [NB_A, 128, 520], BF16).ap()
    C.kaT_d = nc.dram_tensor("kaT_d", [NB_A, 128, 512], BF16).ap()
    C.qaT_d = nc.dram_tensor("qaT_d", [NQ, 128, 512], BF16).ap()
    C.qbT_d = nc.dram_tensor("qbT_d", [NQ, 128, 512], BF16).ap()
    C.qiT_d = nc.dram_tensor("qiT_d", [NQ, 128, 512], BF16).ap()
    C.lohi_d = nc.dram_tensor("lohi_d", [NQ, 128, 16], F32).ap()
    C.mix_d = nc.dram_tensor("mix_d", [NQ, 128, 1024], BF16).ap()
    C.x2_d = nc.dram_tensor("x2_d", [OWN, D], F32).ap()
    C.wob_d = nc.dram_tensor("wob_d", [D, D], BF16).ap()
    C.wupb_d = nc.dram_tensor("wupb_d", [D, 2 * D_FF], BF16).ap()
    C.wdnb_d = nc.dram_tensor("wdnb_d", [D_FF, D], BF16).ap()
    C.wgb_d = nc.dram_tensor("wgb_d", [D, D], BF16).ap()
    C.wpb_d = nc.dram_tensor("wpb_d", [256, D], BF16).ap()
    C.dbg = {}
    if DBG:
        dout = lambda name, shape, dt=F32: nc.dram_tensor(name, shape, dt, kind="ExternalOutput").ap()
        C.dbg["kbT"] = dout("dbg_kbT", [128, NB_ALL * 512], BF16)
        C.dbg["kiT"] = dout("dbg_kiT", [128, T], BF16)
        C.dbg["mix"] = dout("dbg_mix", [NQ, 128, 1024], BF16)
        C.dbg["x2"] = dout("dbg_x2", [OWN, D])

    with ExitStack() as top:
        sems = [top.enter_context(nc.semaphore(f"s{i}")) for i in range(96)]
        S = Sched(nc, sems)
        C.S = S
        C.ident = top.enter_context(nc.sbuf_tensor("ident", [128, 128], BF16))
        C.identf = top.enter_context(nc.sbuf_tensor("identf", [128, 128], F32))
        C.Bident = Buf("ident")
        with ExitStack() as kv:
            C.kbT = kv.enter_context(nc.sbuf_tensor("kbT", [128, NB_ALL, 512], BF16))
            C.kiT = kv.enter_context(nc.sbuf_tensor("kiT", [128, T], BF16))
            C.BkbT = [Buf(f"kbT{i}") for i in range(NB_ALL)]
            C.BkiT = [Buf(f"kiT{i}") for i in range(NB_ALL)]
            phase1(C)
            if STOP_AFTER != "1":
                phaseA(C)
            if STOP_AFTER not in ("1", "A"):
                phaseB(C)
        if STOP_AFTER not in ("1", "A", "B"):
            phaseC1(C)
            if STOP_AFTER != "C1":
                phaseC2(C)
        if STOP_AFTER:
            final_dummy(C)
    return nc


def final_dummy(C):
    nc, S = C.nc, C.S
    with nc.sbuf_tensor("zz", [128, 1024], F32) as zz:
        Bz = Buf()
        S.op("dve", lambda e: e.memset(zz[:], 0.0), writes=[Bz])
        for i in range(16):
            S.dma("sp", lambda e, i=i: e.dma_start(out=C.out[i * 128:(i + 1) * 128, :], in_=zz[:]), reads=[Bz], owner=Bz)
        S.emit_phase()


def phase1(C):
    nc, S = C.nc, C.S
    NTB = NB_ALL + NB_A
    with ExitStack() as es:
        sb = lambda n, s, d: es.enter_context(nc.sbuf_tensor(n, s, d))
        pst = lambda n, s, d: es.enter_context(nc.psum_tensor(n, s, d))
        win = sb("win", [128, 8, IN_COLS], BF16)
        gat = sb("gat", [128, D], F32)
        posi = sb("posi", [128, NTB], I32)
        posf = sb("posf", [128, NTB], F32)
        avl = sb("avl", [128, NB_A], F32)
        invt = sb("invt", [128, 32], F32)
        cosT = sb("cosT", [128, NTB, 32], F32)
        sinT = sb("sinT", [128, NTB, 32], F32)
        identf = C.identf
        ones8 = sb("ones8", [128, 8], F32)
        xt = [sb(f"xt{i}", [128, D], F32) for i in range(2)]
        sqj = sb("sqj", [128, D], BF16)
        ss = [sb(f"ss{i}", [128, 1], F32) for i in range(2)]
        rstd = [sb(f"rstd{i}", [128, 1], F32) for i in range(2)]
        hb = [sb(f"hb{i}", [128, D], BF16) for i in range(2)]
        hT = [sb(f"hT{i}", [128, 8, 128], BF16) for i in range(2)]
        ta = [sb(f"ta{i}", [128, 512], F32) for i in range(2)]
        tb_ = [sb(f"tb{i}", [128, 512], F32) for i in range(2)]
        zb = [sb(f"zb{i}", [128, 512], BF16) for i in range(6)]
        zTs = [sb(f"zTs{i}", [128, 512], BF16) for i in range(3)]
        vsb = [sb(f"vsb{i}", [128, 520], BF16) for i in range(2)]
        vsa = [sb(f"vsa{i}", [128, 520], BF16) for i in range(2)]
        wsc = [sb(f"wsc{i}", [128, 8], F32) for i in range(2)]
        lohi = [sb(f"lohi{i}", [128, 16], F32) for i in range(2)]
        hT_ps = [pst(f"hTps{i}", [128, 1024], BF16) for i in range(2)]
        z_ps = [pst(f"zps{i}", [128, 512], F32) for i in range(4)]
        zT_ps = [pst(f"zTps{i}", [128, 1024], BF16) for i in range(2)]

        Bwin, Bgat, Bpos, Bavl, Binv, Bcs, Bidf, Bones = (Buf() for _ in range(8))
        Bxt = [Buf() for _ in range(2)]
        Bsq = Buf()
        Bss = [Buf() for _ in range(2)]
        Brs = [Buf() for _ in range(2)]
        Bhb = [Buf() for _ in range(2)]
        BhT = [Buf() for _ in range(2)]
        Bta = [Buf() for _ in range(2)]
        Btb = [Buf() for _ in range(2)]
        Bzb = [Buf() for _ in range(6)]
        BzTs = [Buf() for _ in range(3)]
        Bvsb = [Buf() for _ in range(2)]
        Bvsa = [Buf() for _ in range(2)]
        Bwsc = [Buf() for _ in range(2)]
        Blohi = [Buf() for _ in range(2)]
        BhTps = [Buf() for _ in range(2)]
        Bzps = [Buf() for _ in range(4)]
        BzTps = [Buf() for _ in range(2)]

        g0 = S.new_group("p1init")
        gw = S.new_group("p1w")
        S.dma("sp", lambda e: e.dma_start(out=posi[:], in_=C.pos), writes=[Bpos], group=g0)
        S.dma("sp", lambda e: e.dma_start(out=avl[:], in_=C.avalid), writes=[Bavl], group=g0)
        S.dma("sp", lambda e: e.dma_start(out=invt[:], in_=C.invf.partition_broadcast(128)), writes=[Binv], group=g0)
        S.dma("sp", lambda e: e.dma_start(out=gat[:], in_=C.attn_norm.partition_broadcast(128)), writes=[Bgat], group=g0)
        gw2 = S.new_group("p1w2")
        Bwin2 = Buf()
        for (c0, c1, grp, bw) in ((2048, IN_COLS, gw, Bwin), (0, 2048, gw2, Bwin2)):
            for m in range(8):
                S.dma("pool", lambda e, m=m, c0=c0, c1=c1: e.dma_start(out=win[:, m, c0:c1], in_=C.w_in[m * 128:(m + 1) * 128, c0:c1]),
                      writes=[bw], group=grp)
        S.op("pool", lambda e: e.memset(identf[:], 0.0), writes=[Bidf])
        S.op("pool", lambda e: e.affine_select(out=identf[:], in_=identf[:], compare_op=ALU.not_equal, fill=1.0,
                                               base=0, pattern=[[-1, 128]], channel_multiplier=1), reads=[Bidf], writes=[Bidf])
        S.op("dve", lambda e: e.tensor_copy(out=C.ident[:], in_=identf[:]), reads=[Bidf], writes=[C.Bident])
        S.op("dve", lambda e: e.memset(ones8[:], 1.0), writes=[Bones])
        for i in range(2):
            S.op("dve", lambda e, i=i: e.memset(vsb[i][:], 1.0), writes=[Bvsb[i]])
            S.op("dve", lambda e, i=i: e.memset(vsa[i][:], 1.0), writes=[Bvsa[i]])
        cnt = {"z": 0, "zT": 0, "zTs": 0, "vsb": 0, "vsa": 0, "zb": 0, "t": 0}

        def mm_group(s, c0, n):
            r = cnt["z"] % 4
            cnt["z"] += 1
            for m in range(8):
                S.op("pe", lambda e, m=m, r=r, s=s: e.matmul(z_ps[r][:, 0:n], lhsT=hT[s][:, m, :], rhs=win[:, m, c0:c0 + n],
                                                             start=(m == 0), stop=(m == 7)),
                     reads=[BhT[s], Bwin if c0 >= 2048 else Bwin2], writes=[Bzps[r]])
            return r

        def rope(r, n, tbk):
            zi = cnt["zb"] % 6
            cnt["zb"] += 1
            ti = cnt["t"] % 2
            cnt["t"] += 1
            H = n // 64
            zv = z_ps[r][:, 0:n].rearrange("p (h t d) -> p h t d", h=H, t=2)
            A4 = ta[ti][:, 0:n].rearrange("p (h t d) -> p h t d", h=H, t=2)
            B4 = tb_[ti][:, 0:n].rearrange("p (h t d) -> p h t d", h=H, t=2)
            Z4 = zb[zi][:, 0:n].rearrange("p (h t d) -> p h t d", h=H, t=2)
            cosb = cosT[:, tbk, :].unsqueeze(1).unsqueeze(1).to_broadcast([128, H, 2, 32])
            sinb = sinT[:, tbk, :].unsqueeze(1).to_broadcast([128, H, 32])
            S.op("dve", lambda e: e.tensor_tensor(out=A4, in0=zv, in1=cosb, op=ALU.mult), reads=[Bzps[r], Bcs], writes=[Bta[ti]])
            S.op("dve", lambda e: e.tensor_tensor(out=B4[:, :, 0, :], in0=zv[:, :, 1, :], in1=sinb, op=ALU.mult),
                 reads=[Bzps[r], Bcs], writes=[Btb[ti]])
            S.op("dve", lambda e: e.tensor_tensor(out=B4[:, :, 1, :], in0=zv[:, :, 0, :], in1=sinb, op=ALU.mult),
                 reads=[Bzps[r], Bcs], writes=[Btb[ti]])
            S.op("pool", lambda e: e.tensor_tensor(out=Z4[:, :, 0, :], in0=A4[:, :, 0, :], in1=B4[:, :, 0, :], op=ALU.subtract),
                 reads=[Bta[ti], Btb[ti]], writes=[Bzb[zi]])
            S.op("pool", lambda e: e.tensor_tensor(out=Z4[:, :, 1, :], in0=A4[:, :, 1, :], in1=B4[:, :, 1, :], op=ALU.add),
                 reads=[Bta[ti], Btb[ti]], writes=[Bzb[zi]])
            return zi

        pending = []

        def transposeT(zi, ncols, dst_fn, dst_bufs_w, then=None):
            def run():
                q = cnt["zT"] % 2
                cnt["zT"] += 1
                nt = ncols // 128
                for j in range(nt):
                    S.op("pe", lambda e, j=j, q=q: e.transpose(out=zT_ps[q][:, j * 128:(j + 1) * 128], in_=zb[zi][:, j * 128:(j + 1) * 128],
                                                               identity=C.ident[:]),
                         reads=[Bzb[zi], C.Bident], writes=[BzTps[q]])
                S.op("act", lambda e, q=q: e.activation(out=dst_fn(), in_=zT_ps[q][:, 0:ncols], func=AF.Copy),
                     reads=[BzTps[q]], writes=dst_bufs_w)
                if then is not None:
                    then()
            pending.append(run)

        def to_dram_T(zi, dram_ap):
            k = cnt["zTs"] % 3
            cnt["zTs"] += 1
            transposeT(zi, 512, lambda k=k: zTs[k][:], [BzTs[k]],
                       then=lambda k=k: S.dma("sp", lambda e, k=k: e.dma_start(out=dram_ap, in_=zTs[k][:]), reads=[BzTs[k]], writes=[Buf()],
                                              owner=BzTs[k]))

        def vcopy(r, dst, Bdst):
            S.op("act", lambda e: e.activation(out=dst[:].rearrange("p (h c) -> p h c", c=65)[:, :, 0:64],
                                               in_=z_ps[r][:, :].rearrange("p (h d) -> p h d", d=64), func=AF.Copy),
                 reads=[Bzps[r]], writes=[Bdst])

        def load_x(tbk):
            s = tbk % 2
            is_all = tbk < NB_ALL
            blk = tbk if is_all else tbk - NB_ALL
            src = C.xall if is_all else C.xa
            S.dma("sp", lambda e: e.dma_start(out=xt[s][:], in_=src[blk * 128:(blk + 1) * 128, :]), writes=[Bxt[s]], owner=Bxt[s])

        def stageA(tbk):
            s = tbk % 2
            if tbk + 1 < NTB:
                load_x(tbk + 1)
            S.op("act", lambda e, s=s: e.activation(out=sqj[:], in_=xt[s][:], func=AF.Square, accum_out=ss[s][:, 0:1]),
                 reads=[Bxt[s]], writes=[Bsq, Bss[s]])
            S.op("act", lambda e, s=s: e.activation(out=rstd[s][:], in_=ss[s][:], func=AF.Sqrt, scale=1.0 / D, bias=RMS_EPS),
                 reads=[Bss[s]], writes=[Brs[s]])
            S.op("dve", lambda e, s=s: e.reciprocal(out=rstd[s][:], in_=rstd[s][:]), reads=[Brs[s]], writes=[Brs[s]])
            S.op("dve", lambda e, s=s: e.scalar_tensor_tensor(out=hb[s][:], in0=xt[s][:], scalar=rstd[s][:, 0:1], in1=gat[:],
                                                              op0=ALU.mult, op1=ALU.mult),
                 reads=[Bxt[s], Brs[s], Bgat], writes=[Bhb[s]])
            for m in range(8):
                S.op("pe", lambda e, s=s, m=m: e.transpose(out=hT_ps[s][:, m * 128:(m + 1) * 128], in_=hb[s][:, m * 128:(m + 1) * 128],
                                                           identity=C.ident[:]),
                     reads=[Bhb[s], C.Bident], writes=[BhTps[s]])
            S.op("act", lambda e, s=s: e.activation(out=hT[s][:].rearrange("p m t -> p (m t)"), in_=hT_ps[s][:], func=AF.Copy),
                 reads=[BhTps[s]], writes=[BhT[s]])

        load_x(0)
        stageA(0)
        S.op("dve", lambda e: e.tensor_copy(out=posf[:], in_=posi[:]), reads=[Bpos], writes=[Bpos])
        MAGIC = 12582912.0
        TWO_PI = float(2 * np.pi)
        CH = 16
        for b0 in range(0, NTB, CH):
            nb_ = min(CH, NTB - b0)
            ang = ta[0][:, 0:nb_ * 32].rearrange("p (b d) -> p b d", d=32)
            kk = tb_[0][:, 0:nb_ * 32].rearrange("p (b d) -> p b d", d=32)
            Bang, Bkk = Bta[0], Btb[0]
            S.op("dve", lambda e, ang=ang, b0=b0, nb_=nb_: e.tensor_tensor(
                out=ang, in0=posf[:, b0:b0 + nb_].unsqueeze(2).to_broadcast([128, nb_, 32]),
                in1=invt[:, :].unsqueeze(1).to_broadcast([128, nb_, 32]), op=ALU.mult),
                reads=[Bpos, Binv], writes=[Bang])
            for dst, shift in ((sinT, 0.0), (cosT, float(np.pi / 2))):
                S.op("dve", lambda e, ang=ang, kk=kk, shift=shift: e.tensor_scalar(out=kk, in0=ang, scalar1=shift,
                                                                                   scalar2=1.0 / TWO_PI, op0=ALU.add, op1=ALU.mult),
                     reads=[Bang], writes=[Bkk])
                S.op("dve", lambda e, kk=kk: e.tensor_scalar(out=kk, in0=kk, scalar1=MAGIC, scalar2=None, op0=ALU.add),
                     reads=[Bkk], writes=[Bkk])
                S.op("dve", lambda e, kk=kk: e.tensor_scalar(out=kk, in0=kk, scalar1=MAGIC, scalar2=-TWO_PI,
                                                             op0=ALU.subtract, op1=ALU.mult), reads=[Bkk], writes=[Bkk])
                S.op("dve", lambda e, ang=ang, kk=kk, shift=shift: e.scalar_tensor_tensor(out=kk, in0=ang, scalar=shift, in1=kk,
                                                                                          op0=ALU.add, op1=ALU.add),
                     reads=[Bang, Bkk], writes=[Bkk])
                S.op("dve", lambda e, kk=kk: e.tensor_scalar(out=kk, in0=kk, scalar1=float(np.pi), scalar2=float(-np.pi),
                                                             op0=ALU.min, op1=ALU.max), reads=[Bkk], writes=[Bkk])
                S.op("act", lambda e, kk=kk, dst=dst, b0=b0, nb_=nb_: e.activation(out=dst[:, b0:b0 + nb_, :], in_=kk, func=AF.Sin),
                     reads=[Bkk], writes=[Bcs])

        for tbk in range(NTB):
            s = tbk % 2
            is_all = tbk < NB_ALL
            blk = tbk if is_all else tbk - NB_ALL
            isq = (not is_all) and blk >= NB_A - NQ
            qi_ = blk - (NB_A - NQ)
            if is_all:
                groups = [("kb", C_KB, 512), ("vb", C_VB, 512), ("ki", C_KI, 64)]
            elif not isq:
                groups = [("ka", C_KA, 512), ("va", C_VA, 512)]
            else:
                groups = [("ka", C_KA, 512), ("va", C_VA, 512), ("wi", C_WI, 8), ("qa", C_QA, 512), ("qb", C_QB, 512), ("qi", C_QI, 512)]
            for g0i in range(0, len(groups), 3):
                rnd = groups[g0i:g0i + 3]
                rs = [mm_group(s, c0, n) for (_, c0, n) in rnd]
                if g0i == 0 and tbk + 1 < NTB:
                    stageA(tbk + 1)
                for fn in pending:
                    fn()
                pending.clear()
                for (kind, c0, n), r in zip(rnd, rs):
                    if kind == "kb":
                        zi = rope(r, 512, tbk)
                        transposeT(zi, 512, lambda blk=blk: C.kbT[:, blk, :], [C.BkbT[blk]])
                    elif kind == "vb":
                        k = cnt["vsb"] % 2
                        cnt["vsb"] += 1
                        vcopy(r, vsb[k], Bvsb[k])
                        S.dma("sp", lambda e, k=k, blk=blk: e.dma_start(out=C.vb_d[blk], in_=vsb[k][:]), reads=[Bvsb[k]], writes=[Buf()],
                              owner=Bvsb[k])
                    elif kind == "ki":
                        zi = rope(r, 64, tbk)
                        S.op("pool", lambda e, zi=zi: e.tensor_copy(out=zb[zi][:, 64:128], in_=zb[zi][:, 0:64]), reads=[Bzb[zi]], writes=[Bzb[zi]])
                        transposeT(zi, 128, lambda blk=blk: C.kiT[:, blk * 128:(blk + 1) * 128], [C.BkiT[blk]])
                    elif kind == "ka":
                        zi = rope(r, 512, tbk)
                        to_dram_T(zi, C.kaT_d[blk])
                    elif kind == "va":
                        k = cnt["vsa"] % 2
                        cnt["vsa"] += 1
                        vcopy(r, vsa[k], Bvsa[k])
                        S.op("dve", lambda e, k=k, blk=blk: e.tensor_scalar(out=vsa[k][:].rearrange("p (h c) -> p h c", c=65)[:, :, 64],
                                                                            in0=ones8[:], scalar1=avl[:, blk:blk + 1], scalar2=None, op0=ALU.mult),
                             reads=[Bones, Bavl], writes=[Bvsa[k]])
                        S.dma("sp", lambda e, k=k, blk=blk: e.dma_start(out=C.va_d[blk], in_=vsa[k][:]), reads=[Bvsa[k]], writes=[Buf()],
                              owner=Bvsa[k])
                    elif kind == "qa":
                        zi = rope(r, 512, tbk)
                        to_dram_T(zi, C.qaT_d[qi_])
                    elif kind == "qb":
                        zi = rope(r, 512, tbk)
                        to_dram_T(zi, C.qbT_d[qi_])
                    elif kind == "wi":
                        S.op("dve", lambda e, r=r, s=s: e.tensor_scalar(out=wsc[s][:], in0=z_ps[r][:, 0:8], scalar1=float(1.0 / (8.0 * np.sqrt(8.0))),
                                                                        scalar2=None, op0=ALU.mult), reads=[Bzps[r]], writes=[Bwsc[s]])
                        S.op("dve", lambda e, s=s: e.tensor_scalar(out=lohi[s][:, 8:16], in0=wsc[s][:], scalar1=0.0, scalar2=2.0,
                                                                   op0=ALU.is_ge, op1=ALU.mult), reads=[Bwsc[s]], writes=[Blohi[s]])
                        S.op("dve", lambda e, s=s: e.tensor_scalar(out=lohi[s][:, 0:8], in0=lohi[s][:, 8:16], scalar1=-1.0, scalar2=None,
                                                                   op0=ALU.add), reads=[Blohi[s]], writes=[Blohi[s]])
                        S.dma("sp", lambda e, s=s, qi_=qi_: e.dma_start(out=C.lohi_d[qi_], in_=lohi[s][:]), reads=[Blohi[s]], writes=[Buf()],
                              owner=Blohi[s])
                    elif kind == "qi":
                        zi = rope(r, 512, tbk)
                        S.op("pool", lambda e, s=s, zi=zi: e.tensor_tensor(out=zb[zi][:].rearrange("p (h d) -> p h d", d=64),
                                                                           in0=zb[zi][:].rearrange("p (h d) -> p h d", d=64),
                                                                           in1=wsc[s][:, :].unsqueeze(2).to_broadcast([128, 8, 64]), op=ALU.mult),
                             reads=[Bzb[zi], Bwsc[s]], writes=[Bzb[zi]])
                        to_dram_T(zi, C.qiT_d[qi_])
        for fn in pending:
            fn()
        pending.clear()
        if DBG:
            gd = S.new_group("p1dbg")
            S.dma("sp", lambda e: e.dma_start(out=C.dbg["kbT"], in_=C.kbT[:].rearrange("p b c -> p (b c)")), reads=C.BkbT, writes=[Buf()],
                  group=gd)
            S.dma("sp", lambda e: e.dma_start(out=C.dbg["kiT"], in_=C.kiT[:]), reads=C.BkiT, writes=[Buf()], group=gd)
        S.emit_phase()
        S.release(Bxt + BzTs + Bvsb + Bvsa + Blohi)


def phaseA(C):
    nc, S = C.nc, C.S
    with ExitStack() as es:
        sb = lambda n, s, d: es.enter_context(nc.sbuf_tensor(n, s, d))
        pst = lambda n, s, d: es.enter_context(nc.psum_tensor(n, s, d))
        kaT = sb("kaT", [128, NB_A, 512], BF16)
        va = sb("va", [128, NB_A, 520], BF16)
        qaT = sb("qaT", [128, NQ, 512], BF16)
        amk = sb("amk", [128, NQ, 128], BF16)
        pe_ = [sb(f"pe{i}", [128, 1024], BF16) for i in range(2)]
        pm = [sb(f"pm{i}", [128, 1024], BF16) for i in range(2)]
        rd = [sb(f"rd{i}", [128, 8], F32) for i in range(2)]
        mxa = [sb(f"mxa{i}", [128, 512], BF16) for i in range(2)]
        STA = [pst(f"st{i}", [128, 1024], F32) for i in range(2)]
        st = [[STA[i][:, j * 512:(j + 1) * 512] for j in range(2)] for i in range(2)]
        acc = [[pst(f"acc{i}{j}", [128, 512], F32) for j in range(2)] for i in range(2)]
        BkaT, Bva, BqaT, Bamk = Buf(), Buf(), Buf(), Buf()
        Bpe = [Buf() for _ in range(2)]
        Bpm = [Buf() for _ in range(2)]
        Brd = [Buf() for _ in range(2)]
        Bmxa = [Buf() for _ in range(2)]
        Bst = [[Buf() for _ in range(2)] for _ in range(2)]
        Bacc = [[Buf() for _ in range(2)] for _ in range(2)]
        g = S.new_group("pAinit")
        for c0 in range(0, NB_A, 11):
            S.dma("sp", lambda e, c0=c0: e.dma_start(out=kaT[:, c0:c0 + 11, :], in_=C.kaT_d[c0:c0 + 11].rearrange("b p c -> p b c")),
                  writes=[BkaT], group=g)
            S.dma("sp", lambda e, c0=c0: e.dma_start(out=va[:, c0:c0 + 11, :], in_=C.va_d[c0:c0 + 11].rearrange("b p c -> p b c")),
                  writes=[Bva], group=g)
        S.dma("sp", lambda e: e.dma_start(out=qaT[:], in_=C.qaT_d.rearrange("b p c -> p b c")), writes=[BqaT], group=g)
        gp = S.new_group("pAinit_sw")
        S.dma("pool", lambda e: e.dma_start(out=amk[:].rearrange("p r q -> p (r q)"), in_=C.amask), writes=[Bamk], group=gp)
        def qk_A(i, r, sidx):
            kb = i + r
            for h in range(8):
                rows = slice((h % 2) * 64, (h % 2) * 64 + 64)
                pr = h // 2
                S.op("pe", lambda e, h=h, rows=rows, pr=pr: e.matmul(
                    st[sidx][h % 2][:, pr * 128:(pr + 1) * 128], lhsT=kaT[rows, kb, pr * 128:(pr + 1) * 128],
                    rhs=qaT[rows, i, pr * 128:(pr + 1) * 128], start=True, stop=True),
                    reads=[BkaT, BqaT], writes=[Bst[sidx][h % 2]])

        steps = [(i, r) for i in range(NQ) for r in range(NQ)]
        qk_A(0, 0, 0)
        qk_A(steps[1][0], steps[1][1], 1)
        for it, (i, r) in enumerate(steps):
            a = i % 2
            kb = i + r
            sidx = it % 2
            S.op("act", lambda e, sidx=sidx: e.activation(out=pe_[sidx][:, :], in_=STA[sidx][:, :], func=AF.Exp, scale=0.125),
                 reads=[Bst[sidx][0], Bst[sidx][1]], writes=[Bpe[sidx]])
            if it + 2 < len(steps):
                qk_A(steps[it + 2][0], steps[it + 2][1], sidx)
            S.op("dve", lambda e, sidx=sidx, r=r: e.tensor_tensor(out=pm[sidx][:].rearrange("p (h q) -> p h q", q=128),
                                                                  in0=pe_[sidx][:].rearrange("p (h q) -> p h q", q=128),
                                                                  in1=amk[:, r, :].unsqueeze(1).to_broadcast([128, 8, 128]), op=ALU.mult),
                 reads=[Bpe[sidx], Bamk], writes=[Bpm[sidx]])
            for h in range(8):
                gi = h // 4
                cc = (h % 4) * 65
                pcol = (h % 2) * 512 + (h // 2) * 128
                S.op("pe", lambda e, h=h, gi=gi, cc=cc, pcol=pcol, sidx=sidx, kb=kb, a=a, r=r: e.matmul(
                    acc[a][gi][:, cc:cc + 65], lhsT=pm[sidx][:, pcol:pcol + 128], rhs=va[:, kb, h * 65:(h + 1) * 65],
                    start=(r == 0 and h % 4 == 0), stop=(r == NQ - 1), skip_group_check=True),
                    reads=[Bpm[sidx], Bva], writes=[Bacc[a][gi]])
            if r != NQ - 1:
                continue
            for gi in range(2):
                accv = acc[a][gi][:, 0:260].rearrange("p (h c) -> p h c", c=65)
                S.op("dve", lambda e, a=a, gi=gi, accv=accv: e.tensor_scalar(out=rd[a][:, gi * 4:(gi + 1) * 4].unsqueeze(2), in0=accv[:, :, 64:65],
                                                                             scalar1=1e-30, scalar2=None, op0=ALU.max),
                     reads=[Bacc[a][gi]], writes=[Brd[a]])
                S.op("dve", lambda e, a=a, gi=gi: e.reciprocal(out=rd[a][:, gi * 4:(gi + 1) * 4], in_=rd[a][:, gi * 4:(gi + 1) * 4]),
                     reads=[Brd[a]], writes=[Brd[a]])
                S.op("dve", lambda e, a=a, gi=gi, accv=accv: e.tensor_tensor(
                    out=mxa[a][:, gi * 256:(gi + 1) * 256].rearrange("p (h d) -> p h d", d=64), in0=accv[:, :, 0:64],
                    in1=rd[a][:, gi * 4:(gi + 1) * 4].unsqueeze(2).to_broadcast([128, 4, 64]), op=ALU.mult),
                    reads=[Bacc[a][gi], Brd[a]], writes=[Bmxa[a]])
            S.dma("sp", lambda e, a=a, i=i: e.dma_start(out=C.mix_d[i][:, 0:512], in_=mxa[a][:]), reads=[Bmxa[a]], writes=[Buf()], owner=Bmxa[a])
        S.emit_phase()
        S.release(Bmxa)
        if DBG and STOP_AFTER == "A":
            dump_mix(C)


def dump_mix(C):
    nc, S = C.nc, C.S
    with nc.sbuf_tensor("dm0", [128, 1024], BF16) as dm0, nc.sbuf_tensor("dm1", [128, 1024], BF16) as dm1:
        dm = [dm0, dm1]
        Bd = [Buf(), Buf()]
        for i in range(NQ):
            k = i % 2
            c1 = 512 if STOP_AFTER == "A" else 1024
            S.dma("sp", lambda e, i=i, k=k, c1=c1: e.dma_start(out=dm[k][:, 0:c1], in_=C.mix_d[i][:, 0:c1]), writes=[Bd[k]], owner=Bd[k])
            S.dma("sp", lambda e, i=i, k=k, c1=c1: e.dma_start(out=C.dbg["mix"][i][:, 0:c1], in_=dm[k][:, 0:c1]), reads=[Bd[k]], writes=[Buf()],
                  owner=Bd[k])
        S.emit_phase()
        S.release(Bd)


def phaseB(C):
    nc, S = C.nc, C.S
    NCHK = T // 512
    with ExitStack() as es:
        sb = lambda n, s, d: es.enter_context(nc.sbuf_tensor(n, s, d))
        pst = lambda n, s, d: es.enter_context(nc.psum_tensor(n, s, d))
        iotaN = sb("iotaN", [128, T], F32)
        Iacc = sb("Iacc", [128, T], F32)
        Mb = sb("Mb", [128, T], BF16)
        qbT = [sb(f"qbT{i}", [128, 512], BF16) for i in range(2)]
        qiT = [sb(f"qiT{i}", [128, 512], BF16) for i in range(2)]
        lohi = [sb(f"lohiB{i}", [128, 16], F32) for i in range(2)]
        tqf = sb("tqf", [128, NQ], F32)
        cq = sb("cq", [128, NQ], F32)
        fh = [sb(f"fh{i}", [128, 1024], F32) for i in range(3)]
        lo_ = sb("lo_", [128, 1], F32)
        cand = sb("cand", [128, 1], F32)
        cntt = sb("cntt", [128, 1], F32)
        ind = sb("ind", [128, 1], F32)
        ncand = sb("ncand", [128, 1], F32)
        ssum = sb("ssum", [128, 1], F32)
        vbs = [sb(f"vbs{i}", [128, 8, 520], BF16) for i in range(2)]
        pe_ = [sb(f"peB{i}", [128, 1024], BF16) for i in range(2)]
        pm = [sb(f"pmB{i}", [128, 1024], BF16) for i in range(2)]
        mts = [sb(f"mts{i}", [128, 1024], BF16) for i in range(2)]
        rd = [sb(f"rdB{i}", [128, 8], F32) for i in range(2)]
        mxb = [sb(f"mxb{i}", [128, 512], BF16) for i in range(2)]
        PB = [pst(f"PB{i}", [128, 1024], F32) for i in range(4)]
        BP = [[Buf() for _ in range(2)] for _ in range(4)]
        scst = [[PB[i][:, j * 512:(j + 1) * 512] for j in range(2)] for i in range(2)]
        acc = [PB[2][:, j * 512:(j + 1) * 512] for j in range(2)]
        mt_ps = [PB[3][:, j * 512:(j + 1) * 512].bitcast(BF16) for j in range(2)]

        Biota, Btq, Bcq, Blo, Bcand, Bcnt, Bind, Bncand, Bssum, BMall2 = (Buf() for _ in range(10))
        BI = [Buf() for _ in range(NCHK)]
        BM = [Buf() for _ in range(NB_ALL // 8)]
        BMall = Buf()
        BqbT = [Buf() for _ in range(2)]
        BqiT = [Buf() for _ in range(2)]
        Blohi = [Buf() for _ in range(2)]
        Bfh = [Buf() for _ in range(3)]
        Bvbs = [Buf() for _ in range(2)]
        Bpe = [Buf() for _ in range(2)]
        Bpm = [Buf() for _ in range(2)]
        Bmts = [Buf() for _ in range(2)]
        Brd = [Buf() for _ in range(2)]
        Bmxb = [Buf() for _ in range(2)]
        Bscst = [[BP[i][j] for j in range(2)] for i in range(2)]
        Bacc = [BP[2][j] for j in range(2)]
        Bmtps = [BP[3][j] for j in range(2)]

        g = S.new_group("pBinit")
        S.dma("sp", lambda e: e.dma_start(out=tqf[:], in_=C.tq), writes=[Btq], group=g)
        S.op("dve", lambda e: e.tensor_scalar(out=cq[:], in0=tqf[:], scalar1=0.5, scalar2=-BIG, op0=ALU.add, op1=ALU.mult),
             reads=[Btq], writes=[Bcq])
        S.op("pool", lambda e: e.iota(Iacc[:].bitcast(I32), pattern=[[1, T]], base=0, channel_multiplier=0), writes=BI)
        S.op("dve", lambda e: e.tensor_scalar(out=iotaN[:], in0=Iacc[:].bitcast(I32), scalar1=-BIG, scalar2=None, op0=ALU.mult),
             reads=BI, writes=[Biota])

        gcast = S.new_group("pBcast")
        for (src, dst, rows, cols) in ((C.w_o, C.wob_d, D, D), (C.w_up, C.wupb_d, D, 2 * D_FF), (C.w_down, C.wdnb_d, D_FF, D),
                                       (C.w_g, C.wgb_d, D, D), (C.w_p, C.wpb_d, 256, D)):
            cw_ = 1408 if cols == 2 * D_FF else 1024
            for r0 in range(0, rows, 128):
                for c0 in range(0, cols, cw_):
                    S.dma("pool", lambda e, src=src, dst=dst, r0=r0, c0=c0, cw_=cw_: e.dma_start(out=dst[r0:r0 + 128, c0:c0 + cw_],
                                                                                         in_=src[r0:r0 + 128, c0:c0 + cw_]),
                          writes=[Buf()], group=gcast)
        fcnt = 0

        def load_q(i):
            qs = i % 2
            S.dma("sp", lambda e: e.dma_start(out=qbT[qs][:], in_=C.qbT_d[i]), writes=[BqbT[qs]], owner=BqbT[qs])
            S.dma("sp", lambda e: e.dma_start(out=qiT[qs][:], in_=C.qiT_d[i]), writes=[BqiT[qs]], owner=BqiT[qs])
            S.dma("sp", lambda e: e.dma_start(out=lohi[qs][:], in_=C.lohi_d[i]), writes=[Blohi[qs]], owner=Blohi[qs])

        load_q(0)
        for i in range(NQ):
            qs = i % 2
            if i + 1 < NQ:
                load_q(i + 1)
            nkc = min(NCHK, -(-(3 * (OWN // 128) + i) // 4))
            nkb = nkc * 4
            nkeys = nkb * 128
            HB = max(1, (nkc * 7) // 16)
            h1 = HB * 512
            n_act = nkeys - h1
            cthr = float(256 - n_act // 2)
            NG = -(-nkb // 8)
            for cp in range((nkc + 1) // 2):
                nck = min(2, nkc - 2 * cp)
                w = nck * 512
                c0 = cp * 1024
                BIc = BI[2 * cp:2 * cp + nck]
                for h in range(8):
                    rows = slice((h % 2) * 64, (h % 2) * 64 + 64)
                    pr = h // 2
                    ti = (h % 2) + 2 * (pr % 2)
                    for u in range(nck):
                        S.op("pe", lambda e, rows=rows, pr=pr, ti=ti, qs=qs, u=u, c0=c0: e.matmul(
                            PB[ti][:, u * 512:(u + 1) * 512], lhsT=qiT[qs][rows, pr * 128:(pr + 1) * 128],
                            rhs=C.kiT[rows, c0 + u * 512:c0 + (u + 1) * 512], start=True, stop=True),
                            reads=[BqiT[qs]] + C.BkiT[(c0 // 128) + u * 4:(c0 // 128) + (u + 1) * 4], writes=[BP[ti][u]])
                    k = fcnt % 3
                    fcnt += 1
                    S.op("act", lambda e, ti=ti, k=k, qs=qs, h=h, w=w: e.activation(out=fh[k][:, 0:w], in_=PB[ti][:, 0:w], func=AF.Relu,
                                                                                 scale=lohi[qs][:, h:h + 1]),
                         reads=BP[ti][0:nck] + [Blohi[qs]], writes=[Bfh[k]])
                    if h == 0:
                        S.op("dve", lambda e, k=k, c0=c0, w=w, qs=qs, h=h: e.tensor_scalar(out=Iacc[:, c0:c0 + w], in0=fh[k][:, 0:w],
                                                                                       scalar1=lohi[qs][:, h:h + 1], scalar2=None, op0=ALU.mult),
                             reads=[Bfh[k], Blohi[qs]], writes=BIc)
                    else:
                        S.op("dve", lambda e, k=k, c0=c0, w=w, qs=qs, h=h: e.scalar_tensor_tensor(
                            out=Iacc[:, c0:c0 + w], in0=fh[k][:, 0:w], scalar=lohi[qs][:, h:h + 1], in1=Iacc[:, c0:c0 + w],
                            op0=ALU.mult, op1=ALU.add), reads=[Bfh[k], Blohi[qs]] + BIc, writes=BIc)
                S.op("dve", lambda e, c0=c0, w=w, i=i: e.scalar_tensor_tensor(out=Iacc[:, c0:c0 + w], in0=iotaN[:, c0:c0 + w],
                                                                              scalar=cq[:, i:i + 1], in1=Iacc[:, c0:c0 + w],
                                                                              op0=ALU.subtract, op1=ALU.min),
                     reads=[Biota, Bcq] + BIc, writes=BIc)
            S.op("dve", lambda e: e.memset(cand[:], 0.0), writes=[Bcand])
            for b in range(NBIS):
                step = float(BIS_B * 2.0 / (2 ** (b + 1)))
                nstep = step / 2.0
                S.op("dve", lambda e, h1=h1: e.tensor_scalar(out=Mb[:, 0:h1], in0=Iacc[:, 0:h1], scalar1=cand[:, 0:1], scalar2=None,
                                                      op0=ALU.is_ge, op1=ALU.add, accum_out=cntt[:, 0:1]),
                     reads=BI[:HB] + [Bcand], writes=[BMall, Bcnt])
                S.op("act", lambda e, h1=h1, nkeys=nkeys: e.activation(out=Mb[:, h1:nkeys], in_=Iacc[:, h1:nkeys], func=AF.Sign, scale=-1.0, bias=cand[:, 0:1],
                                                   accum_out=ssum[:, 0:1]),
                     reads=BI[HB:nkc] + [Bcand], writes=[BMall2, Bssum])
                S.op("dve", lambda e: e.scalar_tensor_tensor(out=ind[:], in0=ssum[:], scalar=-0.5, in1=cntt[:], op0=ALU.mult, op1=ALU.add),
                     reads=[Bssum, Bcnt], writes=[Bind])
                if b < NBIS - 1:
                    S.op("dve", lambda e, nstep=nstep, cthr=cthr: e.tensor_scalar(out=ind[:], in0=ind[:], scalar1=cthr, scalar2=2.0 * nstep,
                                                                       op0=ALU.is_ge, op1=ALU.mult), reads=[Bind], writes=[Bind])
                    S.op("dve", lambda e, nstep=nstep: e.scalar_tensor_tensor(out=cand[:], in0=ind[:], scalar=-nstep, in1=cand[:],
                                                                              op0=ALU.add, op1=ALU.add), reads=[Bind, Bcand], writes=[Bcand])
                else:
                    S.op("dve", lambda e, step=step, cthr=cthr: e.tensor_scalar(out=ind[:], in0=ind[:], scalar1=cthr, scalar2=step,
                                                                     op0=ALU.is_ge, op1=ALU.mult), reads=[Bind], writes=[Bind])
                    S.op("dve", lambda e, step=step: e.scalar_tensor_tensor(out=lo_[:], in0=ind[:], scalar=-step, in1=cand[:],
                                                                            op0=ALU.add, op1=ALU.add), reads=[Bind, Bcand], writes=[Blo])
            for gq in range(NG):
                ce = min((gq + 1) * 1024, nkeys)
                S.op("dve", lambda e, gq=gq, ce=ce: e.tensor_scalar(out=Mb[:, gq * 1024:ce], in0=Iacc[:, gq * 1024:ce],
                                                                    scalar1=lo_[:, 0:1], scalar2=None, op0=ALU.is_ge),
                     reads=BI[2 * gq:min(2 * gq + 2, nkc)] + [Blo, BMall, BMall2], writes=[BM[gq]])

            def load_group(gq):
                vk = gq % 2
                S.dma("sp", lambda e, gq=gq, vk=vk: e.dma_start(out=vbs[vk][:], in_=C.vb_d[gq * 8:(gq + 1) * 8].rearrange("b p c -> p b c")),
                      writes=[Bvbs[vk]], owner=Bvbs[vk])
                for j in range(min(8, nkb - gq * 8)):
                    n = gq * 8 + j
                    S.op("pe", lambda e, j=j, n=n, vk=vk: e.transpose(out=mt_ps[vk][:, j * 128:(j + 1) * 128], in_=Mb[:, n * 128:(n + 1) * 128],
                                                                      identity=C.ident[:]),
                         reads=[BM[gq], C.Bident], writes=[Bmtps[vk]])
                nj = min(8, nkb - gq * 8)
                S.op("dve", lambda e, vk=vk, nj=nj: e.tensor_copy(out=mts[vk][:, 0:nj * 128], in_=mt_ps[vk][:, 0:nj * 128]),
                     reads=[Bmtps[vk]], writes=[Bmts[vk]])

            def qk_B(n, sidx, qs=qs):
                for h in range(8):
                    rows = slice((h % 2) * 64, (h % 2) * 64 + 64)
                    pr = h // 2
                    S.op("pe", lambda e, h=h, rows=rows, pr=pr: e.matmul(
                        scst[sidx][h % 2][:, pr * 128:(pr + 1) * 128], lhsT=C.kbT[rows, n, pr * 128:(pr + 1) * 128],
                        rhs=qbT[qs][rows, pr * 128:(pr + 1) * 128], start=True, stop=True),
                        reads=[C.BkbT[n], BqbT[qs]], writes=[Bscst[sidx][h % 2]])

            load_group(0)
            qk_B(0, 0)
            qk_B(1, 1)
            for n in range(nkb):
                gq, j = n // 8, n % 8
                vk = gq % 2
                sidx = n % 2
                if j == 0 and gq + 1 < NG:
                    load_group(gq + 1)
                S.op("act", lambda e, sidx=sidx: e.activation(out=pe_[sidx][:, :], in_=PB[sidx][:, :], func=AF.Exp, scale=0.125),
                     reads=[Bscst[sidx][0], Bscst[sidx][1]], writes=[Bpe[sidx]])
                if n + 2 < nkb:
                    qk_B(n + 2, sidx)
                S.op("dve", lambda e, sidx=sidx, vk=vk, j=j: e.tensor_tensor(
                    out=pm[sidx][:].rearrange("p (h q) -> p h q", q=128), in0=pe_[sidx][:].rearrange("p (h q) -> p h q", q=128),
                    in1=mts[vk][:, j * 128:(j + 1) * 128].unsqueeze(1).to_broadcast([128, 8, 128]), op=ALU.mult),
                    reads=[Bpe[sidx], Bmts[vk]], writes=[Bpm[sidx]])
                for h in range(8):
                    gi = h // 4
                    cc = (h % 4) * 65
                    pcol = (h % 2) * 512 + (h // 2) * 128
                    S.op("pe", lambda e, h=h, gi=gi, cc=cc, pcol=pcol, sidx=sidx, vk=vk, j=j, n=n: e.matmul(
                        acc[gi][:, cc:cc + 65], lhsT=pm[sidx][:, pcol:pcol + 128], rhs=vbs[vk][:, j, h * 65:(h + 1) * 65],
                        start=(n == 0 and h % 4 == 0), stop=(n == nkb - 1), skip_group_check=True),
                        reads=[Bpm[sidx], Bvbs[vk]], writes=[Bacc[gi]])
            a = i % 2
            for gi in range(2):
                accv = acc[gi][:, 0:260].rearrange("p (h c) -> p h c", c=65)
                S.op("dve", lambda e, a=a, gi=gi, accv=accv: e.tensor_scalar(out=rd[a][:, gi * 4:(gi + 1) * 4].unsqueeze(2), in0=accv[:, :, 64:65],
                                                                             scalar1=1e-30, scalar2=None, op0=ALU.max),
                     reads=[Bacc[gi]], writes=[Brd[a]])
                S.op("dve", lambda e, a=a, gi=gi: e.reciprocal(out=rd[a][:, gi * 4:(gi + 1) * 4], in_=rd[a][:, gi * 4:(gi + 1) * 4]),
                     reads=[Brd[a]], writes=[Brd[a]])
                S.op("dve", lambda e, a=a, gi=gi, accv=accv: e.tensor_tensor(
                    out=mxb[a][:, gi * 256:(gi + 1) * 256].rearrange("p (h d) -> p h d", d=64), in0=accv[:, :, 0:64],
                    in1=rd[a][:, gi * 4:(gi + 1) * 4].unsqueeze(2).to_broadcast([128, 4, 64]), op=ALU.mult),
                    reads=[Bacc[gi], Brd[a]], writes=[Bmxb[a]])
            S.dma("sp", lambda e, a=a, i=i: e.dma_start(out=C.mix_d[i][:, 512:1024], in_=mxb[a][:]), reads=[Bmxb[a]], writes=[Buf()], owner=Bmxb[a])
        S.emit_phase()
        S.release(BqbT + BqiT + Blohi + Bvbs + Bmxb)
    if DBG and STOP_AFTER == "B":
        dump_mix(C)


def rms_to_bf16(S, xin_ap, Bx, sqj, Bsq, ss, Bss, rstd, Brs, gam, Bgam, out_ap, Bout):
    S.op("act", lambda e: e.activation(out=sqj[:], in_=xin_ap, func=AF.Square, accum_out=ss[:, 0:1]), reads=[Bx], writes=[Bsq, Bss])
    S.op("act", lambda e: e.activation(out=rstd[:], in_=ss[:], func=AF.Sqrt, scale=1.0 / D, bias=RMS_EPS), reads=[Bss], writes=[Brs])
    S.op("dve", lambda e: e.reciprocal(out=rstd[:], in_=rstd[:]), reads=[Brs], writes=[Brs])
    S.op("dve", lambda e: e.scalar_tensor_tensor(out=out_ap, in0=xin_ap, scalar=rstd[:, 0:1], in1=gam[:], op0=ALU.mult, op1=ALU.mult),
         reads=[Bx, Brs, Bgam], writes=[Bout])


def phaseC1(C):
    nc, S = C.nc, C.S
    TT = 256
    NT = OWN // TT
    with ExitStack() as es:
        sb = lambda n, s, d: es.enter_context(nc.sbuf_tensor(n, s, d))
        pst = lambda n, s, d: es.enter_context(nc.psum_tensor(n, s, d))
        wo = sb("wo", [128, 8, D], BF16)
        wup = sb("wup", [128, 8, 2 * D_FF], BF16)
        wdn = sb("wdn", [128, 22, D], BF16)
        gff = sb("gff", [128, D], F32)
        cw = sb("cw", [128, 4, NCH], F32)
        carry = sb("carry", [128, NCH, 2], F32)
        mixt = [sb("mixt0", [128, D], BF16)] * 2
        mixT = [sb("mixT0", [128, 8, 128], BF16)] * 2
        xio = [sb(f"xio{i}", [128, 2, D], F32) for i in range(2)]
        cwl = xio[0][0:NCH, 0, 0:512].rearrange("c (j p) -> c j p", j=4)
        ss = [sb(f"ssC{i}", [128, 1], F32) for i in range(2)]
        rstd = [sb(f"rstdC{i}", [128, 1], F32) for i in range(2)]
        h2 = [sb(f"h2{i}", [128, D], BF16) for i in range(2)]
        h2T = [sb(f"h2T{i}", [128, 8, TT], BF16) for i in range(2)]
        U = [sb(f"U{i}", [128, TT + 2], F32) for i in range(4)]
        y = [sb(f"y{i}", [128, TT], F32) for i in range(5)]
        sg = [sb("sg0", [128, TT], F32)] * 2
        aT = sb("aT", [128, 22, TT], BF16)
        tp_ps = [pst(f"tpps{i}", [128, 1024], BF16) for i in range(2)]
        wo_ps = [pst(f"wops{i}", [128, 512], F32) for i in range(2)]
        u_ps = [pst(f"ups{i}", [128, 512], F32) for i in range(4)]
        wd_ps = wo_ps

        Bwo, Bwup, Bwdn, Bgff, Bcw, Bsq = (Buf() for _ in range(6))
        Bcarry = [Buf() for _ in range(NCH)]
        Bmixt = [Buf()] * 2
        BmixT = [Buf()] * 2
        Bxio = [[Buf() for _ in range(2)] for _ in range(2)]
        Bcwl = Bxio[0][0]
        Bss = [Buf() for _ in range(2)]
        Brs = [Buf() for _ in range(2)]
        Bh2 = [Buf() for _ in range(2)]
        Bh2T = [[Buf() for _ in range(2)] for _ in range(2)]
        BU = [Buf() for _ in range(4)]
        By = [Buf() for _ in range(5)]
        Bsg = [Buf()] * 2
        BaT = [Buf() for _ in range(22)]
        Btp = [Buf() for _ in range(2)]
        Bwops = [Buf() for _ in range(2)]
        Bups = [Buf() for _ in range(4)]
        Bwdps = Bwops

        g = S.new_group("pC1init")
        gw = S.new_group("pC1w")
        S.dma("sp", lambda e: e.dma_start(out=gff[:], in_=C.ffn_norm.partition_broadcast(128)), writes=[Bgff], group=g)
        for j in range(3):
            S.dma("sp", lambda e, j=j: e.dma_start(out=cwl[:, j, :], in_=C.conv_w[j].rearrange("(c p) -> c p", p=128)), writes=[Bcwl], group=g)
        S.dma("sp", lambda e: e.dma_start(out=cwl[:, 3, :], in_=C.conv_b[0].rearrange("(c p) -> c p", p=128)), writes=[Bcwl], group=g)
        gwu = S.new_group("pC1wu")
        gwd = S.new_group("pC1wd")
        S.dma("sp", lambda e: e.dma_start(out=wo[:], in_=C.wob_d.rearrange("(m p) c -> p m c", p=128)), writes=[Bwo], group=gw)
        for m in range(8):
            S.dma("sp", lambda e, m=m: e.dma_start(out=wup[:, m, :], in_=C.wupb_d[m * 128:(m + 1) * 128, :]), writes=[Bwup], group=gwu)
        for k0 in range(0, 22, 11):
            S.dma("sp", lambda e, k0=k0: e.dma_start(out=wdn[:, k0:k0 + 11, :],
                                                     in_=C.wdnb_d[k0 * 128:(k0 + 11) * 128, :].rearrange("(m p) c -> p m c", p=128)),
                  writes=[Bwdn], group=gwd)
        for j in range(4):
            S.op("pe", lambda e, j=j: e.transpose(out=wo_ps[0][:, j * NCH:(j + 1) * NCH], in_=cwl[:, j, :], identity=C.identf[0:NCH, 0:NCH]),
                 reads=[Bcwl, C.Bident], writes=[Bwops[0]])
        S.op("dve", lambda e: e.tensor_copy(out=cw[:].rearrange("p j c -> p (j c)"), in_=wo_ps[0][:, 0:4 * NCH]), reads=[Bwops[0]], writes=[Bcw])

        cnt = {"tp": 0, "u": 0, "U": 0, "y": 0}

        def load_xrow(xa_blk, xdst, Bxdst):
            S.dma("sp", lambda e: e.dma_start(out=xdst, in_=C.xa[xa_blk * 128:(xa_blk + 1) * 128, :]), writes=[Bxdst], owner=Bxdst)

        def attn_out_block(mix_idx, xa_blk, xdst, Bxdst, slot):
            S.dma("sp", lambda e: e.dma_start(out=mixt[slot][:], in_=C.mix_d[mix_idx]), writes=[Bmixt[slot]], owner=Bmixt[slot])
            q = cnt["tp"] % 2
            cnt["tp"] += 1
            for m in range(8):
                S.op("pe", lambda e, m=m, q=q: e.transpose(out=tp_ps[q][:, m * 128:(m + 1) * 128], in_=mixt[slot][:, m * 128:(m + 1) * 128],
                                                           identity=C.ident[:]), reads=[Bmixt[slot], C.Bident], writes=[Btp[q]])
            S.op("act", lambda e, q=q: e.activation(out=mixT[slot][:].rearrange("p m t -> p (m t)"), in_=tp_ps[q][:], func=AF.Copy),
                 reads=[Btp[q]], writes=[BmixT[slot]])
            for hf in range(2):
                for m in range(8):
                    S.op("pe", lambda e, m=m, hf=hf: e.matmul(wo_ps[hf][:, :], lhsT=mixT[slot][:, m, :], rhs=wo[:, m, hf * 512:(hf + 1) * 512],
                                                              start=(m == 0), stop=(m == 7)), reads=[BmixT[slot], Bwo], writes=[Bwops[hf]])
                S.op("dve", lambda e, hf=hf: e.tensor_tensor(out=xdst[:, hf * 512:(hf + 1) * 512], in0=xdst[:, hf * 512:(hf + 1) * 512],
                                                             in1=wo_ps[hf][:, :], op=ALU.add), reads=[Bxdst, Bwops[hf]], writes=[Bxdst])

        def norm_T(xsrc, Bxsrc, slot, dstT_fn, BdstT):
            rms_to_bf16(S, xsrc, Bxsrc, h2[slot], Bh2[slot], ss[slot], Bss[slot], rstd[slot], Brs[slot], gff, Bgff, h2[slot][:], Bh2[slot])
            q = cnt["tp"] % 2
            cnt["tp"] += 1
            for m in range(8):
                S.op("pe", lambda e, m=m, q=q: e.transpose(out=tp_ps[q][:, m * 128:(m + 1) * 128], in_=h2[slot][:, m * 128:(m + 1) * 128],
                                                           identity=C.ident[:]), reads=[Bh2[slot], C.Bident], writes=[Btp[q]])
            S.op("act", lambda e, q=q: e.activation(out=dstT_fn(), in_=tp_ps[q][:].rearrange("p (m t) -> p m t", t=128), func=AF.Copy),
                 reads=[Btp[q]], writes=[BdstT])

        load_xrow(NB_A - NQ, xio[1][:, 1, :], Bxio[1][1])
        for blk in range(2):
            load_xrow(NB_A - NQ + 1 + blk, xio[0][:, blk, :], Bxio[0][blk])
        attn_out_block(0, NB_A - NQ, xio[1][:, 1, :], Bxio[1][1], 1)
        norm_T(xio[1][:, 1, :], Bxio[1][1], 1, lambda: h2T[1][:, :, 128:256], Bh2T[1][1])
        for cc in range(NCH):
            r = cnt["u"] % 4
            cnt["u"] += 1
            ub, uh = u_ps[r], 0
            for m in range(8):
                S.op("pe", lambda e, m=m, cc=cc, ub=ub, uh=uh: e.matmul(ub[:, uh * 256:uh * 256 + 2], lhsT=wup[:, m, cc * 128:(cc + 1) * 128],
                                                                        rhs=h2T[1][:, m, 254:256], start=(m == 0), stop=(m == 7)),
                     reads=[Bh2T[1][1], Bwup], writes=[Bups[r]])
            S.op("dve", lambda e, cc=cc, ub=ub, uh=uh: e.tensor_copy(out=carry[:, cc, :], in_=ub[:, uh * 256:uh * 256 + 2]),
                 reads=[Bups[r]], writes=[Bcarry[cc]])

        def prologue(t):
            xs = t % 2
            for blk in range(2):
                tb = 2 * t + blk
                attn_out_block(tb + 1, NB_A - NQ + 1 + tb, xio[xs][:, blk, :], Bxio[xs][blk], blk)
                norm_T(xio[xs][:, blk, :], Bxio[xs][blk], blk, lambda xs=xs, blk=blk: h2T[xs][:, :, blk * 128:(blk + 1) * 128], Bh2T[xs][blk])

        prologue(0)
        for t in range(NT):
            xs = t % 2
            if t + 1 < NT:
                for blk in range(2):
                    load_xrow(NB_A - NQ + 1 + 2 * (t + 1) + blk, xio[1 - xs][:, blk, :], Bxio[1 - xs][blk])
            for k in range(22):
                ys = []
                pair = []
                for cc in (k, k + 22):
                    r = cnt["u"] % 4
                    cnt["u"] += 1
                    ub = u_ps[r]
                    for m in range(8):
                        S.op("pe", lambda e, m=m, cc=cc, ub=ub, xs=xs: e.matmul(
                            ub[:, 0:TT], lhsT=wup[:, m, cc * 128:(cc + 1) * 128], rhs=h2T[xs][:, m, :],
                            start=(m == 0), stop=(m == 7)), reads=[Bh2T[xs][0], Bh2T[xs][1], Bwup], writes=[Bups[r]])
                    ui = cnt["U"] % 4
                    cnt["U"] += 1
                    yi = cnt["y"] % 5
                    cnt["y"] += 1
                    pair.append((cc, r, ub, ui, yi))
                    ys.append(yi)
                for cc, r, ub, ui, yi in pair:
                    S.op("pool", lambda e, ui=ui, cc=cc: e.tensor_copy(out=U[ui][:, 0:2], in_=carry[:, cc, :]), reads=[Bcarry[cc]], writes=[BU[ui]])
                for cc, r, ub, ui, yi in pair:
                    S.op("act", lambda e, ui=ui, ub=ub: e.activation(out=U[ui][:, 2:TT + 2], in_=ub[:, 0:TT], func=AF.Copy),
                         reads=[Bups[r]], writes=[BU[ui]])
                for cc, r, ub, ui, yi in pair:
                    S.op("pool", lambda e, ui=ui, cc=cc: e.tensor_copy(out=carry[:, cc, :], in_=U[ui][:, TT:TT + 2]), reads=[BU[ui]],
                         writes=[Bcarry[cc]])
                for pi, (cc, r, ub, ui, yi) in enumerate(pair):
                    if pi == 0:
                        S.op("act", lambda e, ui=ui, yi=yi, cc=cc: e.activation(out=y[yi][:], in_=U[ui][:, 2:TT + 2], func=AF.Identity,
                                                                                scale=cw[:, 2, cc:cc + 1], bias=cw[:, 3, cc:cc + 1]),
                             reads=[BU[ui], Bcw], writes=[By[yi]])
                    else:
                        S.op("pool", lambda e, ui=ui, yi=yi, cc=cc: e.tensor_scalar(out=y[yi][:], in0=U[ui][:, 2:TT + 2], scalar1=cw[:, 2, cc:cc + 1],
                                                                                    scalar2=cw[:, 3, cc:cc + 1], op0=ALU.mult, op1=ALU.add),
                             reads=[BU[ui], Bcw], writes=[By[yi]])
                for cc, r, ub, ui, yi in pair:
                    S.op("dve", lambda e, ui=ui, yi=yi, cc=cc: e.scalar_tensor_tensor(out=y[yi][:], in0=U[ui][:, 1:TT + 1], scalar=cw[:, 1, cc:cc + 1],
                                                                                      in1=y[yi][:], op0=ALU.mult, op1=ALU.add),
                         reads=[BU[ui], Bcw, By[yi]], writes=[By[yi]])
                for cc, r, ub, ui, yi in pair:
                    S.op("dve", lambda e, ui=ui, yi=yi, cc=cc: e.scalar_tensor_tensor(out=y[yi][:], in0=U[ui][:, 0:TT], scalar=cw[:, 0, cc:cc + 1],
                                                                                      in1=y[yi][:], op0=ALU.mult, op1=ALU.add),
                         reads=[BU[ui], Bcw, By[yi]], writes=[By[yi]])
                si = k % 2
                S.op("act", lambda e, si=si, yg=ys[0]: e.activation(out=sg[si][:], in_=y[yg][:], func=AF.Silu), reads=[By[ys[0]]], writes=[Bsg[si]])
                S.op("dve", lambda e, si=si, yu=ys[1], k=k: e.tensor_tensor(out=aT[:, k, :], in0=sg[si][:], in1=y[yu][:], op=ALU.mult),
                     reads=[Bsg[si], By[ys[1]]], writes=[BaT[k]])
            if t + 1 < NT:
                prologue(t + 1)
            for blk in range(2):
                tb = 2 * t + blk
                for hf in range(2):
                    for k in range(22):
                        S.op("pe", lambda e, k=k, hf=hf, blk=blk: e.matmul(wd_ps[hf][:, :], lhsT=aT[:, k, blk * 128:(blk + 1) * 128],
                                                                           rhs=wdn[:, k, hf * 512:(hf + 1) * 512], start=(k == 0), stop=(k == 21)),
                             reads=[BaT[k], Bwdn], writes=[Bwdps[hf]])
                    S.op("dve", lambda e, hf=hf, blk=blk, xs=xs: e.tensor_tensor(out=xio[xs][:, blk, hf * 512:(hf + 1) * 512],
                                                                                 in0=xio[xs][:, blk, hf * 512:(hf + 1) * 512],
                                                                                 in1=wd_ps[hf][:, :], op=ALU.add),
                         reads=[Bxio[xs][blk], Bwdps[hf]], writes=[Bxio[xs][blk]])
                S.dma("sp", lambda e, tb=tb, xs=xs, blk=blk: e.dma_start(out=C.x2_d[tb * 128:(tb + 1) * 128, :], in_=xio[xs][:, blk, :]),
                      reads=[Bxio[xs][blk]], writes=[Buf()], owner=Bxio[xs][blk])
        S.emit_phase()
        S.release([Bmixt[0]] + Bxio[0] + Bxio[1])
    if DBG:
        with nc.sbuf_tensor("dx0", [128, D], F32) as dx0, nc.sbuf_tensor("dx1", [128, D], F32) as dx1:
            dx = [dx0, dx1]
            Bd = [Buf(), Buf()]
            for i in range(OWN // 128):
                k = i % 2
                S.dma("sp", lambda e, i=i, k=k: e.dma_start(out=dx[k][:], in_=C.x2_d[i * 128:(i + 1) * 128, :]), writes=[Bd[k]], owner=Bd[k])
                S.dma("sp", lambda e, i=i, k=k: e.dma_start(out=C.dbg["x2"][i * 128:(i + 1) * 128, :], in_=dx[k][:]), reads=[Bd[k]], writes=[Buf()],
                      owner=Bd[k])
            S.emit_phase()
            S.release(Bd)


def phaseC2(C):
    nc, S = C.nc, C.S
    NBK = OWN // 128
    with ExitStack() as es:
        sb = lambda n, s, d: es.enter_context(nc.sbuf_tensor(n, s, d))
        pst = lambda n, s, d: es.enter_context(nc.psum_tensor(n, s, d))
        wg = sb("wg", [128, 8, D], BF16)
        wp = sb("wp", [128, 2, D], BF16)
        gpl = sb("gpl", [128, D], F32)
        gfin = sb("gfin", [128, D], F32)
        x2t = [sb(f"x2t{i}", [128, D], F32) for i in range(2)]
        pt = [sb(f"pt{i}", [128, 256], F32) for i in range(2)]
        pb = [sb(f"pb{i}", [128, 256], BF16) for i in range(2)]
        pT = [sb(f"pT{i}", [128, 2, 128], BF16) for i in range(2)]
        sqj = sb("sqjD", [128, D], BF16)
        sqj2 = sb("sqjD2", [128, D], BF16)
        Bsq2 = Buf()
        ss = [sb(f"ssD{i}", [128, 1], F32) for i in range(2)]
        rstd = [sb(f"rstdD{i}", [128, 1], F32) for i in range(2)]
        ss2 = [sb(f"ssE{i}", [128, 1], F32) for i in range(2)]
        rstd2 = [sb(f"rstdE{i}", [128, 1], F32) for i in range(2)]
        h3 = [sb(f"h3{i}", [128, D], BF16) for i in range(2)]
        h3T = [sb(f"h3T{i}", [128, 8, 128], BF16) for i in range(2)]
        gate = [sb(f"gate{i}", [128, D], F32) for i in range(2)]
        x3 = [sb(f"x3{i}", [128, D], F32) for i in range(2)]
        ot = [sb(f"ot{i}", [128, D], F32) for i in range(2)]
        tp_ps = pst("tpD", [128, 1024], BF16)
        pT_ps = pst("pTD", [128, 1024], BF16)
        g_ps = [pst(f"gps{i}", [128, 512], F32) for i in range(2)]
        pp_ps = [pst(f"ppps{i}", [128, 512], F32) for i in range(2)]
        Bwg, Bwp, Bgpl, Bgfin, Bsq, Btp, BpTps = (Buf() for _ in range(7))
        mk = lambda: [Buf() for _ in range(2)]
        Bx2t, Bpt, Bpb, BpT, Bss, Brs, Bss2, Brs2, Bh3, Bh3T, Bgate, Bx3, Bot, Bgps, Bppps = (mk() for _ in range(15))
        g = S.new_group("pC2init")
        gw = S.new_group("pC2w")
        S.dma("sp", lambda e: e.dma_start(out=gpl[:], in_=C.ple_norm.partition_broadcast(128)), writes=[Bgpl], group=g)
        S.dma("sp", lambda e: e.dma_start(out=gfin[:], in_=C.final_norm.partition_broadcast(128)), writes=[Bgfin], group=g)
        S.dma("sp", lambda e: e.dma_start(out=wg[:], in_=C.wgb_d.rearrange("(m p) c -> p m c", p=128)), writes=[Bwg], group=gw)
        S.dma("sp", lambda e: e.dma_start(out=wp[:], in_=C.wpb_d.rearrange("(m p) c -> p m c", p=128)), writes=[Bwp], group=gw)
        def load_c2(tb):
            s = tb % 2
            S.dma("sp", lambda e: e.dma_start(out=x2t[s][:], in_=C.x2_d[tb * 128:(tb + 1) * 128, :]), writes=[Bx2t[s]], owner=Bx2t[s])
            S.dma("sp", lambda e: e.dma_start(out=pt[s][:], in_=C.p_own[tb * 128:(tb + 1) * 128, :]), writes=[Bpt[s]], owner=Bpt[s])

        def x_norm(tb):
            s = tb % 2
            rms_to_bf16(S, x2t[s][:], Bx2t[s], sqj, Bsq, ss[s], Bss[s], rstd[s], Brs[s], gpl, Bgpl, h3[s][:], Bh3[s])
            S.op("pool", lambda e, s=s: e.tensor_copy(out=pb[s][:], in_=pt[s][:]), reads=[Bpt[s]], writes=[Bpb[s]])

        def x_transposes(tb):
            s = tb % 2
            for m in range(8):
                S.op("pe", lambda e, m=m, s=s: e.transpose(out=tp_ps[:, m * 128:(m + 1) * 128], in_=h3[s][:, m * 128:(m + 1) * 128],
                                                           identity=C.ident[:]), reads=[Bh3[s], C.Bident], writes=[Btp])
            for m in range(2):
                S.op("pe", lambda e, m=m, s=s: e.transpose(out=pT_ps[:, m * 128:(m + 1) * 128], in_=pb[s][:, m * 128:(m + 1) * 128],
                                                           identity=C.ident[:]), reads=[Bpb[s], C.Bident], writes=[BpTps])

        def x_copies(tb):
            s = tb % 2
            S.op("act", lambda e, s=s: e.activation(out=h3T[s][:].rearrange("p m t -> p (m t)"), in_=tp_ps[:], func=AF.Copy),
                 reads=[Btp], writes=[Bh3T[s]])
            S.op("act", lambda e, s=s: e.activation(out=pT[s][:].rearrange("p m t -> p (m t)"), in_=pT_ps[:, 0:256], func=AF.Copy),
                 reads=[BpTps], writes=[BpT[s]])

        load_c2(0)
        load_c2(1)
        x_norm(0)
        x_transposes(0)
        x_copies(0)
        for tb in range(NBK):
            s = tb % 2
            nxt = tb + 1 < NBK
            for hf in range(2):
                for m in range(8):
                    S.op("pe", lambda e, m=m, hf=hf, s=s: e.matmul(g_ps[hf][:, :], lhsT=h3T[s][:, m, :], rhs=wg[:, m, hf * 512:(hf + 1) * 512],
                                                                   start=(m == 0), stop=(m == 7)), reads=[Bh3T[s], Bwg], writes=[Bgps[hf]])
                for m in range(2):
                    S.op("pe", lambda e, m=m, hf=hf, s=s: e.matmul(pp_ps[hf][:, :], lhsT=pT[s][:, m, :], rhs=wp[:, m, hf * 512:(hf + 1) * 512],
                                                                   start=(m == 0), stop=(m == 1)), reads=[BpT[s], Bwp], writes=[Bppps[hf]])
            if nxt:
                x_norm(tb + 1)
                x_transposes(tb + 1)
            for hf in range(2):
                S.op("act", lambda e, hf=hf, s=s: e.activation(out=gate[s][:, hf * 512:(hf + 1) * 512], in_=g_ps[hf][:, :], func=AF.Sigmoid),
                     reads=[Bgps[hf]], writes=[Bgate[s]])
                S.op("dve", lambda e, hf=hf, s=s: e.tensor_tensor(out=gate[s][:, hf * 512:(hf + 1) * 512], in0=gate[s][:, hf * 512:(hf + 1) * 512],
                                                                  in1=pp_ps[hf][:, :], op=ALU.mult), reads=[Bgate[s], Bppps[hf]], writes=[Bgate[s]])
            if nxt:
                x_copies(tb + 1)
            S.op("pool", lambda e, s=s: e.tensor_tensor(out=x3[s][:], in0=gate[s][:], in1=x2t[s][:], op=ALU.add),
                 reads=[Bgate[s], Bx2t[s]], writes=[Bx3[s]])
            if tb + 2 < NBK:
                load_c2(tb + 2)
            S.op("act", lambda e, s=s: e.activation(out=sqj2[:], in_=x3[s][:], func=AF.Square, accum_out=ss2[s][:, 0:1]),
                 reads=[Bx3[s]], writes=[Bsq2, Bss2[s]])
            S.op("act", lambda e, s=s: e.activation(out=rstd2[s][:], in_=ss2[s][:], func=AF.Sqrt, scale=1.0 / D, bias=RMS_EPS),
                 reads=[Bss2[s]], writes=[Brs2[s]])
            S.op("dve", lambda e, s=s: e.reciprocal(out=rstd2[s][:], in_=rstd2[s][:]), reads=[Brs2[s]], writes=[Brs2[s]])
            S.op("dve", lambda e, s=s: e.scalar_tensor_tensor(out=ot[s][:], in0=x3[s][:], scalar=rstd2[s][:, 0:1], in1=gfin[:],
                                                              op0=ALU.mult, op1=ALU.mult), reads=[Bx3[s], Brs2[s], Bgfin], writes=[Bot[s]])
            S.dma("sp", lambda e, tb=tb, s=s: e.dma_start(out=C.out[tb * 128:(tb + 1) * 128, :], in_=ot[s][:]), reads=[Bot[s]], writes=[Buf()],
                  owner=Bot[s])
        S.emit_phase()


def _amask_const():
    s = np.arange(128)[:, None]
    qq = np.arange(128)[None, :]
    out = np.zeros((128, NQ, 128), np.float32)
    for r in range(NQ):
        delta = (16 - r) * 128 + qq - s
        m = ((delta >= 0) & (delta <= 128)).astype(np.float32)
        m += ((delta >= 0) & (delta <= 512) & (delta % 4 == 0)).astype(np.float32)
        m += ((delta >= 0) & (delta <= 2048) & (delta % 16 == 0)).astype(np.float32)
        out[:, r, :] = m
    return out.reshape(128, NQ * 128)


def make_in_maps(x, p, positions, attn_norm, w_in, w_o, ffn_norm, w_up, conv_w, conv_b, w_down, ple_norm,
                 w_ple_gate, w_ple_proj, final_norm):
    f32 = lambda a: np.ascontiguousarray(np.asarray(a, dtype=np.float32))
    x = f32(x)
    p = f32(p)
    positions = np.asarray(positions).astype(np.int32)
    invf = (1.0 / (10000.0 ** (np.arange(0, 64, 2, dtype=np.float32) / np.float32(64)))).astype(np.float32)[None]
    amask = _amask_const()
    shared = dict(invf=invf, amask=amask, w_in=f32(w_in[0]), w_o=f32(w_o[0]), w_up=f32(w_up[0]), w_down=f32(w_down[0]),
                  w_g=f32(w_ple_gate[0]), w_p=f32(w_ple_proj[0]), attn_norm=f32(attn_norm[0:1]), ffn_norm=f32(ffn_norm[0:1]),
                  ple_norm=f32(ple_norm[0:1]), final_norm=f32(final_norm[None]), conv_w=f32(conv_w[0]), conv_b=f32(conv_b[0:1]))
    maps = []
    for c in range(NCORE):
        b, q = c // 4, c % 4
        t_lo = OWN * q - (NB_A - 16) * 128 + 0
        t_lo = OWN * q - 2176
        idx = np.arange(t_lo, t_lo + NB_A * 128)
        valid = idx >= 0
        xa = np.zeros((NB_A * 128, D), np.float32)
        xa[valid] = x[b, idx[valid]]
        posa = np.zeros(NB_A * 128, np.int32)
        posa[valid] = positions[b, idx[valid]]
        pos = np.concatenate([positions[b].reshape(NB_ALL, 128).T, posa.reshape(NB_A, 128).T], axis=1)
        avalid = valid.astype(np.float32).reshape(NB_A, 128).T
        tqi = idx[(NB_A - NQ) * 128:].astype(np.float32)
        tqi = np.where(tqi >= 0, tqi, -1.0).reshape(NQ, 128).T
        m = dict(shared)
        m.update(xall=x[b], xa=xa, pos=np.ascontiguousarray(pos), avalid=np.ascontiguousarray(avalid),
                 tq=np.ascontiguousarray(tqi.astype(np.float32)), p_own=np.ascontiguousarray(p[0, b, OWN * q:OWN * (q + 1)]))
        maps.append(m)
    return maps


_NC_CACHE = {}


def kernel(**inputs):
    if "nc" not in _NC_CACHE:
        _NC_CACHE["nc"] = build_program()
    nc = _NC_CACHE["nc"]
    in_maps = make_in_maps(**inputs)
    res = run_bass_kernel_spmd(nc, in_maps, core_ids=list(range(NCORE)))
    outs = [r["out"] for r in res.results]
    full = np.stack(outs, 0).reshape(2, T, D).astype(np.float32)
    if DBG:
        kernel.last = res.results
    return full
```
